# Optimizing a Trainium2 kernel written in Bass

```python
import math
import jax, jax.numpy as jnp
from jax import lax
import numpy as np

D_MODEL = 4096
BATCH = 4
SEQ = 4096
DEPTH = 1

D_MIX = D_MODEL
D_SSM = D_MIX // 2
D_SC = D_MIX - D_SSM
SSM_HEAD_DIM = 64
SSM_HEADS = D_SSM // SSM_HEAD_DIM
SSM_GROUPS = 8
SSM_STATE = 128
SSM_CONV = 5
SSM_CHUNK = 128
SC_CONV = 3
SC_GROUPS = 16
D_XBC = D_SSM + 2 * SSM_GROUPS * SSM_STATE
D_IN = D_SSM + D_XBC + 2 * SSM_HEADS + 3 * D_SC
D_FF = 4 * D_MODEL
N_MOD = 6
DEEPNORM_ALPHA = (2 * DEPTH) ** 0.25
DEEPNORM_BETA = (8 * DEPTH) ** -0.25
DT_PROJ_SCALE = 0.1
LN_EPS = 1e-5
RMS_EPS = 1e-5

kernel_name = "hymba_ssd_shortconv_deepnorm_adaln_encoder"


def layer_norm(x, g, b):
    xf = x.astype(jnp.float32)
    mu = jnp.mean(xf, axis=-1, keepdims=True)
    var = jnp.mean(jnp.square(xf - mu), axis=-1, keepdims=True)
    return ((xf - mu) * lax.rsqrt(var + LN_EPS) * g + b).astype(x.dtype)


def group_rms_norm(y, w, n_groups):
    bsz, s, d = y.shape
    yf = y.astype(jnp.float32).reshape(bsz, s, n_groups, d // n_groups)
    yf = yf * lax.rsqrt(jnp.mean(yf * yf, axis=-1, keepdims=True) + RMS_EPS)
    return (yf.reshape(bsz, s, d) * w).astype(y.dtype)


def dwconv_centred(u, w):
    k_w, ch = w.shape
    return lax.conv_general_dilated(
        u, w[:, None, :].astype(u.dtype), window_strides=(1,),
        padding=[(k_w // 2, k_w // 2)], dimension_numbers=('NWC', 'WIO', 'NWC'),
        feature_group_count=ch)


def ssd_chunked(x, dt, a, b_in, c_in):
    bsz, s, h, p = x.shape
    g, n = b_in.shape[2], b_in.shape[3]
    r = h // g
    q = SSM_CHUNK
    nc = s // q
    xc = x.reshape(bsz, nc, q, g, r, p)
    dtc = dt.reshape(bsz, nc, q, g, r)
    bc = b_in.reshape(bsz, nc, q, g, n)
    cc = c_in.reshape(bsz, nc, q, g, n)
    a_cum = jnp.cumsum(dtc * a.reshape(g, r), axis=2)
    xdt = xc * dtc[..., None]
    lower = jnp.tril(jnp.ones((q, q), dtype=bool))[:, :, None, None]
    seg = a_cum[:, :, :, None] - a_cum[:, :, None, :]
    decay = jnp.exp(jnp.where(lower, seg, -jnp.inf))
    scores = jnp.einsum('bcign,bcjgn->bcijg', cc, bc)
    y_diag = jnp.einsum('bcijgr,bcjgrp->bcigrp', scores[..., None] * decay, xdt)
    to_end = jnp.exp(a_cum[:, :, -1:] - a_cum)
    states = jnp.einsum('bclgn,bclgrp->bcgrpn', bc, xdt * to_end[..., None])
    chunk_decay = jnp.exp(a_cum[:, :, -1])

    def step(carry, inp):
        st, dec = inp
        return carry * dec[..., None, None] + st, carry

    init = jnp.zeros((bsz, g, r, p, n), dtype=states.dtype)
    _, prev = lax.scan(step, init, (jnp.moveaxis(states, 1, 0), jnp.moveaxis(chunk_decay, 1, 0)))
    prev = jnp.moveaxis(prev, 0, 1)
    y_off = jnp.einsum('bclgn,bcgrpn->bclgrp', cc, prev) * jnp.exp(a_cum)[..., None]
    return (y_diag + y_off).reshape(bsz, s, h, p).astype(x.dtype)


def ssd_mixer(u_z, u_xbc, u_dt, conv_w, conv_b, dt_bias_f, dt_bias_b, a_log_f, a_log_b, d_skip, norm_w):
    bsz, s, _ = u_z.shape
    xbc = jax.nn.silu(dwconv_centred(u_xbc, conv_w) + conv_b)
    xs, bs, cs = jnp.split(xbc, [D_SSM, D_SSM + SSM_GROUPS * SSM_STATE], axis=-1)
    xs = xs.reshape(bsz, s, SSM_HEADS, SSM_HEAD_DIM)
    bs = bs.reshape(bsz, s, SSM_GROUPS, SSM_STATE)
    cs = cs.reshape(bsz, s, SSM_GROUPS, SSM_STATE)
    dt = u_dt.astype(jnp.float32)
    dt_f = jax.nn.softplus(dt[..., :SSM_HEADS] + dt_bias_f)
    dt_b = jax.nn.softplus(dt[..., SSM_HEADS:] + dt_bias_b)
    a_f = -jnp.exp(a_log_f.astype(jnp.float32))
    a_b = -jnp.exp(a_log_b.astype(jnp.float32))
    flip = lambda t: jnp.flip(t, axis=1)
    y_f = ssd_chunked(xs, dt_f, a_f, bs, cs)
    y_b = flip(ssd_chunked(flip(xs), flip(dt_b), a_b, flip(bs), flip(cs)))
    y = y_f + y_b + d_skip[:, None] * xs
    y = y.reshape(bsz, s, D_SSM) * jax.nn.silu(u_z)
    return group_rms_norm(y, norm_w, SSM_GROUPS)


def short_conv_mixer(u_h, u_b, u_c, conv_w, norm_w):
    y = u_b * dwconv_centred(u_c * u_h, conv_w)
    return group_rms_norm(y, norm_w, SC_GROUPS)


def setup_inputs(seed: int = 0) -> dict:
    key = jax.random.key(seed)
    ks = jax.random.split(key, 24)
    nrm = jax.random.normal
    x = nrm(ks[0], (BATCH, SEQ, D_MODEL), jnp.float32)
    c = nrm(ks[1], (BATCH, D_MODEL), jnp.float32)
    w_ada = nrm(ks[2], (DEPTH, D_MODEL, N_MOD * D_MODEL), jnp.float32) * (0.1 * D_MODEL ** -0.5)
    b_ada = 0.01 * nrm(ks[3], (DEPTH, N_MOD * D_MODEL), jnp.float32)
    dt_lo = D_SSM + D_XBC
    col_scale = jnp.ones((D_IN,), jnp.float32).at[dt_lo:dt_lo + 2 * SSM_HEADS].set(DT_PROJ_SCALE)
    w_in = nrm(ks[4], (DEPTH, D_MODEL, D_IN), jnp.float32) * (D_MODEL ** -0.5) * col_scale
    ssm_conv_w = nrm(ks[5], (DEPTH, SSM_CONV, D_XBC), jnp.float32) * SSM_CONV ** -0.5
    ssm_conv_b = 0.01 * nrm(ks[6], (DEPTH, D_XBC), jnp.float32)

    def dt_bias(k):
        dt0 = jnp.exp(jax.random.uniform(k, (DEPTH, SSM_HEADS), jnp.float32, math.log(1e-3), math.log(1e-1)))
        return dt0 + jnp.log(-jnp.expm1(-dt0))

    ssm_dt_bias_f = dt_bias(ks[7])
    ssm_dt_bias_b = dt_bias(ks[8])
    ssm_a_log_f = jnp.log(jax.random.uniform(ks[9], (DEPTH, SSM_HEADS), jnp.float32, 1.0, 16.0))
    ssm_a_log_b = jnp.log(jax.random.uniform(ks[10], (DEPTH, SSM_HEADS), jnp.float32, 1.0, 16.0))
    ssm_d = 1.0 + 0.1 * nrm(ks[11], (DEPTH, SSM_HEADS), jnp.float32)
    ssm_norm_w = 1.0 + 0.02 * nrm(ks[12], (DEPTH, D_SSM), jnp.float32)
    sc_conv_w = nrm(ks[13], (DEPTH, SC_CONV, D_SC), jnp.float32) * SC_CONV ** -0.5
    sc_norm_w = 1.0 + 0.02 * nrm(ks[14], (DEPTH, D_SC), jnp.float32)
    w_out = nrm(ks[15], (DEPTH, D_MIX, D_MODEL), jnp.float32) * (D_MIX ** -0.5 * DEEPNORM_BETA)
    ln1_g = 1.0 + 0.02 * nrm(ks[16], (DEPTH, D_MODEL), jnp.float32)
    ln1_b = 0.01 * nrm(ks[17], (DEPTH, D_MODEL), jnp.float32)
    w_up = nrm(ks[18], (DEPTH, D_MODEL, D_FF), jnp.float32) * D_MODEL ** -0.5
    w_down = nrm(ks[19], (DEPTH, D_FF, D_MODEL), jnp.float32) * (D_FF ** -0.5 * DEEPNORM_BETA)
    ln2_g = 1.0 + 0.02 * nrm(ks[20], (DEPTH, D_MODEL), jnp.float32)
    ln2_b = 0.01 * nrm(ks[21], (DEPTH, D_MODEL), jnp.float32)
    return {"x": x, "c": c, "w_ada": w_ada, "b_ada": b_ada, "w_in": w_in,
            "ssm_conv_w": ssm_conv_w, "ssm_conv_b": ssm_conv_b,
            "ssm_dt_bias_f": ssm_dt_bias_f, "ssm_dt_bias_b": ssm_dt_bias_b,
            "ssm_a_log_f": ssm_a_log_f, "ssm_a_log_b": ssm_a_log_b, "ssm_d": ssm_d,
            "ssm_norm_w": ssm_norm_w, "sc_conv_w": sc_conv_w, "sc_norm_w": sc_norm_w,
            "w_out": w_out, "ln1_g": ln1_g, "ln1_b": ln1_b, "w_up": w_up, "w_down": w_down,
            "ln2_g": ln2_g, "ln2_b": ln2_b}


def reference(x, c, w_ada, b_ada, w_in, ssm_conv_w, ssm_conv_b, ssm_dt_bias_f, ssm_dt_bias_b,
              ssm_a_log_f, ssm_a_log_b, ssm_d, ssm_norm_w, sc_conv_w, sc_norm_w, w_out,
              ln1_g, ln1_b, w_up, w_down, ln2_g, ln2_b):
    bounds = [int(v) for v in np.cumsum([D_SSM, D_XBC, 2 * SSM_HEADS, D_SC, D_SC])]
    for l in range(DEPTH):
        mod = jnp.einsum('bd,dm->bm', jax.nn.silu(c), w_ada[l]) + b_ada[l]
        shift1, scale1, gate1, shift2, scale2, gate2 = jnp.split(mod[:, None, :], N_MOD, axis=-1)
        h = x * (1.0 + scale1) + shift1
        proj = jnp.einsum('bsd,de->bse', h, w_in[l])
        u_z, u_xbc, u_dt, u_h, u_b, u_c = jnp.split(proj, bounds, axis=-1)
        y_ssm = ssd_mixer(u_z, u_xbc, u_dt, ssm_conv_w[l], ssm_conv_b[l], ssm_dt_bias_f[l],
                          ssm_dt_bias_b[l], ssm_a_log_f[l], ssm_a_log_b[l], ssm_d[l], ssm_norm_w[l])
        y_sc = short_conv_mixer(u_h, u_b, u_c, sc_conv_w[l], sc_norm_w[l])
        mix = jnp.einsum('bse,ed->bsd', jnp.concatenate([y_ssm, y_sc], axis=-1), w_out[l])
        x = layer_norm(DEEPNORM_ALPHA * x + (1.0 + gate1) * mix, ln1_g[l], ln1_b[l])
        h = x * (1.0 + scale2) + shift2
        ff = jnp.square(jax.nn.relu(jnp.einsum('bsd,df->bsf', h, w_up[l])))
        ff = jnp.einsum('bsf,fd->bsd', ff, w_down[l])
        x = layer_norm(DEEPNORM_ALPHA * x + (1.0 + gate2) * ff, ln2_g[l], ln2_b[l])
    return x
```

```python
import numpy as np
from contextlib import ExitStack
import concourse.bass as bass
import concourse.mybir as mybir
from concourse.bass_utils import run_bass_kernel_spmd

F32 = mybir.dt.float32
BF16 = mybir.dt.bfloat16
AF = mybir.ActivationFunctionType
ALU = mybir.AluOpType

LN_EPS = 1e-5
RMS_EPS = 1e-5


class Cfg:
    def __init__(s, D, T, NG, NSC, DFF, alpha):
        s.D, s.T, s.NG, s.NSC, s.DFF, s.alpha = D, T, NG, NSC, DFF, alpha
        s.KC = D // 128
        s.DSSM = NG * 256
        s.NH = NG * 4
        s.DSC = NSC * 128
        s.DXBC = s.DSSM + 2 * NG * 128
        s.DIN = s.DSSM + s.DXBC + 2 * s.NH + 3 * s.DSC
        s.FC = DFF // 128
        s.NT = T // 128
        s.TB = min(1024, T)
        s.NFM = (s.DXBC + 3 * s.DSC) // 128
        s.NXO = (s.DSSM + NG * 128) // 128
        assert s.DSSM + s.DSC == D


FULL = Cfg(4096, 2048, 8, 16, 16384, 2.0 ** 0.25)


class Res:
    __slots__ = ("name", "w", "readers", "sem", "semcnt", "sw", "psum")

    def __init__(self, name):
        self.name = name
        self.w = None
        self.readers = {}
        self.sem = None
        self.semcnt = 0
        self.sw = False
        self.psum = False


class Eng:
    def __init__(self, Kx, h, name, is_pe=False):
        self.K, self.h, self.name, self.is_pe = Kx, h, name, is_pe
        self.sem = Kx.new_sem("e_" + name)
        self.cnt = 0
        self.seen = {}

    def _wait(self, tok, war):
        if tok is None:
            return
        sem, val, en = tok
        if en == self.name and (self.is_pe or war):
            return
        key = id(sem)
        if self.seen.get(key, 0) >= val:
            return
        self.h.wait_ge(sem, val)
        self.seen[key] = val

    def deps(self, reads, writes):
        for r in reads:
            self._wait(r.w, False)
            if r.psum:
                for t in r.readers.values():
                    self._wait(t, True)
        for w in writes:
            self._wait(w.w, False)
            for t in w.readers.values():
                self._wait(t, True)

    def op(self, fn, reads=(), writes=(), inc=True):
        self.deps(reads, writes)
        ins = fn(self.h)
        if inc:
            ins.then_inc(self.sem, 1)
            self.cnt += 1
            tok = (self.sem, self.cnt, self.name)
        else:
            tok = (self.sem, self.cnt + 1, self.name)
        for r in reads:
            r.readers[id(tok[0])] = tok
        for w in writes:
            w.w = tok
            w.readers = {}
        return ins

    def dma(self, out, in_, reads=(), writes=(), semres=None):
        self.dma_multi([(out, in_)], reads, writes, semres)

    def dma_multi(self, pairs, reads=(), writes=(), semres=None):
        self.deps(reads, writes)
        if semres.sem is None:
            semres.sw = (self.name == "pool")
            semres.sem, semres.semcnt = self.K.take_dma_sem("d_" + semres.name, semres.sw)
        assert semres.sw == (self.name == "pool"), semres.name
        for out, in_ in pairs:
            self.h.dma_start(out=out, in_=in_).then_inc(semres.sem, 16)
            semres.semcnt += 16
        tok = (semres.sem, semres.semcnt, "dma")
        for r in reads:
            r.readers[id(tok[0])] = tok
        for w in writes:
            w.w = tok
            w.readers = {}


class Kctx:
    def __init__(self, nc, es):
        self.nc, self.es = nc, es
        self.nsem = 0
        self.all_res = []
        self.sem_pool = []
        self.sem_pool_sw = []

    def new_sem(self, name):
        self.nsem += 1
        return self.es.enter_context(self.nc.semaphore("%s_%d" % (name[:20], self.nsem)))

    def take_dma_sem(self, name, sw):
        pool = self.sem_pool_sw if sw else self.sem_pool
        if pool:
            return pool.pop()
        return self.new_sem(name), 0

    def recycle(self, mark):
        for r in self.all_res[mark:]:
            if r.sem is not None:
                (self.sem_pool_sw if r.sw else self.sem_pool).append((r.sem, r.semcnt))
        del self.all_res[mark:]

    def res(self, name):
        r = Res(name)
        self.all_res.append(r)
        return r

    def engines(self):
        return [self.pe, self.act, self.dve, self.pool, self.sp]

    def barrier(self):
        toks = [(e.sem, e.cnt, "x") for e in self.engines() if e.cnt > 0]
        for r in self.all_res:
            if r.sem is not None and r.semcnt > 0:
                toks.append((r.sem, r.semcnt, "x"))
        for e in self.engines():
            for t in toks:
                if t[0] is e.sem and e.is_pe:
                    continue
                e._wait(t, False)
        for r in self.all_res:
            r.w = None
            r.readers = {}


class Ring:
    def __init__(self, Kx, es, name, shape, dtype, n, psum=False):
        self.tiles = []
        for i in range(n):
            nm = "%s%d" % (name, i)
            t = es.enter_context((Kx.nc.psum_tensor if psum else Kx.nc.sbuf_tensor)(nm, shape, dtype))
            rr = Kx.res(nm)
            rr.psum = psum
            self.tiles.append((t, rr))
        self.i = 0

    def next(self):
        t = self.tiles[self.i % len(self.tiles)]
        self.i += 1
        return t


def one(Kx, es, name, shape, dtype, psum=False):
    t = es.enter_context((Kx.nc.psum_tensor if psum else Kx.nc.sbuf_tensor)(name, shape, dtype))
    rr = Kx.res(name)
    rr.psum = psum
    return t, rr


def build(cfg, debug=False, stop_after=99):
    nc = bass.Bass("TRN2", target_bir_lowering=False)
    D, T, KC, NG, NSC, NH, FC, NT, TB = cfg.D, cfg.T, cfg.KC, cfg.NG, cfg.NSC, cfg.NH, cfg.FC, cfg.NT, cfg.TB
    DSSM, DSC, DXBC, DIN, DFF = cfg.DSSM, cfg.DSC, cfg.DXBC, cfg.DIN, cfg.DFF
    NFM, NXO = cfg.NFM, cfg.NXO
    LT = 2 * T
    TH = T + 128
    H2 = 2 * NH
    NTT = 2 * NT
    TBt = TB // 128
    NXT = DXBC // 128
    SSMT = DSSM // 128
    alpha = float(cfg.alpha)

    def din(name, shape, dt=F32):
        return nc.dram_tensor(name, shape, dt, kind="ExternalInput").ap()

    def dscr(name, shape, dt=F32):
        return nc.dram_tensor(name, shape, dt, kind=("ExternalOutput" if debug else "Internal")).ap()

    x_in = din("x", [LT, D])
    c_fm = din("c_fm", [128, KC])
    w_ada = din("w_ada", [D, 6 * D])
    b_ada_fm = din("b_ada_fm", [128, 6 * KC])
    w_in = din("w_in", [D, DIN])
    w_dt = din("w_dt", [D, H2])
    cw_fm = din("cw_fm", [128, NXT, 5])
    cb_fm = din("cb_fm", [128, NXT])
    dtb_bc = din("dtb_bc", [128, H2])
    alog_bc = din("alog_bc", [128, H2])
    dsk_bc = din("dsk_bc", [128, DSSM])
    nw_bc = din("nw_bc", [128, DSSM])
    scw_fm = din("scw_fm", [128, NSC, 3])
    scn_fm = din("scn_fm", [128, NSC])
    w_out = din("w_out", [D, D])
    ln1g_bc = din("ln1g_bc", [128, D])
    ln1b_bc = din("ln1b_bc", [128, D])
    w_up = din("w_up", [D, DFF])
    w_down = din("w_down", [DFF, D])
    ln2g_bc = din("ln2g_bc", [128, D])
    ln2b_bc = din("ln2b_bc", [128, D])
    consts = din("consts", [128, 6, 128])
    out = nc.dram_tensor("out", [T, D], F32, kind="ExternalOutput").ap()

    PTm = dscr("PTm", [NFM, 128, TH])
    PTo = dscr("PTo", [NXO, 128, T])
    Ztm = dscr("Ztm", [T, DSSM])
    GZ = dscr("GZ", [T, DSSM])
    XBC = dscr("XBC", [NXT, 128, T], BF16)
    XBO = dscr("XBO", [NXO, 128, T], BF16)
    YT = dscr("YT", [KC, 128, T], BF16)
    MIX = dscr("MIX", [T, D])
    X1 = dscr("X1", [T, D])
    H2T = dscr("H2T", [KC, 128, T], BF16)
    UT = dscr("UT", [FC, 128, T], BF16)
    FFs = dscr("FF", [T, D])
    BCS = dscr("BCS", [4, 128, D])
    DBG = dscr("DBG", [128, 8 * NTT * H2]) if debug else None

    with ExitStack() as ges:
        Kx = Kctx(nc, ges)
        Kx.pe = Eng(Kx, nc.tensor, "pe", is_pe=True)
        Kx.act = Eng(Kx, nc.scalar, "act")
        Kx.dve = Eng(Kx, nc.vector, "dve")
        Kx.pool = Eng(Kx, nc.gpsimd, "pool")
        Kx.sp = Eng(Kx, nc.sync, "sp")
        pe, act, dve, pool, sp = Kx.pe, Kx.act, Kx.dve, Kx.pool, Kx.sp

        def stage_end(es, mark):
            Kx.barrier()
            Kx.recycle(mark)
            es.close()

        cst, r_cst = one(Kx, ges, "cst", [128, 6, 128], F32)
        identb, r_identb = one(Kx, ges, "identb", [128, 128], BF16)
        onesdiv, r_onesdiv = one(Kx, ges, "onesdiv", [128, 128], F32)
        modfm, r_mod = one(Kx, ges, "modfm", [128, 6 * KC], F32)
        sc1p, r_sc1p = one(Kx, ges, "sc1p", [128, KC], F32)
        dtraw, r_dtraw = one(Kx, ges, "dtraw", [128, NTT, H2], F32)
        scb, r_scb = one(Kx, ges, "scb", [128, KC], BF16)
        IDENT, ONES, LE, GE, GT, LTm = (cst[:, i, :] for i in range(6))

        sp.dma(cst[:], consts[:, :, :], writes=[r_cst], semres=r_cst)
        dve.op(lambda h: h.tensor_copy(out=identb[:], in_=cst[:, 0, :]), reads=[r_cst], writes=[r_identb])
        dve.op(lambda h: h.tensor_scalar(out=onesdiv[:], in0=cst[:, 1, :], scalar1=1.0 / 128.0, scalar2=None,
                                         op0=ALU.mult), reads=[r_cst], writes=[r_onesdiv])

        def evac_copy(i, out_ap, in_ap, reads, writes):
            if i % 2 == 0:
                act.op(lambda h: h.activation(out=out_ap, in_=in_ap, func=AF.Copy), reads=reads, writes=writes)
            else:
                dve.op(lambda h: h.tensor_copy(out=out_ap, in_=in_ap), reads=reads, writes=writes)

        def load_w_slab(Wring, wsrc, col0, width):
            Wt, rW = Wring.next()
            pool.dma(Wt[:, :, 0:width], wsrc.rearrange("(kc p) n -> p kc n", p=128)[:, :, col0:col0 + width],
                     writes=[rW], semres=rW)
            return Wt, rW

        with ExitStack() as es:
            mark = len(Kx.all_res)
            Wring = Ring(Kx, es, "W0_", [128, KC, 512], BF16, 2)
            cf, r_cf = one(Kx, es, "cf", [128, KC], F32)
            bada, r_bada = one(Kx, es, "bada", [128, 6 * KC], F32)
            psm, r_psm = one(Kx, es, "psm", [128, 512], F32, psum=True)
            sp.dma(cf[:], c_fm[:, :], writes=[r_cf], semres=r_cf)
            sp.dma(bada[:], b_ada_fm[:, :], writes=[r_bada], semres=r_bada)
            act.op(lambda h: h.activation(out=scb[:], in_=cf[:], func=AF.Silu), reads=[r_cf], writes=[r_scb])
            nsl = 2 * D // 512
            nxt = load_w_slab(Wring, w_ada, 0, 512)
            for s in range(nsl):
                Wt, rW = nxt
                if s + 1 < nsl:
                    nxt = load_w_slab(Wring, w_ada, (s + 1) * 512, 512)
                for ct in range(4):
                    j = 4 * s + ct
                    for kc in range(KC):
                        pe.op(lambda h, Wt=Wt, ct=ct, kc=kc, j=j: h.matmul(
                            psm[:, j:j + 1], lhsT=Wt[:, kc, ct * 128:(ct + 1) * 128], rhs=scb[:, kc:kc + 1],
                            start=(kc == 0), stop=(kc == KC - 1)),
                            reads=[rW, r_scb], writes=[r_psm], inc=(kc == KC - 1))
            dve.op(lambda h: h.tensor_tensor(out=modfm[:, 0:2 * KC], in0=psm[:, 0:2 * KC], in1=bada[:, 0:2 * KC], op=ALU.add),
                   reads=[r_psm, r_bada], writes=[r_mod])
            dve.op(lambda h: h.tensor_scalar(out=sc1p[:], in0=modfm[:, KC:2 * KC], scalar1=1.0, scalar2=None,
                                             op0=ALU.add), reads=[r_mod], writes=[r_sc1p])
            stage_end(es, mark)

        if stop_after >= 2:
          with ExitStack() as es:
            mark = len(Kx.all_res)
            Wring = Ring(Kx, es, "W2_", [128, KC, 512], BF16, 2)
            hT, r_hT = one(Kx, es, "hT", [128, KC, TB + 128], BF16)
            xring = Ring(Kx, es, "xt", [128, D], F32, 2)
            wdt, r_wdt = one(Kx, es, "wdt", [128, KC, H2], BF16)
            stg = Ring(Kx, es, "stg", [128, TB + 128], F32, 2)
            stz = Ring(Kx, es, "stz", [128, 512], F32, 3)
            pstr = Ring(Kx, es, "pstr", [128, 512], F32, 2, psum=True)
            psacc = Ring(Kx, es, "psacc", [128, 512], F32, 4, psum=True)
            psdt = Ring(Kx, es, "psdt", [128, 512], F32, 2, psum=True)
            pool.dma(wdt[:], w_dt.rearrange("(kc p) n -> p kc n", p=128), writes=[r_wdt], semres=r_wdt)

            def slab_list(c0, c1):
                r = []
                c = c0
                while c < c1:
                    w = min(512, c1 - c)
                    r.append((c, w))
                    c += w
                return r

            sc0 = DSSM + DXBC + H2
            blocks = []
            for b in range(T // TB):
                tiles = list(range(b * TBt, (b + 1) * TBt))
                last = (b == T // TB - 1)
                if last:
                    tiles.append(NT)
                blocks.append(("main", tiles, last))
            for b in range(T // TB):
                blocks.append(("other", [NT + i for i in range(b * TBt, (b + 1) * TBt)], False))

            ev = 0
            for kind, tiles, last in blocks:
                ntok = len(tiles) * 128
                for ti, tile in enumerate(tiles):
                    xt, rx = xring.next()
                    sp.dma(xt[:], x_in[tile * 128:(tile + 1) * 128, :], writes=[rx], semres=rx)
                    for q4 in range(KC // 4):
                        pt, rpt = pstr.next()
                        for q in range(4):
                            kc = q4 * 4 + q
                            pe.op(lambda h, pt=pt, xt=xt, q=q, kc=kc: h.transpose(
                                out=pt[:, q * 128:(q + 1) * 128], in_=xt[:, kc * 128:(kc + 1) * 128],
                                identity=cst[:, 0, :]), reads=[rx, r_cst], writes=[rpt], inc=(q == 3))
                        for q in range(4):
                            kc = q4 * 4 + q
                            dst = hT[:, kc, ti * 128:(ti + 1) * 128]
                            if (ev % 2) == 0:
                                act.op(lambda h, dst=dst, pt=pt, q=q, kc=kc: h.activation(
                                    out=dst, in_=pt[:, q * 128:(q + 1) * 128], func=AF.Identity,
                                    bias=modfm[:, kc:kc + 1], scale=sc1p[:, kc:kc + 1]),
                                    reads=[rpt, r_mod, r_sc1p], writes=[r_hT])
                            else:
                                dve.op(lambda h, dst=dst, pt=pt, q=q, kc=kc: h.tensor_scalar(
                                    out=dst, in0=pt[:, q * 128:(q + 1) * 128], scalar1=sc1p[:, kc:kc + 1],
                                    scalar2=modfm[:, kc:kc + 1], op0=ALU.mult, op1=ALU.add),
                                    reads=[rpt, r_mod, r_sc1p], writes=[r_hT])
                            ev += 1
                for ti, tile in enumerate(tiles):
                    if tile >= NTT or (kind == "main" and tile == NT):
                        continue
                    pd, rpd = psdt.next()
                    for kc in range(KC):
                        pe.op(lambda h, pd=pd, ti=ti, kc=kc: h.matmul(
                            pd[:, 0:H2], lhsT=hT[:, kc, ti * 128:(ti + 1) * 128], rhs=wdt[:, kc, :],
                            start=(kc == 0), stop=(kc == KC - 1)), reads=[r_hT, r_wdt], writes=[rpd],
                            inc=(kc == KC - 1))
                    evac_copy(ev, dtraw[:, tile, :], pd[:, 0:H2], [rpd], [r_dtraw])
                    ev += 1
                if kind == "main":
                    slabs = [("tm", c, w) for c, w in slab_list(0, DSSM)]
                    slabs += [("fm", c, w) for c, w in slab_list(DSSM, DSSM + DXBC)]
                    slabs += [("fm", c, w) for c, w in slab_list(sc0, DIN)]
                else:
                    slabs = [("fm", c, w) for c, w in slab_list(DSSM, DSSM + DSSM + NG * 128)]
                nxt = load_w_slab(Wring, w_in, slabs[0][1], slabs[0][2])
                for si, (mode, c0, wd) in enumerate(slabs):
                    Wt, rW = nxt
                    if si + 1 < len(slabs):
                        nxt = load_w_slab(Wring, w_in, slabs[si + 1][1], slabs[si + 1][2])
                    if mode == "tm":
                        for ti, tile in enumerate(tiles):
                            if tile == NT:
                                continue
                            pa, rpa = psacc.next()
                            for kc in range(KC):
                                pe.op(lambda h, pa=pa, ti=ti, kc=kc, Wt=Wt, wd=wd: h.matmul(
                                    pa[:, 0:wd], lhsT=hT[:, kc, ti * 128:(ti + 1) * 128], rhs=Wt[:, kc, 0:wd],
                                    start=(kc == 0), stop=(kc == KC - 1)), reads=[r_hT, rW], writes=[rpa],
                                    inc=(kc == KC - 1))
                            sz, rsz = stz.next()
                            evac_copy(ev, sz[:, 0:wd], pa[:, 0:wd], [rpa], [rsz])
                            ev += 1
                            sp.dma(Ztm[tile * 128:(tile + 1) * 128, c0:c0 + wd], sz[:, 0:wd], reads=[rsz], semres=rsz)
                    else:
                        for ct in range(wd // 128):
                            col = c0 + ct * 128
                            fmt = (col - DSSM) // 128 if col < sc0 else NXT + (col - sc0) // 128
                            sg, rsg = stg.next()
                            n0 = 0
                            while n0 < ntok:
                                nn = min(512, ntok - n0)
                                pa, rpa = psacc.next()
                                for kc in range(KC):
                                    pe.op(lambda h, pa=pa, kc=kc, Wt=Wt, ct=ct, n0=n0, nn=nn: h.matmul(
                                        pa[:, 0:nn], lhsT=Wt[:, kc, ct * 128:(ct + 1) * 128], rhs=hT[:, kc, n0:n0 + nn],
                                        start=(kc == 0), stop=(kc == KC - 1)), reads=[r_hT, rW], writes=[rpa],
                                        inc=(kc == KC - 1))
                                evac_copy(ev, sg[:, n0:n0 + nn], pa[:, 0:nn], [rpa], [rsg])
                                ev += 1
                                n0 += nn
                            t0 = tiles[0] * 128
                            if kind == "main":
                                sp.dma(PTm[fmt, :, t0:t0 + ntok], sg[:, 0:ntok], reads=[rsg], semres=rsg)
                            else:
                                sp.dma(PTo[fmt, :, t0 - T:t0 - T + ntok], sg[:, 0:ntok], reads=[rsg], semres=rsg)
            stage_end(es, mark)

        if stop_after >= 3:
          with ExitStack() as es:
            mark = len(Kx.all_res)
            WringB = Ring(Kx, es, "W0b_", [128, KC, 512], BF16, 2)
            badaB, r_badaB = one(Kx, es, "badaB", [128, 6 * KC], F32)
            psmB, r_psmB = one(Kx, es, "psmB", [128, 512], F32, psum=True)
            sp.dma(badaB[:], b_ada_fm[:, :], writes=[r_badaB], semres=r_badaB)
            s_lo, s_hi = 2 * D // 512, 6 * D // 512
            nxt = load_w_slab(WringB, w_ada, s_lo * 512, 512)
            for s in range(s_lo, s_hi):
                Wt, rW = nxt
                if s + 1 < s_hi:
                    nxt = load_w_slab(WringB, w_ada, (s + 1) * 512, 512)
                for ct in range(4):
                    j = 4 * s + ct
                    for kc in range(KC):
                        pe.op(lambda h, Wt=Wt, ct=ct, kc=kc, j=j: h.matmul(
                            psmB[:, j:j + 1], lhsT=Wt[:, kc, ct * 128:(ct + 1) * 128], rhs=scb[:, kc:kc + 1],
                            start=(kc == 0), stop=(kc == KC - 1)),
                            reads=[rW, r_scb], writes=[r_psmB], inc=(kc == KC - 1))
            cw, r_cw = one(Kx, es, "cw", [128, NXT, 5], F32)
            cb, r_cb = one(Kx, es, "cb", [128, NXT], F32)
            sp.dma(cw[:], cw_fm[:, :, :], writes=[r_cw], semres=r_cw)
            sp.dma(cb[:], cb_fm[:, :], writes=[r_cb], semres=r_cb)
            Um = Ring(Kx, es, "Um", [128, T + 4], F32, 3)
            Uo = Ring(Kx, es, "Uo", [128, T + 4], F32, 3)
            accr = Ring(Kx, es, "cacc", [128, T], F32, 2)
            obr = Ring(Kx, es, "cob", [128, T], BF16, 3)
            zr = Ring(Kx, es, "zr", [128, DSSM], F32, 3)
            zo = Ring(Kx, es, "zo", [128, DSSM], F32, 2)
            for (u, ru) in Um.tiles:
                dve.op(lambda h, u=u: h.memset(u[:, 0:2], 0.0), writes=[ru])
            for (u, ru) in Uo.tiles:
                dve.op(lambda h, u=u: h.memset(u[:, T + 2:T + 4], 0.0), writes=[ru])

            def conv_tile(U, rU, i, dst):
                acc, racc = accr.next()
                dve.op(lambda h: h.tensor_scalar(out=acc[:], in0=U[:, 0:T], scalar1=cw[:, i, 0:1],
                                                 scalar2=cb[:, i:i + 1], op0=ALU.mult, op1=ALU.add),
                       reads=[rU, r_cw, r_cb], writes=[racc])
                for k in range(1, 5):
                    dve.op(lambda h, k=k: h.scalar_tensor_tensor(
                        out=acc[:], in0=U[:, k:k + T], scalar=cw[:, i, k:k + 1], in1=acc[:],
                        op0=ALU.mult, op1=ALU.add), reads=[rU, r_cw, racc], writes=[racc])
                ob, rob = obr.next()
                act.op(lambda h: h.activation(out=ob[:], in_=acc[:], func=AF.Silu), reads=[racc], writes=[rob])
                sp.dma(dst, ob[:], reads=[rob], semres=rob)

            def pipelined(n, load_fn, compute_fn, depth=2):
                q = [load_fn(i) for i in range(min(depth, n))]
                for i in range(n):
                    if i + depth < n:
                        q.append(load_fn(i + depth))
                    compute_fn(i, q[i])

            def ld_m(i):
                U, rU = Um.next()
                sp.dma(U[:, 2:T + 4], PTm[i, :, 0:T + 2], writes=[rU], semres=rU)
                return U, rU

            def ld_o(i):
                U, rU = Uo.next()
                sp.dma_multi([(U[:, 2:T + 2], PTo[i, :, 0:T]), (U[:, 0:2], PTm[i, :, T - 2:T])], writes=[rU], semres=rU)
                return U, rU

            def ld_z(tt):
                z, rz = zr.next()
                sp.dma(z[:], Ztm[tt * 128:(tt + 1) * 128, :], writes=[rz], semres=rz)
                return z, rz

            def do_z(tt, zz):
                z, rz = zz
                g, rg = zo.next()
                act.op(lambda h: h.activation(out=g[:], in_=z[:], func=AF.Silu), reads=[rz], writes=[rg])
                sp.dma(GZ[tt * 128:(tt + 1) * 128, :], g[:], reads=[rg], semres=rg)

            pipelined(NXT, ld_m, lambda i, u: conv_tile(u[0], u[1], i, XBC[i]))
            pipelined(NXO, ld_o, lambda i, u: conv_tile(u[0], u[1], i, XBO[i]))
            pipelined(NT, ld_z, do_z)
            dve.op(lambda h: h.tensor_tensor(out=modfm[:, 2 * KC:6 * KC], in0=psmB[:, 2 * KC:6 * KC], in1=badaB[:, 2 * KC:6 * KC], op=ALU.add),
                   reads=[r_psmB, r_badaB], writes=[r_mod])
            stage_end(es, mark)

          with ExitStack() as es:
            mark = len(Kx.all_res)
            p1, r_p1 = one(Kx, es, "p1", [128, 3, KC], F32)
            for q, src in enumerate((2, 4, 5)):
                dve.op(lambda h, q=q, src=src: h.tensor_scalar(
                    out=p1[:, q, :], in0=modfm[:, src * KC:(src + 1) * KC], scalar1=1.0, scalar2=None, op0=ALU.add),
                    reads=[r_mod], writes=[r_p1])
            dg, r_dg = one(Kx, es, "dg", [128, KC, 128], F32)
            bct, r_bct = one(Kx, es, "bct", [128, D], F32)
            lg, r_lg = one(Kx, es, "lg", [128, D], F32)
            lb, r_lb = one(Kx, es, "lb", [128, D], F32)
            sp.dma(lg[:], ln1g_bc[:, :], writes=[r_lg], semres=r_lg)
            sp.dma(lb[:], ln1b_bc[:, :], writes=[r_lb], semres=r_lb)
            psb = Ring(Kx, es, "psb", [128, 512], F32, 2, psum=True)

            def bcast_row(vec_ap):
                dve.op(lambda h: h.tensor_tensor(
                    out=dg[:], in0=cst[:, 0, :].unsqueeze(1).broadcast_to([128, KC, 128]),
                    in1=vec_ap.unsqueeze(2).broadcast_to([128, KC, 128]), op=ALU.mult),
                    reads=[r_cst, r_p1, r_mod], writes=[r_dg])
                for q in range(D // 512):
                    pb, rpb = psb.next()
                    pe.op(lambda h, pb=pb, q=q: h.matmul(pb[:], lhsT=cst[:, 1, :], rhs=dg[:, 4 * q:4 * q + 4, :],
                                                         start=True, stop=True), reads=[r_dg, r_cst], writes=[rpb])
                    evac_copy(q, bct[:, q * 512:(q + 1) * 512], pb[:], [rpb], [r_bct])

            bcast_row(p1[:, 0, :])
            sp.dma(BCS[0], bct[:], reads=[r_bct], semres=r_bct)
            bcast_row(p1[:, 1, :])
            dve.op(lambda h: h.tensor_tensor(out=lg[:], in0=lg[:], in1=bct[:], op=ALU.mult),
                   reads=[r_lg, r_bct], writes=[r_lg])
            dve.op(lambda h: h.tensor_tensor(out=lb[:], in0=lb[:], in1=bct[:], op=ALU.mult),
                   reads=[r_lb, r_bct], writes=[r_lb])
            sp.dma(BCS[1], lg[:], reads=[r_lg], semres=r_lg)
            bcast_row(modfm[:, 3 * KC:4 * KC])
            dve.op(lambda h: h.tensor_tensor(out=lb[:], in0=lb[:], in1=bct[:], op=ALU.add),
                   reads=[r_lb, r_bct], writes=[r_lb])
            sp.dma(BCS[2], lb[:], reads=[r_lb], semres=r_lb)
            bcast_row(p1[:, 2, :])
            sp.dma(BCS[3], bct[:], reads=[r_bct], semres=r_bct)
            stage_end(es, mark)

        if stop_after >= 4:
          with ExitStack() as es:
            mark = len(Kx.all_res)
            scw, r_scw = one(Kx, es, "scw", [128, NSC, 3], F32)
            scn, r_scn = one(Kx, es, "scn", [128, NSC], F32)
            sp.dma(scw[:], scw_fm[:, :, :], writes=[r_scw], semres=r_scw)
            sp.dma(scn[:], scn_fm[:, :], writes=[r_scn], semres=r_scn)
            Hr = Ring(Kx, es, "Hr", [128, T + 1], F32, 2)
            Cr = Ring(Kx, es, "Cr", [128, T + 1], F32, 2)
            Br = Ring(Kx, es, "Br", [128, T], F32, 2)
            Mr = Ring(Kx, es, "Mr", [128, T + 2], F32, 2)
            Ar = Ring(Kx, es, "Ar", [128, T], F32, 2)
            Sr = Ring(Kx, es, "Sr", [128, T], F32, 2)
            Rr = Ring(Kx, es, "Rr", [128, T], F32, 2)
            Or = Ring(Kx, es, "Or", [128, T], BF16, 2)
            pss = Ring(Kx, es, "pss", [128, 512], F32, 4, psum=True)
            for (m, rm) in Mr.tiles:
                dve.op(lambda h, m=m: h.memset(m[:, 0:1], 0.0), writes=[rm])
            def ld_sc(j):
                Hb, rH = Hr.next()
                Cb, rC = Cr.next()
                Bb, rB = Br.next()
                sp.dma(Hb[:], PTm[NXT + j, :, 0:T + 1], writes=[rH], semres=rH)
                sp.dma(Cb[:], PTm[NXT + 2 * NSC + j, :, 0:T + 1], writes=[rC], semres=rC)
                sp.dma(Bb[:], PTm[NXT + NSC + j, :, 0:T], writes=[rB], semres=rB)
                return (Hb, rH, Cb, rC, Bb, rB)

            scq = [ld_sc(0)]
            for j in range(NSC):
                if j + 1 < NSC:
                    scq.append(ld_sc(j + 1))
                Hb, rH, Cb, rC, Bb, rB = scq[j]
                M, rM = Mr.next()
                dve.op(lambda h, M=M, Cb=Cb, Hb=Hb: h.tensor_tensor(out=M[:, 1:T + 2], in0=Cb[:], in1=Hb[:], op=ALU.mult),
                       reads=[rC, rH], writes=[rM])
                A, rA = Ar.next()
                dve.op(lambda h, A=A, M=M, j=j: h.tensor_scalar(out=A[:], in0=M[:, 0:T], scalar1=scw[:, j, 0:1],
                                                                scalar2=None, op0=ALU.mult),
                       reads=[rM, r_scw], writes=[rA])
                for k in (1, 2):
                    dve.op(lambda h, A=A, M=M, j=j, k=k: h.scalar_tensor_tensor(
                        out=A[:], in0=M[:, k:k + T], scalar=scw[:, j, k:k + 1], in1=A[:], op0=ALU.mult, op1=ALU.add),
                        reads=[rM, r_scw, rA], writes=[rA])
                dve.op(lambda h, A=A, Bb=Bb: h.tensor_tensor(out=A[:], in0=A[:], in1=Bb[:], op=ALU.mult),
                       reads=[rA, rB], writes=[rA])
                S, rS = Sr.next()
                act.op(lambda h, S=S, A=A: h.activation(out=S[:], in_=A[:], func=AF.Square), reads=[rA], writes=[rS])
                R, rR = Rr.next()
                for q in range(T // 512):
                    ps, rps = pss.next()
                    pe.op(lambda h, ps=ps, S=S, q=q: h.matmul(ps[:], lhsT=onesdiv[:], rhs=S[:, q * 512:(q + 1) * 512],
                                                              start=True, stop=True), reads=[rS, r_onesdiv], writes=[rps])
                    act.op(lambda h, ps=ps, R=R, q=q: h.activation(out=R[:, q * 512:(q + 1) * 512], in_=ps[:], func=AF.Ln,
                                                                   bias=RMS_EPS), reads=[rps], writes=[rR])
                act.op(lambda h, R=R: h.activation(out=R[:], in_=R[:], func=AF.Exp, scale=-0.5), reads=[rR], writes=[rR])
                O, rO = Or.next()
                dve.op(lambda h, O=O, A=A, R=R, j=j: h.scalar_tensor_tensor(
                    out=O[:], in0=A[:], scalar=scn[:, j:j + 1], in1=R[:], op0=ALU.mult, op1=ALU.mult),
                    reads=[rA, rR, r_scn], writes=[rO])
                sp.dma(YT[SSMT + j], O[:], reads=[rO], semres=rO)
            stage_end(es, mark)

        if stop_after >= 5:
          with ExitStack() as es:
            mark = len(Kx.all_res)
            def sb(name, shape, dt=F32):
                return one(Kx, es, name, shape, dt)
            dtb, r_dtb = sb("dtb", [128, H2])
            alg, r_alg = sb("alg", [128, H2])
            dsk, r_dsk = sb("dsk", [128, DSSM])
            nwb, r_nwb = sb("nwb", [128, DSSM])
            for t_, r_, s_ in ((dtb, r_dtb, dtb_bc), (alg, r_alg, alog_bc), (dsk, r_dsk, dsk_bc), (nwb, r_nwb, nw_bc)):
                sp.dma(t_[:], s_[:, :], writes=[r_], semres=r_)
            dtv, r_dtv = sb("dtv", [128, NTT, H2])
            adt, r_adt = sb("adt", [128, NTT, H2])
            wv, r_wv = sb("wv", [128, NTT, H2])
            cdv, r_cd = sb("cdv", [128, NTT, H2])
            eQ, r_eQ = sb("eQ", [128, NTT, H2])
            psq = Ring(Kx, es, "psq", [128, 512], F32, 2, psum=True)
            pstA, r_pstA = one(Kx, es, "pstA", [128, 1024], BF16, psum=True)
            psseg = Ring(Kx, es, "psseg", [128, 512], F32, 2, psum=True)
            psYr = Ring(Kx, es, "psY", [128, 512], F32, 2, psum=True)
            psO, r_psO = one(Kx, es, "psO", [128, 512], F32, psum=True)

            tes = ExitStack()
            tmark = len(Kx.all_res)
            xb, r_xb = one(Kx, tes, "xb", [128, NTT, H2], F32)
            tA, r_tA = one(Kx, tes, "tA", [128, NTT, H2], F32)
            tB, r_tB = one(Kx, tes, "tB", [128, NTT, H2], F32)
            Qv, r_Q = one(Kx, tes, "Qv", [128, NTT, H2], F32)
            Qt, r_Qt = one(Kx, tes, "Qt", [128, NTT, H2], F32)
            bc3 = lambda ap: ap.unsqueeze(1).broadcast_to([128, NTT, H2])
            dve.op(lambda h: h.tensor_tensor(out=xb[:], in0=dtraw[:], in1=bc3(dtb[:]), op=ALU.add),
                   reads=[r_dtraw, r_dtb], writes=[r_xb])
            dve.op(lambda h: h.scalar_tensor_tensor(out=tA[:], in0=xb[:], scalar=-1.0, in1=xb[:], op0=ALU.mult, op1=ALU.min),
                   reads=[r_xb], writes=[r_tA])
            act.op(lambda h: h.activation(out=tA[:], in_=tA[:], func=AF.Exp), reads=[r_tA], writes=[r_tA])
            act.op(lambda h: h.activation(out=tA[:], in_=tA[:], func=AF.Ln, bias=1.0), reads=[r_tA], writes=[r_tA])
            dve.op(lambda h: h.scalar_tensor_tensor(out=dtv[:], in0=xb[:], scalar=0.0, in1=tA[:], op0=ALU.max, op1=ALU.add),
                   reads=[r_xb, r_tA], writes=[r_dtv])
            act.op(lambda h: h.activation(out=alg[:], in_=alg[:], func=AF.Exp), reads=[r_alg], writes=[r_alg])
            dve.op(lambda h: h.scalar_tensor_tensor(out=adt[:], in0=dtv[:], scalar=-1.0, in1=bc3(alg[:]),
                                                    op0=ALU.mult, op1=ALU.mult), reads=[r_dtv, r_alg], writes=[r_adt])
            for c in range(NTT):
                pq, rpq = psq.next()
                pe.op(lambda h, pq=pq, c=c: h.matmul(pq[:, 0:NH], lhsT=cst[:, 2, :], rhs=adt[:, c, 0:NH], start=True, stop=True),
                      reads=[r_adt, r_cst], writes=[rpq])
                pe.op(lambda h, pq=pq, c=c: h.matmul(pq[:, NH:H2], lhsT=cst[:, 3, :], rhs=adt[:, c, NH:H2], start=True, stop=True),
                      reads=[r_adt, r_cst], writes=[rpq])
                pe.op(lambda h, pq=pq, c=c: h.matmul(pq[:, H2:2 * H2], lhsT=cst[:, 1, :], rhs=adt[:, c, :], start=True, stop=True),
                      reads=[r_adt, r_cst], writes=[rpq])
                dve.op(lambda h, pq=pq, c=c: h.tensor_copy(out=Qv[:, c, :], in_=pq[:, 0:H2]), reads=[rpq], writes=[r_Q])
                act.op(lambda h, pq=pq, c=c: h.activation(out=Qt[:, c, :], in_=pq[:, H2:2 * H2], func=AF.Copy),
                       reads=[rpq], writes=[r_Qt])
            dve.op(lambda h: h.tensor_tensor(out=tB[:], in0=Qt[:], in1=Qv[:], op=ALU.subtract), reads=[r_Qt, r_Q], writes=[r_tB])
            act.op(lambda h: h.activation(out=tB[:], in_=tB[:], func=AF.Exp), reads=[r_tB], writes=[r_tB])
            dve.op(lambda h: h.tensor_tensor(out=wv[:], in0=dtv[:], in1=tB[:], op=ALU.mult), reads=[r_dtv, r_tB], writes=[r_wv])
            act.op(lambda h: h.activation(out=cdv[:], in_=Qt[:], func=AF.Exp), reads=[r_Qt], writes=[r_cd])
            act.op(lambda h: h.activation(out=eQ[:], in_=Qv[:], func=AF.Exp), reads=[r_Q], writes=[r_eQ])
            if debug:
                for qi, (t_, r_) in enumerate(((dtv, r_dtv), (adt, r_adt), (Qv, r_Q), (Qt, r_Qt), (wv, r_wv), (cdv, r_cd), (eQ, r_eQ))):
                    sp.dma(DBG[:, qi * NTT * H2:(qi + 1) * NTT * H2], t_[:].rearrange("p a b -> p (a b)"), reads=[r_], semres=r_)

            Kx.barrier()
            Kx.recycle(tmark)
            tes.close()
            Xf = Ring(Kx, es, "Xf", [128, 2, T], BF16, 2)
            Xo = Ring(Kx, es, "Xo", [128, 2, T], BF16, 1)
            BTr = Ring(Kx, es, "BTr", [128, T], BF16, 2)
            CTr = Ring(Kx, es, "CTr", [128, T], BF16, 2)
            BOr = Ring(Kx, es, "BOr", [128, T], BF16, 1)
            GZr = Ring(Kx, es, "GZr", [128, NT, 256], F32, 1)
            xtm, r_xtm = sb("xtm", [128, NTT, 256], BF16)
            btm, r_btm = sb("btm", [128, NTT, 128], BF16)
            prevb, r_prevb = sb("prevb", [128, NT, 256], BF16)
            carry, r_carry = sb("carry", [128, 256])
            ctmp, r_ctmp = sb("ctmp", [128, 256])
            prevf, r_prevf = sb("prevf", [128, 256], BF16)
            xwr = Ring(Kx, es, "xw", [128, 256], BF16, 3)
            xdr = Ring(Kx, es, "xd", [128, 256], BF16, 4)
            t3r = Ring(Kx, es, "t3", [128, 256], F32, 2)
            ls4r = Ring(Kx, es, "ls4", [128, 4, 128], F32, 2)
            dc4r = Ring(Kx, es, "dc4", [128, 4, 128], F32, 4)
            lt4r = Ring(Kx, es, "lt4", [128, 4, 128], BF16, 4)
            Sfr = Ring(Kx, es, "Sf", [128, 128], F32, 3)
            Sbr = Ring(Kx, es, "Sb", [128, 128], F32, 3)
            t1r = Ring(Kx, es, "t1", [128, 256], F32, 2)
            t2r = Ring(Kx, es, "t2", [128, 256], F32, 2)
            sqr = Ring(Kx, es, "sq", [128, 256], F32, 2)
            ssr = Ring(Kx, es, "ss", [128, 2], F32, 2)
            ynr = Ring(Kx, es, "yn", [128, 256], BF16, 2)
            yTs = Ring(Kx, es, "yTs", [128, 2, T], BF16, 1)

            def hb(ap4):
                return ap4.unsqueeze(2).broadcast_to([128, 4, 64])

            def v4(ap):
                return ap.rearrange("p (h q) -> p h q", h=4)

            def load_main(g):
                X, rX = Xf.next()
                BT, rBT = BTr.next()
                CT, rCT = CTr.next()
                sp.dma_multi([(X[:, half, :], XBC[2 * g + half]) for half in range(2)], writes=[rX], semres=rX)
                sp.dma(BT[:], XBC[SSMT + g], writes=[rBT], semres=rBT)
                sp.dma(CT[:], XBC[SSMT + NG + g], writes=[rCT], semres=rCT)
                return (X, rX, BT, rBT, CT, rCT)

            def load_other(g):
                XO, rXO = Xo.next()
                BO, rBO = BOr.next()
                sp.dma_multi([(XO[:, half, :], XBO[2 * g + half]) for half in range(2)], writes=[rXO], semres=rXO)
                sp.dma(BO[:], XBO[SSMT + g], writes=[rBO], semres=rBO)
                return (XO, rXO, BO, rBO)

            def load_gz(g):
                Gz, rGz = GZr.next()
                sp.dma(Gz[:], GZ.rearrange("(c p) n -> p c n", p=128)[:, :, g * 256:(g + 1) * 256], writes=[rGz], semres=rGz)
                return (Gz, rGz)

            def emit_xw(c, colbase, g):
                xw, rxw = xwr.next()
                pool.op(lambda h: h.tensor_tensor(out=v4(xw[:]), in0=v4(xtm[:, c, :]),
                                                  in1=hb(wv[:, c, colbase + 4 * g:colbase + 4 * g + 4]), op=ALU.mult),
                        reads=[r_xtm, r_wv], writes=[rxw])
                return xw, rxw

            def emit_state(c, colbase, g, xw, rxw):
                ps, rps = psq.next()
                pe.op(lambda h: h.matmul(ps[:, 0:256], lhsT=btm[:, c, :], rhs=xw[:], start=True, stop=True),
                      reads=[r_btm, rxw], writes=[rps])
                dve.op(lambda h: h.tensor_tensor(out=v4(ctmp[:]), in0=v4(carry[:]),
                                                 in1=hb(cdv[:, c, colbase + 4 * g:colbase + 4 * g + 4]), op=ALU.mult),
                       reads=[r_carry, r_cd], writes=[r_ctmp])
                dve.op(lambda h: h.tensor_tensor(out=carry[:], in0=ctmp[:], in1=ps[:, 0:256], op=ALU.add),
                       reads=[r_ctmp, rps], writes=[r_carry])

            nxt = load_main(0)
            nxo = load_other(0)
            for g in range(NG):
                X, rX, BT, rBT, CT, rCT = nxt
                XO, rXO, BO, rBO = nxo
                if g + 1 < NG:
                    nxt = load_main(g + 1)

                def p1_front(c):
                    if c >= NT:
                        xs_, rxs_, bs_, rbs_, cc = XO, rXO, BO, rBO, c - NT
                    else:
                        xs_, rxs_, bs_, rbs_, cc = X, rX, BT, rBT, c
                    for half in range(2):
                        pe.op(lambda h, half=half: h.transpose(
                            out=pstA[:, half * 128:(half + 1) * 128], in_=xs_[:, half, cc * 128:(cc + 1) * 128],
                            identity=identb[:]), reads=[rxs_, r_identb], writes=[r_pstA], inc=False)
                    pe.op(lambda h: h.transpose(
                        out=pstA[:, 256:384], in_=bs_[:, cc * 128:(cc + 1) * 128], identity=identb[:]),
                        reads=[rbs_, r_identb], writes=[r_pstA])
                    act.op(lambda h: h.activation(out=xtm[:, c, :], in_=pstA[:, 0:256], func=AF.Copy),
                           reads=[r_pstA], writes=[r_xtm])
                    dve.op(lambda h: h.tensor_copy(out=btm[:, c, :], in_=pstA[:, 256:384]),
                           reads=[r_pstA], writes=[r_btm])
                    return emit_xw(c, NH, g)

                dve.op(lambda h: h.memset(carry[:], 0.0), writes=[r_carry])
                pend = p1_front(NTT - 1)
                for c in range(NTT - 1, -1, -1):
                    cur = pend
                    if c - 1 >= 0:
                        pend = p1_front(c - 1)
                    if c < NT:
                        act.op(lambda h, c=c: h.activation(out=prevb[:, c, :], in_=carry[:], func=AF.Copy),
                               reads=[r_carry], writes=[r_prevb])
                    emit_state(c, NH, g, *cur)
                if g + 1 < NG:
                    nxo = load_other(g + 1)
                Gz, rGz = load_gz(g)

                def p2_A(c):
                    csl = slice(c * 128, (c + 1) * 128)
                    pssc, r_pssc = psq.next()
                    pe.op(lambda h: h.matmul(pssc[:, 0:128], lhsT=BT[:, csl], rhs=CT[:, csl], start=True, stop=True),
                          reads=[rBT, rCT], writes=[r_pssc])
                    st = {"c": c, "csl": csl}
                    decs = []
                    for d in range(2):
                        c0 = d * NH + 4 * g
                        ls, rls = ls4r.next()
                        pool.op(lambda h, ls=ls, d=d, c0=c0: h.tensor_tensor(
                            out=ls[:], in0=cst[:, 4 + d, :].unsqueeze(1).broadcast_to([128, 4, 128]),
                            in1=adt[:, c, c0:c0 + 4].unsqueeze(2).broadcast_to([128, 4, 128]), op=ALU.mult),
                            reads=[r_cst, r_adt], writes=[rls])
                        pg, rpg = psseg.next()
                        for hh in range(4):
                            pe.op(lambda h, pg=pg, ls=ls, d=d, hh=hh: h.matmul(
                                pg[:, hh * 128:(hh + 1) * 128], lhsT=ls[:, hh, :], rhs=cst[:, 2 + d, :], start=True, stop=True),
                                reads=[rls, r_cst], writes=[rpg], inc=(hh == 3))
                        dc, rdc = dc4r.next()
                        act.op(lambda h, dc=dc, pg=pg: h.activation(out=dc[:].rearrange("p a b -> p (a b)"), in_=pg[:], func=AF.Exp),
                               reads=[rpg], writes=[rdc])
                        decs.append((dc, rdc))
                    st["decs"] = decs
                    Sf, rSf = Sfr.next()
                    Sb, rSb = Sbr.next()
                    dve.op(lambda h: h.tensor_tensor(out=Sf[:], in0=pssc[:, 0:128], in1=cst[:, 2, :], op=ALU.mult),
                           reads=[r_pssc, r_cst], writes=[rSf])
                    dve.op(lambda h: h.tensor_tensor(out=Sb[:], in0=pssc[:, 0:128], in1=cst[:, 3, :], op=ALU.mult),
                           reads=[r_pssc, r_cst], writes=[rSb])
                    st["S"] = [(Sf, rSf), (Sb, rSb)]
                    xds = []
                    for d in range(2):
                        c0 = d * NH + 4 * g
                        xd, rxd = xdr.next()
                        pool.op(lambda h, xd=xd, c0=c0: h.tensor_tensor(out=v4(xd[:]), in0=v4(xtm[:, c, :]),
                                                                        in1=hb(dtv[:, c, c0:c0 + 4]), op=ALU.mult),
                                reads=[r_xtm, r_dtv], writes=[rxd])
                        xds.append((xd, rxd))
                    st["xd"] = xds
                    st["xw"] = emit_xw(c, 0, g)
                    t3, rt3 = t3r.next()
                    pool.op(lambda h: h.tensor_tensor(out=t3[:], in0=xtm[:, c, :], in1=dsk[:, g * 256:(g + 1) * 256], op=ALU.mult),
                            reads=[r_xtm, r_dsk], writes=[rt3])
                    st["t3"] = (t3, rt3)
                    return st

                def p2_B(st):
                    c, csl = st["c"], st["csl"]
                    lts = []
                    for d in range(2):
                        dc, rdc = st["decs"][d]
                        Sd, rSd = st["S"][d]
                        Lt, rLt = lt4r.next()
                        dve.op(lambda h, Lt=Lt, dc=dc, Sd=Sd: h.tensor_tensor(
                            out=Lt[:], in0=dc[:], in1=Sd[:].unsqueeze(1).broadcast_to([128, 4, 128]), op=ALU.mult),
                            reads=[rdc, rSd], writes=[rLt])
                        lts.append((Lt, rLt))
                    pY, rpY = psYr.next()
                    for hh in range(4):
                        for d in range(2):
                            Lt, rLt = lts[d]
                            xd, rxd = st["xd"][d]
                            pe.op(lambda h, Lt=Lt, xd=xd, hh=hh, d=d: h.matmul(
                                pY[:, hh * 64:(hh + 1) * 64], lhsT=Lt[:, hh, :], rhs=xd[:, hh * 64:(hh + 1) * 64],
                                start=(d == 0), stop=(d == 1)), reads=[rLt, rxd], writes=[rpY], inc=(d == 1))
                    st["pY"] = (pY, rpY)
                    act.op(lambda h: h.activation(out=prevf[:], in_=carry[:], func=AF.Copy), reads=[r_carry], writes=[r_prevf])
                    pe.op(lambda h: h.matmul(psO[:, 0:256], lhsT=CT[:, csl], rhs=prevf[:], start=True, stop=True),
                          reads=[rCT, r_prevf], writes=[r_psO], inc=False)
                    pe.op(lambda h: h.matmul(psO[:, 256:512], lhsT=CT[:, csl], rhs=prevb[:, c, :], start=True, stop=True),
                          reads=[rCT, r_prevb], writes=[r_psO])
                    emit_state(c, 0, g, *st["xw"])

                def p2_C(st, yT, ryT):
                    c, csl = st["c"], st["csl"]
                    pY, rpY = st["pY"]
                    t3, rt3 = st["t3"]
                    t1, rt1 = t1r.next()
                    t2, rt2 = t2r.next()
                    dve.op(lambda h: h.tensor_tensor(out=v4(t1[:]), in0=v4(psO[:, 0:256]),
                                                     in1=hb(eQ[:, c, 4 * g:4 * g + 4]), op=ALU.mult),
                           reads=[r_psO, r_eQ], writes=[rt1])
                    dve.op(lambda h: h.tensor_tensor(out=v4(t2[:]), in0=v4(psO[:, 256:512]),
                                                     in1=hb(eQ[:, c, NH + 4 * g:NH + 4 * g + 4]), op=ALU.mult),
                           reads=[r_psO, r_eQ], writes=[rt2])
                    dve.op(lambda h: h.tensor_tensor(out=t2[:], in0=t2[:], in1=t3[:], op=ALU.add),
                           reads=[rt2, rt3], writes=[rt2])
                    dve.op(lambda h: h.tensor_tensor(out=t1[:], in0=t1[:], in1=pY[:, 0:256], op=ALU.add),
                           reads=[rt1, rpY], writes=[rt1])
                    dve.op(lambda h: h.tensor_tensor(out=t1[:], in0=t1[:], in1=t2[:], op=ALU.add),
                           reads=[rt1, rt2], writes=[rt1])
                    dve.op(lambda h: h.tensor_tensor(out=t1[:], in0=t1[:], in1=Gz[:, c, :], op=ALU.mult),
                           reads=[rt1, rGz], writes=[rt1])
                    sq, rsq = sqr.next()
                    ss, rss = ssr.next()
                    act.op(lambda h: h.activation(out=sq[:], in_=t1[:], func=AF.Square, accum_out=ss[:, 0:1]),
                           reads=[rt1], writes=[rsq, rss])
                    act.op(lambda h: h.activation(out=ss[:, 1:2], in_=ss[:, 0:1], func=AF.Ln, scale=1.0 / 256.0, bias=RMS_EPS),
                           reads=[rss], writes=[rss])
                    act.op(lambda h: h.activation(out=ss[:, 1:2], in_=ss[:, 1:2], func=AF.Exp, scale=-0.5),
                           reads=[rss], writes=[rss])
                    yn, ryn = ynr.next()
                    dve.op(lambda h: h.scalar_tensor_tensor(
                        out=yn[:], in0=t1[:], scalar=ss[:, 1:2], in1=nwb[:, g * 256:(g + 1) * 256], op0=ALU.mult, op1=ALU.mult),
                        reads=[rt1, rss, r_nwb], writes=[ryn])
                    for half in range(2):
                        pe.op(lambda h, half=half: h.transpose(
                            out=pstA[:, 512 + half * 128:512 + (half + 1) * 128], in_=yn[:, half * 128:(half + 1) * 128], identity=identb[:]),
                            reads=[ryn, r_identb], writes=[r_pstA], inc=(half == 1))
                    act.op(lambda h: h.activation(
                        out=yT[:, :, csl], in_=pstA[:, 512:768].rearrange("p (a b) -> p a b", a=2), func=AF.Copy),
                        reads=[r_pstA], writes=[ryT])

                dve.op(lambda h: h.memset(carry[:], 0.0), writes=[r_carry])
                yT, ryT = yTs.next()
                stn = p2_A(0)
                for c in range(NT):
                    stc = stn
                    if c + 1 < NT:
                        stn = p2_A(c + 1)
                    p2_B(stc)
                    p2_C(stc, yT, ryT)
                sp.dma_multi([(YT[2 * g + half], yT[:, half, :]) for half in range(2)], reads=[ryT], semres=ryT)
            stage_end(es, mark)

        def gemm_tm(es, src_T, wsrc, ncols, dst, tag):
            Wring = Ring(Kx, es, "W%s_" % tag, [128, KC, 512], BF16, 2)
            aT, r_aT = one(Kx, es, "aT" + tag, [128, KC, TB], BF16)
            stz = Ring(Kx, es, "st" + tag, [128, 512], F32, 4)
            psacc = Ring(Kx, es, "ps" + tag, [128, 512], F32, 6, psum=True)
            ev = 0
            for b in range(T // TB):
                sv = src_T.rearrange("k p t -> p k t")
                sp.dma_multi([(aT[:, k0:k0 + 8, :], sv[:, k0:k0 + 8, b * TB:(b + 1) * TB]) for k0 in range(0, KC, 8)],
                             writes=[r_aT], semres=r_aT)
                nsl = ncols // 512
                nxt = load_w_slab(Wring, wsrc, 0, 512)
                for s in range(nsl):
                    Wt, rW = nxt
                    if s + 1 < nsl:
                        nxt = load_w_slab(Wring, wsrc, (s + 1) * 512, 512)
                    for tt in range(TBt):
                        pa, rpa = psacc.next()
                        for kc in range(KC):
                            pe.op(lambda h, pa=pa, tt=tt, kc=kc, Wt=Wt: h.matmul(
                                pa[:], lhsT=aT[:, kc, tt * 128:(tt + 1) * 128], rhs=Wt[:, kc, :],
                                start=(kc == 0), stop=(kc == KC - 1)), reads=[r_aT, rW], writes=[rpa], inc=(kc == KC - 1))
                        sz, rsz = stz.next()
                        evac_copy(ev, sz[:], pa[:], [rpa], [rsz])
                        ev += 1
                        r0 = b * TB + tt * 128
                        sp.dma(dst[r0:r0 + 128, s * 512:(s + 1) * 512], sz[:], reads=[rsz], semres=rsz)

        if stop_after >= 6:
          with ExitStack() as es:
            mark = len(Kx.all_res)
            gemm_tm(es, YT, w_out, D, MIX, "4")
            stage_end(es, mark)

        def ln_stage(es, branch, resid, gate_idx, g_src, b_src, dst, with_h2, tag):
            gbc, r_gbc = one(Kx, es, "gbc" + tag, [128, D], F32)
            lg, r_lg = one(Kx, es, "lg" + tag, [128, D], F32)
            lb, r_lb = one(Kx, es, "lb" + tag, [128, D], F32)
            sp.dma(gbc[:], BCS[gate_idx], writes=[r_gbc], semres=r_gbc)
            sp.dma(lg[:], g_src[:, :], writes=[r_lg], semres=r_lg)
            sp.dma(lb[:], b_src[:, :], writes=[r_lb], semres=r_lb)
            if with_h2:
                G2, r_G2 = one(Kx, es, "G2" + tag, [128, D], F32)
                B2, r_B2 = one(Kx, es, "B2" + tag, [128, D], F32)
                sp.dma(G2[:], BCS[1], writes=[r_G2], semres=r_G2)
                sp.dma(B2[:], BCS[2], writes=[r_B2], semres=r_B2)
                h2r = Ring(Kx, es, "h2" + tag, [128, D], BF16, 1)
                h2s = Ring(Kx, es, "h2s" + tag, [128, KC, 256], BF16, 1)
                pst = Ring(Kx, es, "pst" + tag, [128, 1024], BF16, 4, psum=True)
            nring = 2 if with_h2 else 3
            mr = Ring(Kx, es, "mr" + tag, [128, D], F32, nring)
            xr = Ring(Kx, es, "xr" + tag, [128, D], F32, nring)
            str_ = Ring(Kx, es, "bs" + tag, [128, D // 512, 6], F32, 3)
            mvr = Ring(Kx, es, "mv" + tag, [128, 4], F32, 3)
            ev = [0]
            hsb = [None]

            def ln_load(tt):
                rows = slice(tt * 128, (tt + 1) * 128)
                m, rm = mr.next()
                xx, rxx = xr.next()
                sp.dma(m[:], branch[rows, :], writes=[rm], semres=rm)
                sp.dma(xx[:], resid[rows, :], writes=[rxx], semres=rxx)
                return (m, rm, xx, rxx)

            def ln_A(tt, ld):
                m, rm, xx, rxx = ld
                pool.op(lambda h: h.tensor_tensor(out=m[:], in0=m[:], in1=gbc[:], op=ALU.mult), reads=[rm, r_gbc], writes=[rm])
                dve.op(lambda h: h.scalar_tensor_tensor(out=xx[:], in0=xx[:], scalar=alpha, in1=m[:], op0=ALU.mult, op1=ALU.add),
                       reads=[rxx, rm], writes=[rxx])
                st, rst = str_.next()
                for q in range(D // 512):
                    dve.op(lambda h, q=q: h.bn_stats(out=st[:, q, :], in_=xx[:, q * 512:(q + 1) * 512]),
                           reads=[rxx], writes=[rst])
                mv, rmv = mvr.next()
                dve.op(lambda h: h.bn_aggr(out=mv[:, 0:2], in_=st[:].rearrange("p a b -> p (a b)")),
                       reads=[rst], writes=[rmv])
                act.op(lambda h: h.activation(out=mv[:, 2:3], in_=mv[:, 1:2], func=AF.Ln, bias=LN_EPS), reads=[rmv], writes=[rmv])
                act.op(lambda h: h.activation(out=mv[:, 2:3], in_=mv[:, 2:3], func=AF.Exp, scale=-0.5), reads=[rmv], writes=[rmv])
                return (mv, rmv)

            def ln_B(tt, ld, mvv):
                rows = slice(tt * 128, (tt + 1) * 128)
                m, rm, xx, rxx = ld
                mv, rmv = mvv
                dve.op(lambda h: h.scalar_tensor_tensor(out=mv[:, 3:4], in0=mv[:, 0:1], scalar=-1.0, in1=mv[:, 2:3],
                                                        op0=ALU.mult, op1=ALU.mult), reads=[rmv], writes=[rmv])
                act.op(lambda h: h.activation(out=xx[:], in_=xx[:], func=AF.Identity, bias=mv[:, 3:4], scale=mv[:, 2:3]),
                       reads=[rxx, rmv], writes=[rxx])
                dve.op(lambda h: h.tensor_tensor(out=m[:], in0=xx[:], in1=lg[:], op=ALU.mult), reads=[rxx, r_lg], writes=[rm])
                pool.op(lambda h: h.tensor_tensor(out=m[:], in0=m[:], in1=lb[:], op=ALU.add), reads=[rm, r_lb], writes=[rm])
                sp.dma(dst[rows, :], m[:], reads=[rm], semres=rm)
                if with_h2:
                    h2, rh2 = h2r.next()
                    dve.op(lambda h: h.tensor_tensor(out=xx[:], in0=xx[:], in1=G2[:], op=ALU.mult), reads=[rxx, r_G2], writes=[rxx])
                    dve.op(lambda h: h.tensor_tensor(out=h2[:], in0=xx[:], in1=B2[:], op=ALU.add), reads=[rxx, r_B2], writes=[rh2])
                    if tt % 2 == 0:
                        hsb[0] = h2s.next()
                    hs, rhs = hsb[0]
                    for q8 in range(KC // 8):
                        pt, rpt = pst.next()
                        for q in range(8):
                            kc = q8 * 8 + q
                            pe.op(lambda h, q=q, kc=kc: h.transpose(
                                out=pt[:, q * 128:(q + 1) * 128], in_=h2[:, kc * 128:(kc + 1) * 128], identity=identb[:]),
                                reads=[rh2, r_identb], writes=[rpt], inc=(q == 7))
                        o_ap = hs[:, q8 * 8:(q8 + 1) * 8, (tt % 2) * 128:(tt % 2 + 1) * 128]
                        i_ap = pt[:].rearrange("p (a b) -> p a b", a=8)
                        evac_copy(ev[0], o_ap, i_ap, [rpt], [rhs])
                        ev[0] += 1
                    if tt % 2 == 1:
                        t0 = (tt - 1) * 128
                        hv = H2T.rearrange("k p t -> p k t")
                        sp.dma_multi([(hv[:, k0:k0 + 8, t0:t0 + 256], hs[:, k0:k0 + 8, :]) for k0 in range(0, KC, 8)],
                                     reads=[rhs], semres=rhs)

            lds = [ln_load(0)]
            pend = None
            for tt in range(NT):
                if nring >= 3 and tt + 1 < NT:
                    lds.append(ln_load(tt + 1))
                mvv = ln_A(tt, lds[tt])
                if pend is not None:
                    ln_B(*pend)
                if nring < 3 and tt + 1 < NT:
                    lds.append(ln_load(tt + 1))
                pend = (tt, lds[tt], mvv)
            ln_B(*pend)

        if stop_after >= 7:
          with ExitStack() as es:
            mark = len(Kx.all_res)
            ln_stage(es, MIX, x_in, 0, ln1g_bc, ln1b_bc, X1, True, "5")
            stage_end(es, mark)

        if stop_after >= 8:
          with ExitStack() as es:
            mark = len(Kx.all_res)
            Wring = Ring(Kx, es, "W6_", [128, KC, 512], BF16, 2)
            aT, r_aT = one(Kx, es, "aT6", [128, KC, TB], BF16)
            rr = Ring(Kx, es, "rl6", [128, 512], F32, 3)
            us = Ring(Kx, es, "us6", [128, TB], BF16, 3)
            psacc = Ring(Kx, es, "ps6", [128, 512], F32, 6, psum=True)
            for b in range(T // TB):
                sv = H2T.rearrange("k p t -> p k t")
                sp.dma_multi([(aT[:, k0:k0 + 8, :], sv[:, k0:k0 + 8, b * TB:(b + 1) * TB]) for k0 in range(0, KC, 8)],
                             writes=[r_aT], semres=r_aT)
                nsl = DFF // 512
                nxt = load_w_slab(Wring, w_up, 0, 512)
                for s in range(nsl):
                    Wt, rW = nxt
                    if s + 1 < nsl:
                        nxt = load_w_slab(Wring, w_up, (s + 1) * 512, 512)
                    for ct in range(4):
                        u, ru = us.next()
                        for sub in range(TB // 512):
                            pa, rpa = psacc.next()
                            for kc in range(KC):
                                pe.op(lambda h, pa=pa, kc=kc, Wt=Wt, ct=ct, sub=sub: h.matmul(
                                    pa[:], lhsT=Wt[:, kc, ct * 128:(ct + 1) * 128], rhs=aT[:, kc, sub * 512:(sub + 1) * 512],
                                    start=(kc == 0), stop=(kc == KC - 1)), reads=[r_aT, rW], writes=[rpa], inc=(kc == KC - 1))
                            r_, rr_ = rr.next()
                            act.op(lambda h, r_=r_, pa=pa: h.activation(out=r_[:], in_=pa[:], func=AF.Relu), reads=[rpa], writes=[rr_])
                            dve.op(lambda h, r_=r_, u=u, sub=sub: h.tensor_tensor(out=u[:, sub * 512:(sub + 1) * 512], in0=r_[:], in1=r_[:], op=ALU.mult),
                                   reads=[rr_], writes=[ru])
                        sp.dma(UT[4 * s + ct, :, b * TB:(b + 1) * TB], u[:], reads=[ru], semres=ru)
            stage_end(es, mark)

        if stop_after >= 9:
          with ExitStack() as es:
            mark = len(Kx.all_res)
            FCG = 8
            uT, r_uT = one(Kx, es, "uT7", [128, FC, 512], BF16)
            Wd = Ring(Kx, es, "Wd7", [128, 2, FCG, 512], BF16, 2)
            stz = Ring(Kx, es, "st7", [128, 512], F32, 4)
            ps8 = [one(Kx, es, "ps7_%d" % i, [128, 512], F32, psum=True) for i in range(8)]
            wdv = w_down.rearrange("(fc p) n -> p fc n", p=128)

            def load_wd(sp_i, fcg):
                W_, rW_ = Wd.next()
                pool.dma_multi([(W_[:, s, :, :], wdv[:, fcg * FCG:(fcg + 1) * FCG, (2 * sp_i + s) * 512:(2 * sp_i + s + 1) * 512])
                                for s in range(2)], writes=[rW_], semres=rW_)
                return W_, rW_

            ev = 0
            for b in range(T // 512):
                sv = UT.rearrange("f p t -> p f t")
                sp.dma_multi([(uT[:, k0:k0 + 8, :], sv[:, k0:k0 + 8, b * 512:(b + 1) * 512]) for k0 in range(0, FC, 8)],
                             writes=[r_uT], semres=r_uT)
                seq = [(spi, fcg) for spi in range(D // 1024) for fcg in range(FC // FCG)]
                nxt = load_wd(*seq[0])
                for qi, (spi, fcg) in enumerate(seq):
                    W_, rW_ = nxt
                    if qi + 1 < len(seq):
                        nxt = load_wd(*seq[qi + 1])
                    for fcl in range(FCG):
                        fc = fcg * FCG + fcl
                        for tt in range(4):
                            for s in range(2):
                                pa, rpa = ps8[tt * 2 + s]
                                pe.op(lambda h, pa=pa, fc=fc, tt=tt, s=s, fcl=fcl, W_=W_: h.matmul(
                                    pa[:], lhsT=uT[:, fc, tt * 128:(tt + 1) * 128], rhs=W_[:, s, fcl, :],
                                    start=(fc == 0), stop=(fc == FC - 1)), reads=[r_uT, rW_], writes=[rpa],
                                    inc=(fc == FC - 1) or (fcl == FCG - 1 and tt == 3 and s == 1))
                    if fcg == FC // FCG - 1:
                        for tt in range(4):
                            for s in range(2):
                                pa, rpa = ps8[tt * 2 + s]
                                sz, rsz = stz.next()
                                evac_copy(ev, sz[:], pa[:], [rpa], [rsz])
                                ev += 1
                                r0 = b * 512 + tt * 128
                                c0 = (2 * spi + s) * 512
                                sp.dma(FFs[r0:r0 + 128, c0:c0 + 512], sz[:], reads=[rsz], semres=rsz)
            stage_end(es, mark)

        if stop_after >= 10:
          with ExitStack() as es:
            mark = len(Kx.all_res)
            ln_stage(es, FFs, X1, 3, ln2g_bc, ln2b_bc, out, False, "8")
            stage_end(es, mark)
        Kx.barrier()
    return nc


def make_consts():
    r = np.arange(128)[:, None]
    c = np.arange(128)[None, :]
    m = np.stack([(r == c), np.ones((128, 128), bool), (r <= c), (r >= c), (r > c), (r < c)], axis=1)
    return np.ascontiguousarray(m.astype(np.float32))


def fm(v, nchunk):
    return np.ascontiguousarray(np.asarray(v).reshape(nchunk, 128).T)


def bc(v):
    v = np.asarray(v, dtype=np.float32).reshape(1, -1)
    return np.ascontiguousarray(np.broadcast_to(v, (128, v.shape[1])))


def prep_inputs(cfg, inp, n_batch):
    D, T, KC, NG, NSC, NH = cfg.D, cfg.T, cfg.KC, cfg.NG, cfg.NSC, cfg.NH
    DSSM, DXBC = cfg.DSSM, cfg.DXBC
    f32 = lambda a: np.ascontiguousarray(np.asarray(a, dtype=np.float32))
    x = np.asarray(inp["x"])
    w_in = f32(inp["w_in"][0])
    dt0 = DSSM + DXBC
    wdt_e = f32(w_in[:, dt0:dt0 + 2 * NH])
    wdt_o = f32(np.concatenate([w_in[:, dt0 + NH:dt0 + 2 * NH], w_in[:, dt0:dt0 + NH]], axis=1))
    shared = {
        "w_ada": f32(inp["w_ada"][0]), "b_ada_fm": fm(inp["b_ada"][0], 6 * KC), "w_in": w_in,
        "cb_fm": fm(inp["ssm_conv_b"][0], DXBC // 128),
        "dsk_bc": bc(np.repeat(np.asarray(inp["ssm_d"][0]), 64)), "nw_bc": bc(inp["ssm_norm_w"][0]),
        "scn_fm": fm(inp["sc_norm_w"][0], NSC), "w_out": f32(inp["w_out"][0]),
        "ln1g_bc": bc(inp["ln1_g"][0]), "ln1b_bc": bc(inp["ln1_b"][0]),
        "w_up": f32(inp["w_up"][0]), "w_down": f32(inp["w_down"][0]),
        "ln2g_bc": bc(inp["ln2_g"][0]), "ln2b_bc": bc(inp["ln2_b"][0]), "consts": make_consts(),
    }
    cwv = np.asarray(inp["ssm_conv_w"][0])
    scwv = np.asarray(inp["sc_conv_w"][0])
    par = []
    for odd in (0, 1):
        cw_ = cwv[::-1] if odd else cwv
        sc_ = scwv[::-1] if odd else scwv
        f, b_ = ("b", "f") if odd else ("f", "b")
        par.append({
            "w_dt": wdt_o if odd else wdt_e,
            "cw_fm": np.ascontiguousarray(cw_.T.reshape(DXBC // 128, 128, 5).transpose(1, 0, 2).astype(np.float32)),
            "scw_fm": np.ascontiguousarray(sc_.T.reshape(NSC, 128, 3).transpose(1, 0, 2).astype(np.float32)),
            "dtb_bc": bc(np.concatenate([inp["ssm_dt_bias_" + f][0], inp["ssm_dt_bias_" + b_][0]])),
            "alog_bc": bc(np.concatenate([inp["ssm_a_log_" + f][0], inp["ssm_a_log_" + b_][0]])),
        })
    maps = []
    for core in range(2 * n_batch):
        b, odd = core // 2, core % 2
        xl = x[b, ::-1] if odd else x[b]
        m = dict(shared)
        m.update(par[odd])
        m["x"] = f32(xl)
        m["c_fm"] = fm(inp["c"][b], KC)
        maps.append(m)
    return maps


def assemble(cfg, results, n_batch):
    T, D = cfg.T, cfg.D
    o = np.empty((n_batch, 2 * T, D), np.float32)
    for core in range(2 * n_batch):
        b, odd = core // 2, core % 2
        r = np.asarray(results[core]["out"])
        if odd:
            o[b, T:] = r[::-1]
        else:
            o[b, :T] = r
    return o


_NC_CACHE = {}


def kernel(**inputs):
    cfg = FULL
    if "nc" not in _NC_CACHE:
        _NC_CACHE["nc"] = build(cfg)
    nc = _NC_CACHE["nc"]
    maps = prep_inputs(cfg, inputs, 4)
    res = run_bass_kernel_spmd(nc, maps, core_ids=list(range(8)))
    return assemble(cfg, res.results, 4)
```

```python
import numpy as np
from contextlib import ExitStack
import concourse.bass as bass
import concourse.mybir as mybir
from concourse.bass_utils import run_bass_kernel_spmd

F32 = mybir.dt.float32
BF16 = mybir.dt.bfloat16
AF = mybir.ActivationFunctionType
ALU = mybir.AluOpType

LN_EPS = 1e-5
RMS_EPS = 1e-5


class Cfg:
    def __init__(s, D, T, NG, NSC, DFF, alpha):
        s.D, s.T, s.NG, s.NSC, s.DFF, s.alpha = D, T, NG, NSC, DFF, alpha
        s.KC = D // 128
        s.DSSM = NG * 256
        s.NH = NG * 4
        s.DSC = NSC * 128
        s.DXBC = s.DSSM + 2 * NG * 128
        s.DIN = s.DSSM + s.DXBC + 2 * s.NH + 3 * s.DSC
        s.FC = DFF // 128
        s.NT = T // 128
        s.TB = min(1024, T)
        s.NFM = (s.DXBC + 3 * s.DSC) // 128
        s.NXO = (s.DSSM + NG * 128) // 128
        assert s.DSSM + s.DSC == D


FULL = Cfg(4096, 2048, 8, 16, 16384, 2.0 ** 0.25)


class Res:
    __slots__ = ("name", "w", "readers", "sem", "semcnt", "sw", "psum")

    def __init__(self, name):
        self.name = name
        self.w = None
        self.readers = {}
        self.sem = None
        self.semcnt = 0
        self.sw = False
        self.psum = False


class Eng:
    def __init__(self, Kx, h, name, is_pe=False):
        self.K, self.h, self.name, self.is_pe = Kx, h, name, is_pe
        self.sem = Kx.new_sem("e_" + name)
        self.cnt = 0
        self.seen = {}

    def _wait(self, tok, war):
        if tok is None:
            return
        sem, val, en = tok
        if en == self.name and (self.is_pe or war):
            return
        key = id(sem)
        if self.seen.get(key, 0) >= val:
            return
        self.h.wait_ge(sem, val)
        self.seen[key] = val

    def deps(self, reads, writes):
        for r in reads:
            self._wait(r.w, False)
            if r.psum:
                for t in r.readers.values():
                    self._wait(t, True)
        for w in writes:
            self._wait(w.w, False)
            for t in w.readers.values():
                self._wait(t, True)

    def op(self, fn, reads=(), writes=(), inc=True):
        self.deps(reads, writes)
        ins = fn(self.h)
        if inc:
            ins.then_inc(self.sem, 1)
            self.cnt += 1
            tok = (self.sem, self.cnt, self.name)
        else:
            tok = (self.sem, self.cnt + 1, self.name)
        for r in reads:
            r.readers[id(tok[0])] = tok
        for w in writes:
            w.w = tok
            w.readers = {}
        return ins

    def dma(self, out, in_, reads=(), writes=(), semres=None):
        self.dma_multi([(out, in_)], reads, writes, semres)

    def dma_multi(self, pairs, reads=(), writes=(), semres=None):
        self.deps(reads, writes)
        if semres.sem is None:
            semres.sw = (self.name == "pool")
            semres.sem, semres.semcnt = self.K.take_dma_sem("d_" + semres.name, semres.sw)
        assert semres.sw == (self.name == "pool"), semres.name
        for out, in_ in pairs:
            self.h.dma_start(out=out, in_=in_).then_inc(semres.sem, 16)
            semres.semcnt += 16
        tok = (semres.sem, semres.semcnt, "dma")
        for r in reads:
            r.readers[id(tok[0])] = tok
        for w in writes:
            w.w = tok
            w.readers = {}


class Kctx:
    def __init__(self, nc, es):
        self.nc, self.es = nc, es
        self.nsem = 0
        self.all_res = []
        self.sem_pool = []
        self.sem_pool_sw = []

    def new_sem(self, name):
        self.nsem += 1
        return self.es.enter_context(self.nc.semaphore("%s_%d" % (name[:20], self.nsem)))

    def take_dma_sem(self, name, sw):
        pool = self.sem_pool_sw if sw else self.sem_pool
        if pool:
            return pool.pop()
        return self.new_sem(name), 0

    def recycle(self, mark):
        for r in self.all_res[mark:]:
            if r.sem is not None:
                (self.sem_pool_sw if r.sw else self.sem_pool).append((r.sem, r.semcnt))
        del self.all_res[mark:]

    def res(self, name):
        r = Res(name)
        self.all_res.append(r)
        return r

    def engines(self):
        return [self.pe, self.act, self.dve, self.pool, self.sp]

    def barrier(self):
        toks = [(e.sem, e.cnt, "x") for e in self.engines() if e.cnt > 0]
        for r in self.all_res:
            if r.sem is not None and r.semcnt > 0:
                toks.append((r.sem, r.semcnt, "x"))
        for e in self.engines():
            for t in toks:
                if t[0] is e.sem and e.is_pe:
                    continue
                e._wait(t, False)
        for r in self.all_res:
            r.w = None
            r.readers = {}


class Ring:
    def __init__(self, Kx, es, name, shape, dtype, n, psum=False):
        self.tiles = []
        for i in range(n):
            nm = "%s%d" % (name, i)
            t = es.enter_context((Kx.nc.psum_tensor if psum else Kx.nc.sbuf_tensor)(nm, shape, dtype))
            rr = Kx.res(nm)
            rr.psum = psum
            self.tiles.append((t, rr))
        self.i = 0

    def next(self):
        t = self.tiles[self.i % len(self.tiles)]
        self.i += 1
        return t


def one(Kx, es, name, shape, dtype, psum=False):
    t = es.enter_context((Kx.nc.psum_tensor if psum else Kx.nc.sbuf_tensor)(name, shape, dtype))
    rr = Kx.res(name)
    rr.psum = psum
    return t, rr


def build(cfg, debug=False, stop_after=99):
    nc = bass.Bass("TRN2", target_bir_lowering=False)
    D, T, KC, NG, NSC, NH, FC, NT, TB = cfg.D, cfg.T, cfg.KC, cfg.NG, cfg.NSC, cfg.NH, cfg.FC, cfg.NT, cfg.TB
    DSSM, DSC, DXBC, DIN, DFF = cfg.DSSM, cfg.DSC, cfg.DXBC, cfg.DIN, cfg.DFF
    NFM, NXO = cfg.NFM, cfg.NXO
    LT = 2 * T
    TH = T + 128
    H2 = 2 * NH
    NTT = 2 * NT
    TBt = TB // 128
    NXT = DXBC // 128
    SSMT = DSSM // 128
    alpha = float(cfg.alpha)

    def din(name, shape, dt=F32):
        return nc.dram_tensor(name, shape, dt, kind="ExternalInput").ap()

    def dscr(name, shape, dt=F32):
        return nc.dram_tensor(name, shape, dt, kind=("ExternalOutput" if debug else "Internal")).ap()

    x_in = din("x", [LT, D])
    c_fm = din("c_fm", [128, KC])
    w_ada = din("w_ada", [D, 6 * D])
    b_ada_fm = din("b_ada_fm", [128, 6 * KC])
    w_in = din("w_in", [D, DIN])
    w_dt = din("w_dt", [D, H2])
    cw_fm = din("cw_fm", [128, NXT, 5])
    cb_fm = din("cb_fm", [128, NXT])
    dtb_bc = din("dtb_bc", [128, H2])
    alog_bc = din("alog_bc", [128, H2])
    dsk_bc = din("dsk_bc", [128, DSSM])
    nw_bc = din("nw_bc", [128, DSSM])
    scw_fm = din("scw_fm", [128, NSC, 3])
    scn_fm = din("scn_fm", [128, NSC])
    w_out = din("w_out", [D, D])
    ln1g_bc = din("ln1g_bc", [128, D])
    ln1b_bc = din("ln1b_bc", [128, D])
    w_up = din("w_up", [D, DFF])
    w_down = din("w_down", [DFF, D])
    ln2g_bc = din("ln2g_bc", [128, D])
    ln2b_bc = din("ln2b_bc", [128, D])
    consts = din("consts", [128, 6, 128])
    out = nc.dram_tensor("out", [T, D], F32, kind="ExternalOutput").ap()

    PTm = dscr("PTm", [NFM, 128, TH])
    PTo = dscr("PTo", [NXO, 128, T])
    Ztm = dscr("Ztm", [T, DSSM])
    GZ = dscr("GZ", [T, DSSM])
    XBC = dscr("XBC", [NXT, 128, T], BF16)
    XBO = dscr("XBO", [NXO, 128, T], BF16)
    YT = dscr("YT", [KC, 128, T], BF16)
    MIX = dscr("MIX", [T, D])
    X1 = dscr("X1", [T, D])
    H2T = dscr("H2T", [KC, 128, T], BF16)
    UT = dscr("UT", [FC, 128, T], BF16)
    FFs = dscr("FF", [T, D])
    BCS = dscr("BCS", [4, 128, D])
    DBG = dscr("DBG", [128, 8 * NTT * H2]) if debug else None

    with ExitStack() as ges:
        Kx = Kctx(nc, ges)
        Kx.pe = Eng(Kx, nc.tensor, "pe", is_pe=True)
        Kx.act = Eng(Kx, nc.scalar, "act")
        Kx.dve = Eng(Kx, nc.vector, "dve")
        Kx.pool = Eng(Kx, nc.gpsimd, "pool")
        Kx.sp = Eng(Kx, nc.sync, "sp")
        pe, act, dve, pool, sp = Kx.pe, Kx.act, Kx.dve, Kx.pool, Kx.sp

        def stage_end(es, mark):
            Kx.barrier()
            Kx.recycle(mark)
            es.close()

        cst, r_cst = one(Kx, ges, "cst", [128, 6, 128], F32)
        identb, r_identb = one(Kx, ges, "identb", [128, 128], BF16)
        onesdiv, r_onesdiv = one(Kx, ges, "onesdiv", [128, 128], F32)
        modfm, r_mod = one(Kx, ges, "modfm", [128, 6 * KC], F32)
        sc1p, r_sc1p = one(Kx, ges, "sc1p", [128, KC], F32)
        dtraw, r_dtraw = one(Kx, ges, "dtraw", [128, NTT, H2], F32)
        scb, r_scb = one(Kx, ges, "scb", [128, KC], BF16)
        IDENT, ONES, LE, GE, GT, LTm = (cst[:, i, :] for i in range(6))

        sp.dma(cst[:], consts[:, :, :], writes=[r_cst], semres=r_cst)
        dve.op(lambda h: h.tensor_copy(out=identb[:], in_=cst[:, 0, :]), reads=[r_cst], writes=[r_identb])
        dve.op(lambda h: h.tensor_scalar(out=onesdiv[:], in0=cst[:, 1, :], scalar1=1.0 / 128.0, scalar2=None,
                                         op0=ALU.mult), reads=[r_cst], writes=[r_onesdiv])

        def evac_copy(i, out_ap, in_ap, reads, writes):
            if i % 2 == 0:
                act.op(lambda h: h.activation(out=out_ap, in_=in_ap, func=AF.Copy), reads=reads, writes=writes)
            else:
                dve.op(lambda h: h.tensor_copy(out=out_ap, in_=in_ap), reads=reads, writes=writes)

        def load_w_slab(Wring, wsrc, col0, width):
            Wt, rW = Wring.next()
            pool.dma(Wt[:, :, 0:width], wsrc.rearrange("(kc p) n -> p kc n", p=128)[:, :, col0:col0 + width],
                     writes=[rW], semres=rW)
            return Wt, rW

        with ExitStack() as es:
            mark = len(Kx.all_res)
            Wring = Ring(Kx, es, "W0_", [128, KC, 512], BF16, 2)
            cf, r_cf = one(Kx, es, "cf", [128, KC], F32)
            bada, r_bada = one(Kx, es, "bada", [128, 6 * KC], F32)
            psm, r_psm = one(Kx, es, "psm", [128, 512], F32, psum=True)
            sp.dma(cf[:], c_fm[:, :], writes=[r_cf], semres=r_cf)
            sp.dma(bada[:], b_ada_fm[:, :], writes=[r_bada], semres=r_bada)
            act.op(lambda h: h.activation(out=scb[:], in_=cf[:], func=AF.Silu), reads=[r_cf], writes=[r_scb])
            nsl = 2 * D // 512
            nxt = load_w_slab(Wring, w_ada, 0, 512)
            for s in range(nsl):
                Wt, rW = nxt
                if s + 1 < nsl:
                    nxt = load_w_slab(Wring, w_ada, (s + 1) * 512, 512)
                for ct in range(4):
                    j = 4 * s + ct
                    for kc in range(KC):
                        pe.op(lambda h, Wt=Wt, ct=ct, kc=kc, j=j: h.matmul(
                            psm[:, j:j + 1], lhsT=Wt[:, kc, ct * 128:(ct + 1) * 128], rhs=scb[:, kc:kc + 1],
                            start=(kc == 0), stop=(kc == KC - 1)),
                            reads=[rW, r_scb], writes=[r_psm], inc=(kc == KC - 1))
            dve.op(lambda h: h.tensor_tensor(out=modfm[:, 0:2 * KC], in0=psm[:, 0:2 * KC], in1=bada[:, 0:2 * KC], op=ALU.add),
                   reads=[r_psm, r_bada], writes=[r_mod])
            dve.op(lambda h: h.tensor_scalar(out=sc1p[:], in0=modfm[:, KC:2 * KC], scalar1=1.0, scalar2=None,
                                             op0=ALU.add), reads=[r_mod], writes=[r_sc1p])
            stage_end(es, mark)

        if stop_after >= 2:
          with ExitStack() as es:
            mark = len(Kx.all_res)
            Wring = Ring(Kx, es, "W2_", [128, KC, 512], BF16, 2)
            hT, r_hT = one(Kx, es, "hT", [128, KC, TB + 128], BF16)
            xring = Ring(Kx, es, "xt", [128, D], F32, 2)
            wdt, r_wdt = one(Kx, es, "wdt", [128, KC, H2], BF16)
            stg = Ring(Kx, es, "stg", [128, TB + 128], F32, 2)
            stz = Ring(Kx, es, "stz", [128, 512], F32, 3)
            pstr = Ring(Kx, es, "pstr", [128, 512], F32, 2, psum=True)
            psacc = Ring(Kx, es, "psacc", [128, 512], F32, 4, psum=True)
            psdt = Ring(Kx, es, "psdt", [128, 512], F32, 2, psum=True)
            pool.dma(wdt[:], w_dt.rearrange("(kc p) n -> p kc n", p=128), writes=[r_wdt], semres=r_wdt)

            def slab_list(c0, c1):
                r = []
                c = c0
                while c < c1:
                    w = min(512, c1 - c)
                    r.append((c, w))
                    c += w
                return r

            sc0 = DSSM + DXBC + H2
            blocks = []
            for b in range(T // TB):
                tiles = list(range(b * TBt, (b + 1) * TBt))
                last = (b == T // TB - 1)
                if last:
                    tiles.append(NT)
                blocks.append(("main", tiles, last))
            for b in range(T // TB):
                blocks.append(("other", [NT + i for i in range(b * TBt, (b + 1) * TBt)], False))

            ev = 0
            for kind, tiles, last in blocks:
                ntok = len(tiles) * 128
                for ti, tile in enumerate(tiles):
                    xt, rx = xring.next()
                    sp.dma(xt[:], x_in[tile * 128:(tile + 1) * 128, :], writes=[rx], semres=rx)
                    for q4 in range(KC // 4):
                        pt, rpt = pstr.next()
                        for q in range(4):
                            kc = q4 * 4 + q
                            pe.op(lambda h, pt=pt, xt=xt, q=q, kc=kc: h.transpose(
                                out=pt[:, q * 128:(q + 1) * 128], in_=xt[:, kc * 128:(kc + 1) * 128],
                                identity=cst[:, 0, :]), reads=[rx, r_cst], writes=[rpt], inc=(q == 3))
                        for q in range(4):
                            kc = q4 * 4 + q
                            dst = hT[:, kc, ti * 128:(ti + 1) * 128]
                            if (ev % 2) == 0:
                                act.op(lambda h, dst=dst, pt=pt, q=q, kc=kc: h.activation(
                                    out=dst, in_=pt[:, q * 128:(q + 1) * 128], func=AF.Identity,
                                    bias=modfm[:, kc:kc + 1], scale=sc1p[:, kc:kc + 1]),
                                    reads=[rpt, r_mod, r_sc1p], writes=[r_hT])
                            else:
                                dve.op(lambda h, dst=dst, pt=pt, q=q, kc=kc: h.tensor_scalar(
                                    out=dst, in0=pt[:, q * 128:(q + 1) * 128], scalar1=sc1p[:, kc:kc + 1],
                                    scalar2=modfm[:, kc:kc + 1], op0=ALU.mult, op1=ALU.add),
                                    reads=[rpt, r_mod, r_sc1p], writes=[r_hT])
                            ev += 1
                for ti, tile in enumerate(tiles):
                    if tile >= NTT or (kind == "main" and tile == NT):
                        continue
                    pd, rpd = psdt.next()
                    for kc in range(KC):
                        pe.op(lambda h, pd=pd, ti=ti, kc=kc: h.matmul(
                            pd[:, 0:H2], lhsT=hT[:, kc, ti * 128:(ti + 1) * 128], rhs=wdt[:, kc, :],
                            start=(kc == 0), stop=(kc == KC - 1)), reads=[r_hT, r_wdt], writes=[rpd],
                            inc=(kc == KC - 1))
                    evac_copy(ev, dtraw[:, tile, :], pd[:, 0:H2], [rpd], [r_dtraw])
                    ev += 1
                if kind == "main":
                    slabs = [("tm", c, w) for c, w in slab_list(0, DSSM)]
                    slabs += [("fm", c, w) for c, w in slab_list(DSSM, DSSM + DXBC)]
                    slabs += [("fm", c, w) for c, w in slab_list(sc0, DIN)]
                else:
                    slabs = [("fm", c, w) for c, w in slab_list(DSSM, DSSM + DSSM + NG * 128)]
                nxt = load_w_slab(Wring, w_in, slabs[0][1], slabs[0][2])
                for si, (mode, c0, wd) in enumerate(slabs):
                    Wt, rW = nxt
                    if si + 1 < len(slabs):
                        nxt = load_w_slab(Wring, w_in, slabs[si + 1][1], slabs[si + 1][2])
                    if mode == "tm":
                        for ti, tile in enumerate(tiles):
                            if tile == NT:
                                continue
                            pa, rpa = psacc.next()
                            for kc in range(KC):
                                pe.op(lambda h, pa=pa, ti=ti, kc=kc, Wt=Wt, wd=wd: h.matmul(
                                    pa[:, 0:wd], lhsT=hT[:, kc, ti * 128:(ti + 1) * 128], rhs=Wt[:, kc, 0:wd],
                                    start=(kc == 0), stop=(kc == KC - 1)), reads=[r_hT, rW], writes=[rpa],
                                    inc=(kc == KC - 1))
                            sz, rsz = stz.next()
                            evac_copy(ev, sz[:, 0:wd], pa[:, 0:wd], [rpa], [rsz])
                            ev += 1
                            sp.dma(Ztm[tile * 128:(tile + 1) * 128, c0:c0 + wd], sz[:, 0:wd], reads=[rsz], semres=rsz)
                    else:
                        for ct in range(wd // 128):
                            col = c0 + ct * 128
                            fmt = (col - DSSM) // 128 if col < sc0 else NXT + (col - sc0) // 128
                            sg, rsg = stg.next()
                            n0 = 0
                            while n0 < ntok:
                                nn = min(512, ntok - n0)
                                pa, rpa = psacc.next()
                                for kc in range(KC):
                                    pe.op(lambda h, pa=pa, kc=kc, Wt=Wt, ct=ct, n0=n0, nn=nn: h.matmul(
                                        pa[:, 0:nn], lhsT=Wt[:, kc, ct * 128:(ct + 1) * 128], rhs=hT[:, kc, n0:n0 + nn],
                                        start=(kc == 0), stop=(kc == KC - 1)), reads=[r_hT, rW], writes=[rpa],
                                        inc=(kc == KC - 1))
                                evac_copy(ev, sg[:, n0:n0 + nn], pa[:, 0:nn], [rpa], [rsg])
                                ev += 1
                                n0 += nn
                            t0 = tiles[0] * 128
                            if kind == "main":
                                sp.dma(PTm[fmt, :, t0:t0 + ntok], sg[:, 0:ntok], reads=[rsg], semres=rsg)
                            else:
                                sp.dma(PTo[fmt, :, t0 - T:t0 - T + ntok], sg[:, 0:ntok], reads=[rsg], semres=rsg)
            stage_end(es, mark)

        if stop_after >= 3:
          with ExitStack() as es:
            mark = len(Kx.all_res)
            WringB = Ring(Kx, es, "W0b_", [128, KC, 512], BF16, 2)
            badaB, r_badaB = one(Kx, es, "badaB", [128, 6 * KC], F32)
            psmB, r_psmB = one(Kx, es, "psmB", [128, 512], F32, psum=True)
            sp.dma(badaB[:], b_ada_fm[:, :], writes=[r_badaB], semres=r_badaB)
            s_lo, s_hi = 2 * D // 512, 6 * D // 512
            nxt = load_w_slab(WringB, w_ada, s_lo * 512, 512)
            for s in range(s_lo, s_hi):
                Wt, rW = nxt
                if s + 1 < s_hi:
                    nxt = load_w_slab(WringB, w_ada, (s + 1) * 512, 512)
                for ct in range(4):
                    j = 4 * s + ct
                    for kc in range(KC):
                        pe.op(lambda h, Wt=Wt, ct=ct, kc=kc, j=j: h.matmul(
                            psmB[:, j:j + 1], lhsT=Wt[:, kc, ct * 128:(ct + 1) * 128], rhs=scb[:, kc:kc + 1],
                            start=(kc == 0), stop=(kc == KC - 1)),
                            reads=[rW, r_scb], writes=[r_psmB], inc=(kc == KC - 1))
            cw, r_cw = one(Kx, es, "cw", [128, NXT, 5], F32)
            cb, r_cb = one(Kx, es, "cb", [128, NXT], F32)
            sp.dma(cw[:], cw_fm[:, :, :], writes=[r_cw], semres=r_cw)
            sp.dma(cb[:], cb_fm[:, :], writes=[r_cb], semres=r_cb)
            Um = Ring(Kx, es, "Um", [128, T + 4], F32, 3)
            Uo = Ring(Kx, es, "Uo", [128, T + 4], F32, 3)
            accr = Ring(Kx, es, "cacc", [128, T], F32, 2)
            obr = Ring(Kx, es, "cob", [128, T], BF16, 3)
            zr = Ring(Kx, es, "zr", [128, DSSM], F32, 3)
            zo = Ring(Kx, es, "zo", [128, DSSM], F32, 2)
            for (u, ru) in Um.tiles:
                dve.op(lambda h, u=u: h.memset(u[:, 0:2], 0.0), writes=[ru])
            for (u, ru) in Uo.tiles:
                dve.op(lambda h, u=u: h.memset(u[:, T + 2:T + 4], 0.0), writes=[ru])

            def conv_tile(U, rU, i, dst):
                acc, racc = accr.next()
                dve.op(lambda h: h.tensor_scalar(out=acc[:], in0=U[:, 0:T], scalar1=cw[:, i, 0:1],
                                                 scalar2=cb[:, i:i + 1], op0=ALU.mult, op1=ALU.add),
                       reads=[rU, r_cw, r_cb], writes=[racc])
                for k in range(1, 5):
                    dve.op(lambda h, k=k: h.scalar_tensor_tensor(
                        out=acc[:], in0=U[:, k:k + T], scalar=cw[:, i, k:k + 1], in1=acc[:],
                        op0=ALU.mult, op1=ALU.add), reads=[rU, r_cw, racc], writes=[racc])
                ob, rob = obr.next()
                act.op(lambda h: h.activation(out=ob[:], in_=acc[:], func=AF.Silu), reads=[racc], writes=[rob])
                sp.dma(dst, ob[:], reads=[rob], semres=rob)

            def pipelined(n, load_fn, compute_fn, depth=2):
                q = [load_fn(i) for i in range(min(depth, n))]
                for i in range(n):
                    if i + depth < n:
                        q.append(load_fn(i + depth))
                    compute_fn(i, q[i])

            def ld_m(i):
                U, rU = Um.next()
                sp.dma(U[:, 2:T + 4], PTm[i, :, 0:T + 2], writes=[rU], semres=rU)
                return U, rU

            def ld_o(i):
                U, rU = Uo.next()
                sp.dma_multi([(U[:, 2:T + 2], PTo[i, :, 0:T]), (U[:, 0:2], PTm[i, :, T - 2:T])], writes=[rU], semres=rU)
                return U, rU

            def ld_z(tt):
                z, rz = zr.next()
                sp.dma(z[:], Ztm[tt * 128:(tt + 1) * 128, :], writes=[rz], semres=rz)
                return z, rz

            def do_z(tt, zz):
                z, rz = zz
                g, rg = zo.next()
                act.op(lambda h: h.activation(out=g[:], in_=z[:], func=AF.Silu), reads=[rz], writes=[rg])
                sp.dma(GZ[tt * 128:(tt + 1) * 128, :], g[:], reads=[rg], semres=rg)

            pipelined(NXT, ld_m, lambda i, u: conv_tile(u[0], u[1], i, XBC[i]))
            pipelined(NXO, ld_o, lambda i, u: conv_tile(u[0], u[1], i, XBO[i]))
            pipelined(NT, ld_z, do_z)
            dve.op(lambda h: h.tensor_tensor(out=modfm[:, 2 * KC:6 * KC], in0=psmB[:, 2 * KC:6 * KC], in1=badaB[:, 2 * KC:6 * KC], op=ALU.add),
                   reads=[r_psmB, r_badaB], writes=[r_mod])
            stage_end(es, mark)

          with ExitStack() as es:
            mark = len(Kx.all_res)
            p1, r_p1 = one(Kx, es, "p1", [128, 3, KC], F32)
            for q, src in enumerate((2, 4, 5)):
                dve.op(lambda h, q=q, src=src: h.tensor_scalar(
                    out=p1[:, q, :], in0=modfm[:, src * KC:(src + 1) * KC], scalar1=1.0, scalar2=None, op0=ALU.add),
                    reads=[r_mod], writes=[r_p1])
            dg, r_dg = one(Kx, es, "dg", [128, KC, 128], F32)
            bct, r_bct = one(Kx, es, "bct", [128, D], F32)
            lg, r_lg = one(Kx, es, "lg", [128, D], F32)
            lb, r_lb = one(Kx, es, "lb", [128, D], F32)
            sp.dma(lg[:], ln1g_bc[:, :], writes=[r_lg], semres=r_lg)
            sp.dma(lb[:], ln1b_bc[:, :], writes=[r_lb], semres=r_lb)
            psb = Ring(Kx, es, "psb", [128, 512], F32, 2, psum=True)

            def bcast_row(vec_ap):
                dve.op(lambda h: h.tensor_tensor(
                    out=dg[:], in0=cst[:, 0, :].unsqueeze(1).broadcast_to([128, KC, 128]),
                    in1=vec_ap.unsqueeze(2).broadcast_to([128, KC, 128]), op=ALU.mult),
                    reads=[r_cst, r_p1, r_mod], writes=[r_dg])
                for q in range(D // 512):
                    pb, rpb = psb.next()
                    pe.op(lambda h, pb=pb, q=q: h.matmul(pb[:], lhsT=cst[:, 1, :], rhs=dg[:, 4 * q:4 * q + 4, :],
                                                         start=True, stop=True), reads=[r_dg, r_cst], writes=[rpb])
                    evac_copy(q, bct[:, q * 512:(q + 1) * 512], pb[:], [rpb], [r_bct])

            bcast_row(p1[:, 0, :])
            sp.dma(BCS[0], bct[:], reads=[r_bct], semres=r_bct)
            bcast_row(p1[:, 1, :])
            dve.op(lambda h: h.tensor_tensor(out=lg[:], in0=lg[:], in1=bct[:], op=ALU.mult),
                   reads=[r_lg, r_bct], writes=[r_lg])
            dve.op(lambda h: h.tensor_tensor(out=lb[:], in0=lb[:], in1=bct[:], op=ALU.mult),
                   reads=[r_lb, r_bct], writes=[r_lb])
            sp.dma(BCS[1], lg[:], reads=[r_lg], semres=r_lg)
            bcast_row(modfm[:, 3 * KC:4 * KC])
            dve.op(lambda h: h.tensor_tensor(out=lb[:], in0=lb[:], in1=bct[:], op=ALU.add),
                   reads=[r_lb, r_bct], writes=[r_lb])
            sp.dma(BCS[2], lb[:], reads=[r_lb], semres=r_lb)
            bcast_row(p1[:, 2, :])
            sp.dma(BCS[3], bct[:], reads=[r_bct], semres=r_bct)
            stage_end(es, mark)

        if stop_after >= 4:
          with ExitStack() as es:
            mark = len(Kx.all_res)
            scw, r_scw = one(Kx, es, "scw", [128, NSC, 3], F32)
            scn, r_scn = one(Kx, es, "scn", [128, NSC], F32)
            sp.dma(scw[:], scw_fm[:, :, :], writes=[r_scw], semres=r_scw)
            sp.dma(scn[:], scn_fm[:, :], writes=[r_scn], semres=r_scn)
            Hr = Ring(Kx, es, "Hr", [128, T + 1], F32, 2)
            Cr = Ring(Kx, es, "Cr", [128, T + 1], F32, 2)
            Br = Ring(Kx, es, "Br", [128, T], F32, 2)
            Mr = Ring(Kx, es, "Mr", [128, T + 2], F32, 2)
            Ar = Ring(Kx, es, "Ar", [128, T], F32, 2)
            Sr = Ring(Kx, es, "Sr", [128, T], F32, 2)
            Rr = Ring(Kx, es, "Rr", [128, T], F32, 2)
            Or = Ring(Kx, es, "Or", [128, T], BF16, 2)
            pss = Ring(Kx, es, "pss", [128, 512], F32, 4, psum=True)
            for (m, rm) in Mr.tiles:
                dve.op(lambda h, m=m: h.memset(m[:, 0:1], 0.0), writes=[rm])
            def ld_sc(j):
                Hb, rH = Hr.next()
                Cb, rC = Cr.next()
                Bb, rB = Br.next()
                sp.dma(Hb[:], PTm[NXT + j, :, 0:T + 1], writes=[rH], semres=rH)
                sp.dma(Cb[:], PTm[NXT + 2 * NSC + j, :, 0:T + 1], writes=[rC], semres=rC)
                sp.dma(Bb[:], PTm[NXT + NSC + j, :, 0:T], writes=[rB], semres=rB)
                return (Hb, rH, Cb, rC, Bb, rB)

            scq = [ld_sc(0)]
            for j in range(NSC):
                if j + 1 < NSC:
                    scq.append(ld_sc(j + 1))
                Hb, rH, Cb, rC, Bb, rB = scq[j]
                M, rM = Mr.next()
                dve.op(lambda h, M=M, Cb=Cb, Hb=Hb: h.tensor_tensor(out=M[:, 1:T + 2], in0=Cb[:], in1=Hb[:], op=ALU.mult),
                       reads=[rC, rH], writes=[rM])
                A, rA = Ar.next()
                dve.op(lambda h, A=A, M=M, j=j: h.tensor_scalar(out=A[:], in0=M[:, 0:T], scalar1=scw[:, j, 0:1],
                                                                scalar2=None, op0=ALU.mult),
                       reads=[rM, r_scw], writes=[rA])
                for k in (1, 2):
                    dve.op(lambda h, A=A, M=M, j=j, k=k: h.scalar_tensor_tensor(
                        out=A[:], in0=M[:, k:k + T], scalar=scw[:, j, k:k + 1], in1=A[:], op0=ALU.mult, op1=ALU.add),
                        reads=[rM, r_scw, rA], writes=[rA])
                dve.op(lambda h, A=A, Bb=Bb: h.tensor_tensor(out=A[:], in0=A[:], in1=Bb[:], op=ALU.mult),
                       reads=[rA, rB], writes=[rA])
                S, rS = Sr.next()
                act.op(lambda h, S=S, A=A: h.activation(out=S[:], in_=A[:], func=AF.Square), reads=[rA], writes=[rS])
                R, rR = Rr.next()
                for q in range(T // 512):
                    ps, rps = pss.next()
                    pe.op(lambda h, ps=ps, S=S, q=q: h.matmul(ps[:], lhsT=onesdiv[:], rhs=S[:, q * 512:(q + 1) * 512],
                                                              start=True, stop=True), reads=[rS, r_onesdiv], writes=[rps])
                    act.op(lambda h, ps=ps, R=R, q=q: h.activation(out=R[:, q * 512:(q + 1) * 512], in_=ps[:], func=AF.Ln,
                                                                   bias=RMS_EPS), reads=[rps], writes=[rR])
                act.op(lambda h, R=R: h.activation(out=R[:], in_=R[:], func=AF.Exp, scale=-0.5), reads=[rR], writes=[rR])
                O, rO = Or.next()
                dve.op(lambda h, O=O, A=A, R=R, j=j: h.scalar_tensor_tensor(
                    out=O[:], in0=A[:], scalar=scn[:, j:j + 1], in1=R[:], op0=ALU.mult, op1=ALU.mult),
                    reads=[rA, rR, r_scn], writes=[rO])
                sp.dma(YT[SSMT + j], O[:], reads=[rO], semres=rO)
            stage_end(es, mark)

        if stop_after >= 5:
          with ExitStack() as es:
            mark = len(Kx.all_res)
            def sb(name, shape, dt=F32):
                return one(Kx, es, name, shape, dt)
            dtb, r_dtb = sb("dtb", [128, H2])
            alg, r_alg = sb("alg", [128, H2])
            dsk, r_dsk = sb("dsk", [128, DSSM])
            nwb, r_nwb = sb("nwb", [128, DSSM])
            for t_, r_, s_ in ((dtb, r_dtb, dtb_bc), (alg, r_alg, alog_bc), (dsk, r_dsk, dsk_bc), (nwb, r_nwb, nw_bc)):
                sp.dma(t_[:], s_[:, :], writes=[r_], semres=r_)
            dtv, r_dtv = sb("dtv", [128, NTT, H2])
            adt, r_adt = sb("adt", [128, NTT, H2])
            wv, r_wv = sb("wv", [128, NTT, H2])
            cdv, r_cd = sb("cdv", [128, NTT, H2])
            eQ, r_eQ = sb("eQ", [128, NTT, H2])
            psq = Ring(Kx, es, "psq", [128, 512], F32, 2, psum=True)
            pstA, r_pstA = one(Kx, es, "pstA", [128, 1024], BF16, psum=True)
            psseg = Ring(Kx, es, "psseg", [128, 512], F32, 2, psum=True)
            psYr = Ring(Kx, es, "psY", [128, 512], F32, 2, psum=True)
            psO, r_psO = one(Kx, es, "psO", [128, 512], F32, psum=True)

            tes = ExitStack()
            tmark = len(Kx.all_res)
            xb, r_xb = one(Kx, tes, "xb", [128, NTT, H2], F32)
            tA, r_tA = one(Kx, tes, "tA", [128, NTT, H2], F32)
            tB, r_tB = one(Kx, tes, "tB", [128, NTT, H2], F32)
            Qv, r_Q = one(Kx, tes, "Qv", [128, NTT, H2], F32)
            Qt, r_Qt = one(Kx, tes, "Qt", [128, NTT, H2], F32)
            bc3 = lambda ap: ap.unsqueeze(1).broadcast_to([128, NTT, H2])
            dve.op(lambda h: h.tensor_tensor(out=xb[:], in0=dtraw[:], in1=bc3(dtb[:]), op=ALU.add),
                   reads=[r_dtraw, r_dtb], writes=[r_xb])
            dve.op(lambda h: h.scalar_tensor_tensor(out=tA[:], in0=xb[:], scalar=-1.0, in1=xb[:], op0=ALU.mult, op1=ALU.min),
                   reads=[r_xb], writes=[r_tA])
            act.op(lambda h: h.activation(out=tA[:], in_=tA[:], func=AF.Exp), reads=[r_tA], writes=[r_tA])
            act.op(lambda h: h.activation(out=tA[:], in_=tA[:], func=AF.Ln, bias=1.0), reads=[r_tA], writes=[r_tA])
            dve.op(lambda h: h.scalar_tensor_tensor(out=dtv[:], in0=xb[:], scalar=0.0, in1=tA[:], op0=ALU.max, op1=ALU.add),
                   reads=[r_xb, r_tA], writes=[r_dtv])
            act.op(lambda h: h.activation(out=alg[:], in_=alg[:], func=AF.Exp), reads=[r_alg], writes=[r_alg])
            dve.op(lambda h: h.scalar_tensor_tensor(out=adt[:], in0=dtv[:], scalar=-1.0, in1=bc3(alg[:]),
                                                    op0=ALU.mult, op1=ALU.mult), reads=[r_dtv, r_alg], writes=[r_adt])
            for c in range(NTT):
                pq, rpq = psq.next()
                pe.op(lambda h, pq=pq, c=c: h.matmul(pq[:, 0:NH], lhsT=cst[:, 2, :], rhs=adt[:, c, 0:NH], start=True, stop=True),
                      reads=[r_adt, r_cst], writes=[rpq])
                pe.op(lambda h, pq=pq, c=c: h.matmul(pq[:, NH:H2], lhsT=cst[:, 3, :], rhs=adt[:, c, NH:H2], start=True, stop=True),
                      reads=[r_adt, r_cst], writes=[rpq])
                pe.op(lambda h, pq=pq, c=c: h.matmul(pq[:, H2:2 * H2], lhsT=cst[:, 1, :], rhs=adt[:, c, :], start=True, stop=True),
                      reads=[r_adt, r_cst], writes=[rpq])
                dve.op(lambda h, pq=pq, c=c: h.tensor_copy(out=Qv[:, c, :], in_=pq[:, 0:H2]), reads=[rpq], writes=[r_Q])
                act.op(lambda h, pq=pq, c=c: h.activation(out=Qt[:, c, :], in_=pq[:, H2:2 * H2], func=AF.Copy),
                       reads=[rpq], writes=[r_Qt])
            dve.op(lambda h: h.tensor_tensor(out=tB[:], in0=Qt[:], in1=Qv[:], op=ALU.subtract), reads=[r_Qt, r_Q], writes=[r_tB])
            act.op(lambda h: h.activation(out=tB[:], in_=tB[:], func=AF.Exp), reads=[r_tB], writes=[r_tB])
            dve.op(lambda h: h.tensor_tensor(out=wv[:], in0=dtv[:], in1=tB[:], op=ALU.mult), reads=[r_dtv, r_tB], writes=[r_wv])
            act.op(lambda h: h.activation(out=cdv[:], in_=Qt[:], func=AF.Exp), reads=[r_Qt], writes=[r_cd])
            act.op(lambda h: h.activation(out=eQ[:], in_=Qv[:], func=AF.Exp), reads=[r_Q], writes=[r_eQ])
            if debug:
                for qi, (t_, r_) in enumerate(((dtv, r_dtv), (adt, r_adt), (Qv, r_Q), (Qt, r_Qt), (wv, r_wv), (cdv, r_cd), (eQ, r_eQ))):
                    sp.dma(DBG[:, qi * NTT * H2:(qi + 1) * NTT * H2], t_[:].rearrange("p a b -> p (a b)"), reads=[r_], semres=r_)

            Kx.barrier()
            Kx.recycle(tmark)
            tes.close()
            Xf = Ring(Kx, es, "Xf", [128, 2, T], BF16, 2)
            Xo = Ring(Kx, es, "Xo", [128, 2, T], BF16, 1)
            BTr = Ring(Kx, es, "BTr", [128, T], BF16, 2)
            CTr = Ring(Kx, es, "CTr", [128, T], BF16, 2)
            BOr = Ring(Kx, es, "BOr", [128, T], BF16, 1)
            GZr = Ring(Kx, es, "GZr", [128, NT, 256], F32, 1)
            xtm, r_xtm = sb("xtm", [128, NTT, 256], BF16)
            btm, r_btm = sb("btm", [128, NTT, 128], BF16)
            prevb, r_prevb = sb("prevb", [128, NT, 256], BF16)
            carry, r_carry = sb("carry", [128, 256])
            ctmp, r_ctmp = sb("ctmp", [128, 256])
            prevf, r_prevf = sb("prevf", [128, 256], BF16)
            xwr = Ring(Kx, es, "xw", [128, 256], BF16, 3)
            xdr = Ring(Kx, es, "xd", [128, 256], BF16, 4)
            t3r = Ring(Kx, es, "t3", [128, 256], F32, 2)
            ls4r = Ring(Kx, es, "ls4", [128, 4, 128], F32, 2)
            dc4r = Ring(Kx, es, "dc4", [128, 4, 128], F32, 4)
            lt4r = Ring(Kx, es, "lt4", [128, 4, 128], BF16, 4)
            Sfr = Ring(Kx, es, "Sf", [128, 128], F32, 3)
            Sbr = Ring(Kx, es, "Sb", [128, 128], F32, 3)
            t1r = Ring(Kx, es, "t1", [128, 256], F32, 2)
            t2r = Ring(Kx, es, "t2", [128, 256], F32, 2)
            sqr = Ring(Kx, es, "sq", [128, 256], F32, 2)
            ssr = Ring(Kx, es, "ss", [128, 2], F32, 2)
            ynr = Ring(Kx, es, "yn", [128, 256], BF16, 2)
            yTs = Ring(Kx, es, "yTs", [128, 2, T], BF16, 1)

            def hb(ap4):
                return ap4.unsqueeze(2).broadcast_to([128, 4, 64])

            def v4(ap):
                return ap.rearrange("p (h q) -> p h q", h=4)

            def load_main(g):
                X, rX = Xf.next()
                BT, rBT = BTr.next()
                CT, rCT = CTr.next()
                sp.dma_multi([(X[:, half, :], XBC[2 * g + half]) for half in range(2)], writes=[rX], semres=rX)
                sp.dma(BT[:], XBC[SSMT + g], writes=[rBT], semres=rBT)
                sp.dma(CT[:], XBC[SSMT + NG + g], writes=[rCT], semres=rCT)
                return (X, rX, BT, rBT, CT, rCT)

            def load_other(g):
                XO, rXO = Xo.next()
                BO, rBO = BOr.next()
                sp.dma_multi([(XO[:, half, :], XBO[2 * g + half]) for half in range(2)], writes=[rXO], semres=rXO)
                sp.dma(BO[:], XBO[SSMT + g], writes=[rBO], semres=rBO)
                return (XO, rXO, BO, rBO)

            def load_gz(g):
                Gz, rGz = GZr.next()
                sp.dma(Gz[:], GZ.rearrange("(c p) n -> p c n", p=128)[:, :, g * 256:(g + 1) * 256], writes=[rGz], semres=rGz)
                return (Gz, rGz)

            def emit_xw(c, colbase, g):
                xw, rxw = xwr.next()
                dve.op(lambda h: h.tensor_tensor(out=v4(xw[:]), in0=v4(xtm[:, c, :]),
                                                 in1=hb(wv[:, c, colbase + 4 * g:colbase + 4 * g + 4]), op=ALU.mult),
                       reads=[r_xtm, r_wv], writes=[rxw])
                return xw, rxw

            def emit_state(c, colbase, g, xw, rxw):
                ps, rps = psq.next()
                pe.op(lambda h: h.matmul(ps[:, 0:256], lhsT=btm[:, c, :], rhs=xw[:], start=True, stop=True),
                      reads=[r_btm, rxw], writes=[rps])
                dve.op(lambda h: h.tensor_tensor(out=v4(ctmp[:]), in0=v4(carry[:]),
                                                 in1=hb(cdv[:, c, colbase + 4 * g:colbase + 4 * g + 4]), op=ALU.mult),
                       reads=[r_carry, r_cd], writes=[r_ctmp])
                dve.op(lambda h: h.tensor_tensor(out=carry[:], in0=ctmp[:], in1=ps[:, 0:256], op=ALU.add),
                       reads=[r_ctmp, rps], writes=[r_carry])

            nxt = load_main(0)
            nxo = load_other(0)
            for g in range(NG):
                X, rX, BT, rBT, CT, rCT = nxt
                XO, rXO, BO, rBO = nxo
                if g + 1 < NG:
                    nxt = load_main(g + 1)

                def p1_front(c):
                    if c >= NT:
                        xs_, rxs_, bs_, rbs_, cc = XO, rXO, BO, rBO, c - NT
                    else:
                        xs_, rxs_, bs_, rbs_, cc = X, rX, BT, rBT, c
                    for half in range(2):
                        pe.op(lambda h, half=half: h.transpose(
                            out=pstA[:, half * 128:(half + 1) * 128], in_=xs_[:, half, cc * 128:(cc + 1) * 128],
                            identity=identb[:]), reads=[rxs_, r_identb], writes=[r_pstA], inc=False)
                    pe.op(lambda h: h.transpose(
                        out=pstA[:, 256:384], in_=bs_[:, cc * 128:(cc + 1) * 128], identity=identb[:]),
                        reads=[rbs_, r_identb], writes=[r_pstA])
                    act.op(lambda h: h.activation(out=xtm[:, c, :], in_=pstA[:, 0:256], func=AF.Copy),
                           reads=[r_pstA], writes=[r_xtm])
                    dve.op(lambda h: h.tensor_copy(out=btm[:, c, :], in_=pstA[:, 256:384]),
                           reads=[r_pstA], writes=[r_btm])
                    return emit_xw(c, NH, g)

                dve.op(lambda h: h.memset(carry[:], 0.0), writes=[r_carry])
                pend = p1_front(NTT - 1)
                for c in range(NTT - 1, -1, -1):
                    cur = pend
                    if c - 1 >= 0:
                        pend = p1_front(c - 1)
                    if c < NT:
                        act.op(lambda h, c=c: h.activation(out=prevb[:, c, :], in_=carry[:], func=AF.Copy),
                               reads=[r_carry], writes=[r_prevb])
                    emit_state(c, NH, g, *cur)
                if g + 1 < NG:
                    nxo = load_other(g + 1)
                Gz, rGz = load_gz(g)

                def p2_A(c):
                    csl = slice(c * 128, (c + 1) * 128)
                    pssc, r_pssc = psq.next()
                    pe.op(lambda h: h.matmul(pssc[:, 0:128], lhsT=BT[:, csl], rhs=CT[:, csl], start=True, stop=True),
                          reads=[rBT, rCT], writes=[r_pssc])
                    st = {"c": c, "csl": csl}
                    decs = []
                    for d in range(2):
                        c0 = d * NH + 4 * g
                        ls, rls = ls4r.next()
                        dve.op(lambda h, ls=ls, d=d, c0=c0: h.tensor_tensor(
                            out=ls[:], in0=cst[:, 4 + d, :].unsqueeze(1).broadcast_to([128, 4, 128]),
                            in1=adt[:, c, c0:c0 + 4].unsqueeze(2).broadcast_to([128, 4, 128]), op=ALU.mult),
                            reads=[r_cst, r_adt], writes=[rls])
                        pg, rpg = psseg.next()
                        for hh in range(4):
                            pe.op(lambda h, pg=pg, ls=ls, d=d, hh=hh: h.matmul(
                                pg[:, hh * 128:(hh + 1) * 128], lhsT=ls[:, hh, :], rhs=cst[:, 2 + d, :], start=True, stop=True),
                                reads=[rls, r_cst], writes=[rpg], inc=(hh == 3))
                        dc, rdc = dc4r.next()
                        act.op(lambda h, dc=dc, pg=pg: h.activation(out=dc[:].rearrange("p a b -> p (a b)"), in_=pg[:], func=AF.Exp),
                               reads=[rpg], writes=[rdc])
                        decs.append((dc, rdc))
                    st["decs"] = decs
                    Sf, rSf = Sfr.next()
                    Sb, rSb = Sbr.next()
                    dve.op(lambda h: h.tensor_tensor(out=Sf[:], in0=pssc[:, 0:128], in1=cst[:, 2, :], op=ALU.mult),
                           reads=[r_pssc, r_cst], writes=[rSf])
                    dve.op(lambda h: h.tensor_tensor(out=Sb[:], in0=pssc[:, 0:128], in1=cst[:, 3, :], op=ALU.mult),
                           reads=[r_pssc, r_cst], writes=[rSb])
                    st["S"] = [(Sf, rSf), (Sb, rSb)]
                    xds = []
                    for d in range(2):
                        c0 = d * NH + 4 * g
                        xd, rxd = xdr.next()
                        dve.op(lambda h, xd=xd, c0=c0: h.tensor_tensor(out=v4(xd[:]), in0=v4(xtm[:, c, :]),
                                                                       in1=hb(dtv[:, c, c0:c0 + 4]), op=ALU.mult),
                               reads=[r_xtm, r_dtv], writes=[rxd])
                        xds.append((xd, rxd))
                    st["xd"] = xds
                    st["xw"] = emit_xw(c, 0, g)
                    t3, rt3 = t3r.next()
                    dve.op(lambda h: h.tensor_tensor(out=t3[:], in0=xtm[:, c, :], in1=dsk[:, g * 256:(g + 1) * 256], op=ALU.mult),
                           reads=[r_xtm, r_dsk], writes=[rt3])
                    st["t3"] = (t3, rt3)
                    return st

                def p2_B(st):
                    c, csl = st["c"], st["csl"]
                    lts = []
                    for d in range(2):
                        dc, rdc = st["decs"][d]
                        Sd, rSd = st["S"][d]
                        Lt, rLt = lt4r.next()
                        dve.op(lambda h, Lt=Lt, dc=dc, Sd=Sd: h.tensor_tensor(
                            out=Lt[:], in0=dc[:], in1=Sd[:].unsqueeze(1).broadcast_to([128, 4, 128]), op=ALU.mult),
                            reads=[rdc, rSd], writes=[rLt])
                        lts.append((Lt, rLt))
                    pY, rpY = psYr.next()
                    for hh in range(4):
                        for d in range(2):
                            Lt, rLt = lts[d]
                            xd, rxd = st["xd"][d]
                            pe.op(lambda h, Lt=Lt, xd=xd, hh=hh, d=d: h.matmul(
                                pY[:, hh * 64:(hh + 1) * 64], lhsT=Lt[:, hh, :], rhs=xd[:, hh * 64:(hh + 1) * 64],
                                start=(d == 0), stop=(d == 1)), reads=[rLt, rxd], writes=[rpY], inc=(d == 1))
                    st["pY"] = (pY, rpY)
                    act.op(lambda h: h.activation(out=prevf[:], in_=carry[:], func=AF.Copy), reads=[r_carry], writes=[r_prevf])
                    pe.op(lambda h: h.matmul(psO[:, 0:256], lhsT=CT[:, csl], rhs=prevf[:], start=True, stop=True),
                          reads=[rCT, r_prevf], writes=[r_psO], inc=False)
                    pe.op(lambda h: h.matmul(psO[:, 256:512], lhsT=CT[:, csl], rhs=prevb[:, c, :], start=True, stop=True),
                          reads=[rCT, r_prevb], writes=[r_psO])
                    emit_state(c, 0, g, *st["xw"])

                def p2_C(st, yT, ryT):
                    c, csl = st["c"], st["csl"]
                    pY, rpY = st["pY"]
                    t3, rt3 = st["t3"]
                    t1, rt1 = t1r.next()
                    t2, rt2 = t2r.next()
                    dve.op(lambda h: h.tensor_tensor(out=v4(t1[:]), in0=v4(psO[:, 0:256]),
                                                     in1=hb(eQ[:, c, 4 * g:4 * g + 4]), op=ALU.mult),
                           reads=[r_psO, r_eQ], writes=[rt1])
                    dve.op(lambda h: h.tensor_tensor(out=v4(t2[:]), in0=v4(psO[:, 256:512]),
                                                     in1=hb(eQ[:, c, NH + 4 * g:NH + 4 * g + 4]), op=ALU.mult),
                           reads=[r_psO, r_eQ], writes=[rt2])
                    dve.op(lambda h: h.tensor_tensor(out=t2[:], in0=t2[:], in1=t3[:], op=ALU.add),
                           reads=[rt2, rt3], writes=[rt2])
                    dve.op(lambda h: h.tensor_tensor(out=t1[:], in0=t1[:], in1=pY[:, 0:256], op=ALU.add),
                           reads=[rt1, rpY], writes=[rt1])
                    dve.op(lambda h: h.tensor_tensor(out=t1[:], in0=t1[:], in1=t2[:], op=ALU.add),
                           reads=[rt1, rt2], writes=[rt1])
                    dve.op(lambda h: h.tensor_tensor(out=t1[:], in0=t1[:], in1=Gz[:, c, :], op=ALU.mult),
                           reads=[rt1, rGz], writes=[rt1])
                    sq, rsq = sqr.next()
                    ss, rss = ssr.next()
                    act.op(lambda h: h.activation(out=sq[:], in_=t1[:], func=AF.Square, accum_out=ss[:, 0:1]),
                           reads=[rt1], writes=[rsq, rss])
                    act.op(lambda h: h.activation(out=ss[:, 1:2], in_=ss[:, 0:1], func=AF.Ln, scale=1.0 / 256.0, bias=RMS_EPS),
                           reads=[rss], writes=[rss])
                    act.op(lambda h: h.activation(out=ss[:, 1:2], in_=ss[:, 1:2], func=AF.Exp, scale=-0.5),
                           reads=[rss], writes=[rss])
                    yn, ryn = ynr.next()
                    dve.op(lambda h: h.scalar_tensor_tensor(
                        out=yn[:], in0=t1[:], scalar=ss[:, 1:2], in1=nwb[:, g * 256:(g + 1) * 256], op0=ALU.mult, op1=ALU.mult),
                        reads=[rt1, rss, r_nwb], writes=[ryn])
                    for half in range(2):
                        pe.op(lambda h, half=half: h.transpose(
                            out=pstA[:, 512 + half * 128:512 + (half + 1) * 128], in_=yn[:, half * 128:(half + 1) * 128], identity=identb[:]),
                            reads=[ryn, r_identb], writes=[r_pstA], inc=(half == 1))
                    act.op(lambda h: h.activation(
                        out=yT[:, :, csl], in_=pstA[:, 512:768].rearrange("p (a b) -> p a b", a=2), func=AF.Copy),
                        reads=[r_pstA], writes=[ryT])

                dve.op(lambda h: h.memset(carry[:], 0.0), writes=[r_carry])
                yT, ryT = yTs.next()
                stn = p2_A(0)
                for c in range(NT):
                    stc = stn
                    if c + 1 < NT:
                        stn = p2_A(c + 1)
                    p2_B(stc)
                    p2_C(stc, yT, ryT)
                sp.dma_multi([(YT[2 * g + half], yT[:, half, :]) for half in range(2)], reads=[ryT], semres=ryT)
            stage_end(es, mark)

        def gemm_tm(es, src_T, wsrc, ncols, dst, tag):
            Wring = Ring(Kx, es, "W%s_" % tag, [128, KC, 512], BF16, 2)
            aT, r_aT = one(Kx, es, "aT" + tag, [128, KC, TB], BF16)
            r_aTg = [Kx.res("aTg%s_%d" % (tag, i)) for i in range((KC + 7) // 8)]
            stz = Ring(Kx, es, "st" + tag, [128, 512], F32, 4)
            psacc = Ring(Kx, es, "ps" + tag, [128, 512], F32, 6, psum=True)
            ev = 0
            for b in range(T // TB):
                sv = src_T.rearrange("k p t -> p k t")
                for gi, k0 in enumerate(range(0, KC, 8)):
                    sp.dma(aT[:, k0:k0 + 8, :], sv[:, k0:k0 + 8, b * TB:(b + 1) * TB], writes=[r_aTg[gi]], semres=r_aTg[gi])
                nsl = ncols // 512
                nxt = load_w_slab(Wring, wsrc, 0, 512)
                for s in range(nsl):
                    Wt, rW = nxt
                    if s + 1 < nsl:
                        nxt = load_w_slab(Wring, wsrc, (s + 1) * 512, 512)
                    for tt in range(TBt):
                        pa, rpa = psacc.next()
                        for kc in range(KC):
                            pe.op(lambda h, pa=pa, tt=tt, kc=kc, Wt=Wt: h.matmul(
                                pa[:], lhsT=aT[:, kc, tt * 128:(tt + 1) * 128], rhs=Wt[:, kc, :],
                                start=(kc == 0), stop=(kc == KC - 1)), reads=[r_aTg[kc // 8], rW], writes=[rpa], inc=(kc == KC - 1))
                        sz, rsz = stz.next()
                        evac_copy(ev, sz[:], pa[:], [rpa], [rsz])
                        ev += 1
                        r0 = b * TB + tt * 128
                        sp.dma(dst[r0:r0 + 128, s * 512:(s + 1) * 512], sz[:], reads=[rsz], semres=rsz)

        if stop_after >= 6:
          with ExitStack() as es:
            mark = len(Kx.all_res)
            gemm_tm(es, YT, w_out, D, MIX, "4")
            stage_end(es, mark)

        def ln_stage(es, branch, resid, gate_idx, g_src, b_src, dst, with_h2, tag):
            gbc, r_gbc = one(Kx, es, "gbc" + tag, [128, D], F32)
            lg, r_lg = one(Kx, es, "lg" + tag, [128, D], F32)
            lb, r_lb = one(Kx, es, "lb" + tag, [128, D], F32)
            sp.dma(gbc[:], BCS[gate_idx], writes=[r_gbc], semres=r_gbc)
            sp.dma(lg[:], g_src[:, :], writes=[r_lg], semres=r_lg)
            sp.dma(lb[:], b_src[:, :], writes=[r_lb], semres=r_lb)
            if with_h2:
                G2, r_G2 = one(Kx, es, "G2" + tag, [128, D], F32)
                B2, r_B2 = one(Kx, es, "B2" + tag, [128, D], F32)
                sp.dma(G2[:], BCS[1], writes=[r_G2], semres=r_G2)
                sp.dma(B2[:], BCS[2], writes=[r_B2], semres=r_B2)
                h2r = Ring(Kx, es, "h2" + tag, [128, D], BF16, 1)
                h2s = Ring(Kx, es, "h2s" + tag, [128, KC, 256], BF16, 1)
                pst = Ring(Kx, es, "pst" + tag, [128, 1024], BF16, 4, psum=True)
            nring = 2 if with_h2 else 3
            mr = Ring(Kx, es, "mr" + tag, [128, D], F32, nring)
            xr = Ring(Kx, es, "xr" + tag, [128, D], F32, nring)
            str_ = Ring(Kx, es, "bs" + tag, [128, D // 512, 6], F32, 3)
            mvr = Ring(Kx, es, "mv" + tag, [128, 4], F32, 3)
            ev = [0]
            hsb = [None]

            def ln_load(tt):
                rows = slice(tt * 128, (tt + 1) * 128)
                m, rm = mr.next()
                xx, rxx = xr.next()
                sp.dma(m[:], branch[rows, :], writes=[rm], semres=rm)
                sp.dma(xx[:], resid[rows, :], writes=[rxx], semres=rxx)
                return (m, rm, xx, rxx)

            def ln_A(tt, ld):
                m, rm, xx, rxx = ld
                dve.op(lambda h: h.tensor_tensor(out=m[:], in0=m[:], in1=gbc[:], op=ALU.mult), reads=[rm, r_gbc], writes=[rm])
                dve.op(lambda h: h.scalar_tensor_tensor(out=xx[:], in0=xx[:], scalar=alpha, in1=m[:], op0=ALU.mult, op1=ALU.add),
                       reads=[rxx, rm], writes=[rxx])
                st, rst = str_.next()
                for q in range(D // 512):
                    dve.op(lambda h, q=q: h.bn_stats(out=st[:, q, :], in_=xx[:, q * 512:(q + 1) * 512]),
                           reads=[rxx], writes=[rst])
                mv, rmv = mvr.next()
                dve.op(lambda h: h.bn_aggr(out=mv[:, 0:2], in_=st[:].rearrange("p a b -> p (a b)")),
                       reads=[rst], writes=[rmv])
                act.op(lambda h: h.activation(out=mv[:, 2:3], in_=mv[:, 1:2], func=AF.Ln, bias=LN_EPS), reads=[rmv], writes=[rmv])
                act.op(lambda h: h.activation(out=mv[:, 2:3], in_=mv[:, 2:3], func=AF.Exp, scale=-0.5), reads=[rmv], writes=[rmv])
                return (mv, rmv)

            def ln_B(tt, ld, mvv):
                rows = slice(tt * 128, (tt + 1) * 128)
                m, rm, xx, rxx = ld
                mv, rmv = mvv
                dve.op(lambda h: h.scalar_tensor_tensor(out=mv[:, 3:4], in0=mv[:, 0:1], scalar=-1.0, in1=mv[:, 2:3],
                                                        op0=ALU.mult, op1=ALU.mult), reads=[rmv], writes=[rmv])
                act.op(lambda h: h.activation(out=xx[:], in_=xx[:], func=AF.Identity, bias=mv[:, 3:4], scale=mv[:, 2:3]),
                       reads=[rxx, rmv], writes=[rxx])
                dve.op(lambda h: h.tensor_tensor(out=m[:], in0=xx[:], in1=lg[:], op=ALU.mult), reads=[rxx, r_lg], writes=[rm])
                dve.op(lambda h: h.tensor_tensor(out=m[:], in0=m[:], in1=lb[:], op=ALU.add), reads=[rm, r_lb], writes=[rm])
                sp.dma(dst[rows, :], m[:], reads=[rm], semres=rm)
                if with_h2:
                    h2, rh2 = h2r.next()
                    dve.op(lambda h: h.tensor_tensor(out=xx[:], in0=xx[:], in1=G2[:], op=ALU.mult), reads=[rxx, r_G2], writes=[rxx])
                    dve.op(lambda h: h.tensor_tensor(out=h2[:], in0=xx[:], in1=B2[:], op=ALU.add), reads=[rxx, r_B2], writes=[rh2])
                    if tt % 2 == 0:
                        hsb[0] = h2s.next()
                    hs, rhs = hsb[0]
                    for q8 in range(KC // 8):
                        pt, rpt = pst.next()
                        for q in range(8):
                            kc = q8 * 8 + q
                            pe.op(lambda h, q=q, kc=kc: h.transpose(
                                out=pt[:, q * 128:(q + 1) * 128], in_=h2[:, kc * 128:(kc + 1) * 128], identity=identb[:]),
                                reads=[rh2, r_identb], writes=[rpt], inc=(q == 7))
                        o_ap = hs[:, q8 * 8:(q8 + 1) * 8, (tt % 2) * 128:(tt % 2 + 1) * 128]
                        i_ap = pt[:].rearrange("p (a b) -> p a b", a=8)
                        evac_copy(ev[0], o_ap, i_ap, [rpt], [rhs])
                        ev[0] += 1
                    if tt % 2 == 1:
                        t0 = (tt - 1) * 128
                        hv = H2T.rearrange("k p t -> p k t")
                        sp.dma_multi([(hv[:, k0:k0 + 8, t0:t0 + 256], hs[:, k0:k0 + 8, :]) for k0 in range(0, KC, 8)],
                                     reads=[rhs], semres=rhs)

            lds = [ln_load(0)]
            pend = None
            for tt in range(NT):
                if nring >= 3 and tt + 1 < NT:
                    lds.append(ln_load(tt + 1))
                mvv = ln_A(tt, lds[tt])
                if pend is not None:
                    ln_B(*pend)
                if nring < 3 and tt + 1 < NT:
                    lds.append(ln_load(tt + 1))
                pend = (tt, lds[tt], mvv)
            ln_B(*pend)

        if stop_after >= 7:
          with ExitStack() as es:
            mark = len(Kx.all_res)
            ln_stage(es, MIX, x_in, 0, ln1g_bc, ln1b_bc, X1, True, "5")
            stage_end(es, mark)

        if stop_after >= 8:
          with ExitStack() as es:
            mark = len(Kx.all_res)
            Wring = Ring(Kx, es, "W6_", [128, KC, 512], BF16, 2)
            aT, r_aT = one(Kx, es, "aT6", [128, KC, TB], BF16)
            r_aTg = [Kx.res("aTg6_%d" % i) for i in range((KC + 7) // 8)]
            rr = Ring(Kx, es, "rl6", [128, 512], F32, 3)
            us = Ring(Kx, es, "us6", [128, TB], BF16, 3)
            psacc = Ring(Kx, es, "ps6", [128, 512], F32, 6, psum=True)
            for b in range(T // TB):
                sv = H2T.rearrange("k p t -> p k t")
                for gi, k0 in enumerate(range(0, KC, 8)):
                    sp.dma(aT[:, k0:k0 + 8, :], sv[:, k0:k0 + 8, b * TB:(b + 1) * TB], writes=[r_aTg[gi]], semres=r_aTg[gi])
                nsl = DFF // 512
                nxt = load_w_slab(Wring, w_up, 0, 512)
                for s in range(nsl):
                    Wt, rW = nxt
                    if s + 1 < nsl:
                        nxt = load_w_slab(Wring, w_up, (s + 1) * 512, 512)
                    for ct in range(4):
                        u, ru = us.next()
                        for sub in range(TB // 512):
                            pa, rpa = psacc.next()
                            for kc in range(KC):
                                pe.op(lambda h, pa=pa, kc=kc, Wt=Wt, ct=ct, sub=sub: h.matmul(
                                    pa[:], lhsT=Wt[:, kc, ct * 128:(ct + 1) * 128], rhs=aT[:, kc, sub * 512:(sub + 1) * 512],
                                    start=(kc == 0), stop=(kc == KC - 1)), reads=[r_aTg[kc // 8], rW], writes=[rpa], inc=(kc == KC - 1))
                            r_, rr_ = rr.next()
                            act.op(lambda h, r_=r_, pa=pa: h.activation(out=r_[:], in_=pa[:], func=AF.Relu), reads=[rpa], writes=[rr_])
                            dve.op(lambda h, r_=r_, u=u, sub=sub: h.tensor_tensor(out=u[:, sub * 512:(sub + 1) * 512], in0=r_[:], in1=r_[:], op=ALU.mult),
                                   reads=[rr_], writes=[ru])
                        sp.dma(UT[4 * s + ct, :, b * TB:(b + 1) * TB], u[:], reads=[ru], semres=ru)
            stage_end(es, mark)

        if stop_after >= 9:
          with ExitStack() as es:
            mark = len(Kx.all_res)
            FCG = 8
            uT, r_uT = one(Kx, es, "uT7", [128, FC, 512], BF16)
            r_uTg = [Kx.res("uTg7_%d" % i) for i in range((FC + 7) // 8)]
            Wd = Ring(Kx, es, "Wd7", [128, 2, FCG, 512], BF16, 2)
            stz = Ring(Kx, es, "st7", [128, 512], F32, 4)
            ps8 = [one(Kx, es, "ps7_%d" % i, [128, 512], F32, psum=True) for i in range(8)]
            wdv = w_down.rearrange("(fc p) n -> p fc n", p=128)

            def load_wd(sp_i, fcg):
                W_, rW_ = Wd.next()
                pool.dma_multi([(W_[:, s, :, :], wdv[:, fcg * FCG:(fcg + 1) * FCG, (2 * sp_i + s) * 512:(2 * sp_i + s + 1) * 512])
                                for s in range(2)], writes=[rW_], semres=rW_)
                return W_, rW_

            ev = 0
            for b in range(T // 512):
                sv = UT.rearrange("f p t -> p f t")
                for gi, k0 in enumerate(range(0, FC, 8)):
                    sp.dma(uT[:, k0:k0 + 8, :], sv[:, k0:k0 + 8, b * 512:(b + 1) * 512], writes=[r_uTg[gi]], semres=r_uTg[gi])
                seq = [(spi, fcg) for spi in range(D // 1024) for fcg in range(FC // FCG)]
                nxt = load_wd(*seq[0])
                for qi, (spi, fcg) in enumerate(seq):
                    W_, rW_ = nxt
                    if qi + 1 < len(seq):
                        nxt = load_wd(*seq[qi + 1])
                    for fcl in range(FCG):
                        fc = fcg * FCG + fcl
                        for tt in range(4):
                            for s in range(2):
                                pa, rpa = ps8[tt * 2 + s]
                                pe.op(lambda h, pa=pa, fc=fc, tt=tt, s=s, fcl=fcl, W_=W_: h.matmul(
                                    pa[:], lhsT=uT[:, fc, tt * 128:(tt + 1) * 128], rhs=W_[:, s, fcl, :],
                                    start=(fc == 0), stop=(fc == FC - 1)), reads=[r_uTg[fc // 8], rW_], writes=[rpa],
                                    inc=(fc == FC - 1) or (fcl == FCG - 1 and tt == 3 and s == 1))
                    if fcg == FC // FCG - 1:
                        for tt in range(4):
                            for s in range(2):
                                pa, rpa = ps8[tt * 2 + s]
                                sz, rsz = stz.next()
                                evac_copy(ev, sz[:], pa[:], [rpa], [rsz])
                                ev += 1
                                r0 = b * 512 + tt * 128
                                c0 = (2 * spi + s) * 512
                                sp.dma(FFs[r0:r0 + 128, c0:c0 + 512], sz[:], reads=[rsz], semres=rsz)
            stage_end(es, mark)

        if stop_after >= 10:
          with ExitStack() as es:
            mark = len(Kx.all_res)
            ln_stage(es, FFs, X1, 3, ln2g_bc, ln2b_bc, out, False, "8")
            stage_end(es, mark)
        Kx.barrier()
    return nc


def make_consts():
    r = np.arange(128)[:, None]
    c = np.arange(128)[None, :]
    m = np.stack([(r == c), np.ones((128, 128), bool), (r <= c), (r >= c), (r > c), (r < c)], axis=1)
    return np.ascontiguousarray(m.astype(np.float32))


def fm(v, nchunk):
    return np.ascontiguousarray(np.asarray(v).reshape(nchunk, 128).T)


def bc(v):
    v = np.asarray(v, dtype=np.float32).reshape(1, -1)
    return np.ascontiguousarray(np.broadcast_to(v, (128, v.shape[1])))


def prep_inputs(cfg, inp, n_batch):
    D, T, KC, NG, NSC, NH = cfg.D, cfg.T, cfg.KC, cfg.NG, cfg.NSC, cfg.NH
    DSSM, DXBC = cfg.DSSM, cfg.DXBC
    f32 = lambda a: np.ascontiguousarray(np.asarray(a, dtype=np.float32))
    x = np.asarray(inp["x"])
    w_in = f32(inp["w_in"][0])
    dt0 = DSSM + DXBC
    wdt_e = f32(w_in[:, dt0:dt0 + 2 * NH])
    wdt_o = f32(np.concatenate([w_in[:, dt0 + NH:dt0 + 2 * NH], w_in[:, dt0:dt0 + NH]], axis=1))
    shared = {
        "w_ada": f32(inp["w_ada"][0]), "b_ada_fm": fm(inp["b_ada"][0], 6 * KC), "w_in": w_in,
        "cb_fm": fm(inp["ssm_conv_b"][0], DXBC // 128),
        "dsk_bc": bc(np.repeat(np.asarray(inp["ssm_d"][0]), 64)), "nw_bc": bc(inp["ssm_norm_w"][0]),
        "scn_fm": fm(inp["sc_norm_w"][0], NSC), "w_out": f32(inp["w_out"][0]),
        "ln1g_bc": bc(inp["ln1_g"][0]), "ln1b_bc": bc(inp["ln1_b"][0]),
        "w_up": f32(inp["w_up"][0]), "w_down": f32(inp["w_down"][0]),
        "ln2g_bc": bc(inp["ln2_g"][0]), "ln2b_bc": bc(inp["ln2_b"][0]), "consts": make_consts(),
    }
    cwv = np.asarray(inp["ssm_conv_w"][0])
    scwv = np.asarray(inp["sc_conv_w"][0])
    par = []
    for odd in (0, 1):
        cw_ = cwv[::-1] if odd else cwv
        sc_ = scwv[::-1] if odd else scwv
        f, b_ = ("b", "f") if odd else ("f", "b")
        par.append({
            "w_dt": wdt_o if odd else wdt_e,
            "cw_fm": np.ascontiguousarray(cw_.T.reshape(DXBC // 128, 128, 5).transpose(1, 0, 2).astype(np.float32)),
            "scw_fm": np.ascontiguousarray(sc_.T.reshape(NSC, 128, 3).transpose(1, 0, 2).astype(np.float32)),
            "dtb_bc": bc(np.concatenate([inp["ssm_dt_bias_" + f][0], inp["ssm_dt_bias_" + b_][0]])),
            "alog_bc": bc(np.concatenate([inp["ssm_a_log_" + f][0], inp["ssm_a_log_" + b_][0]])),
        })
    maps = []
    for core in range(2 * n_batch):
        b, odd = core // 2, core % 2
        xl = x[b, ::-1] if odd else x[b]
        m = dict(shared)
        m.update(par[odd])
        m["x"] = f32(xl)
        m["c_fm"] = fm(inp["c"][b], KC)
        maps.append(m)
    return maps


def assemble(cfg, results, n_batch):
    T, D = cfg.T, cfg.D
    o = np.empty((n_batch, 2 * T, D), np.float32)
    for core in range(2 * n_batch):
        b, odd = core // 2, core % 2
        r = np.asarray(results[core]["out"])
        if odd:
            o[b, T:] = r[::-1]
        else:
            o[b, :T] = r
    return o


_NC_CACHE = {}


def kernel(**inputs):
    cfg = FULL
    if "nc" not in _NC_CACHE:
        _NC_CACHE["nc"] = build(cfg)
    nc = _NC_CACHE["nc"]
    maps = prep_inputs(cfg, inputs, 4)
    res = run_bass_kernel_spmd(nc, maps, core_ids=list(range(8)))
    return assemble(cfg, res.results, 4)
```

```python
import numpy as np
from contextlib import ExitStack
import concourse.bass as bass
import concourse.mybir as mybir
from concourse.bass_utils import run_bass_kernel_spmd

F32 = mybir.dt.float32
BF16 = mybir.dt.bfloat16
AF = mybir.ActivationFunctionType
ALU = mybir.AluOpType

LN_EPS = 1e-5
RMS_EPS = 1e-5


class Cfg:
    def __init__(s, D, T, NG, NSC, DFF, alpha):
        s.D, s.T, s.NG, s.NSC, s.DFF, s.alpha = D, T, NG, NSC, DFF, alpha
        s.KC = D // 128
        s.DSSM = NG * 256
        s.NH = NG * 4
        s.DSC = NSC * 128
        s.DXBC = s.DSSM + 2 * NG * 128
        s.DIN = s.DSSM + s.DXBC + 2 * s.NH + 3 * s.DSC
        s.FC = DFF // 128
        s.NT = T // 128
        s.TB = min(1024, T)
        s.NFM = (s.DXBC + 3 * s.DSC) // 128
        s.NXO = (s.DSSM + NG * 128) // 128
        assert s.DSSM + s.DSC == D


FULL = Cfg(4096, 2048, 8, 16, 16384, 2.0 ** 0.25)


class Res:
    __slots__ = ("name", "w", "readers", "sem", "semcnt", "sw", "psum")

    def __init__(self, name):
        self.name = name
        self.w = None
        self.readers = {}
        self.sem = None
        self.semcnt = 0
        self.sw = False
        self.psum = False


class Eng:
    def __init__(self, Kx, h, name, is_pe=False):
        self.K, self.h, self.name, self.is_pe = Kx, h, name, is_pe
        self.sem = Kx.new_sem("e_" + name)
        self.cnt = 0
        self.seen = {}

    def _wait(self, tok, war):
        if tok is None:
            return
        sem, val, en = tok
        if en == self.name and (self.is_pe or war):
            return
        key = id(sem)
        if self.seen.get(key, 0) >= val:
            return
        self.h.wait_ge(sem, val)
        self.seen[key] = val

    def deps(self, reads, writes):
        for r in reads:
            self._wait(r.w, False)
            if r.psum:
                for t in r.readers.values():
                    self._wait(t, True)
        for w in writes:
            self._wait(w.w, False)
            for t in w.readers.values():
                self._wait(t, True)

    def op(self, fn, reads=(), writes=(), inc=True):
        self.deps(reads, writes)
        ins = fn(self.h)
        if inc:
            ins.then_inc(self.sem, 1)
            self.cnt += 1
            tok = (self.sem, self.cnt, self.name)
        else:
            tok = (self.sem, self.cnt + 1, self.name)
        for r in reads:
            r.readers[id(tok[0])] = tok
        for w in writes:
            w.w = tok
            w.readers = {}
        return ins

    def dma(self, out, in_, reads=(), writes=(), semres=None):
        self.dma_multi([(out, in_)], reads, writes, semres)

    def dma_multi(self, pairs, reads=(), writes=(), semres=None):
        self.deps(reads, writes)
        if semres.sem is None:
            semres.sw = (self.name == "pool")
            semres.sem, semres.semcnt = self.K.take_dma_sem("d_" + semres.name, semres.sw)
        assert semres.sw == (self.name == "pool"), semres.name
        for out, in_ in pairs:
            self.h.dma_start(out=out, in_=in_).then_inc(semres.sem, 16)
            semres.semcnt += 16
        tok = (semres.sem, semres.semcnt, "dma")
        for r in reads:
            r.readers[id(tok[0])] = tok
        for w in writes:
            w.w = tok
            w.readers = {}


class Kctx:
    def __init__(self, nc, es):
        self.nc, self.es = nc, es
        self.nsem = 0
        self.all_res = []
        self.sem_pool = []
        self.sem_pool_sw = []

    def new_sem(self, name):
        self.nsem += 1
        return self.es.enter_context(self.nc.semaphore("%s_%d" % (name[:20], self.nsem)))

    def take_dma_sem(self, name, sw):
        pool = self.sem_pool_sw if sw else self.sem_pool
        if pool:
            return pool.pop()
        return self.new_sem(name), 0

    def recycle(self, mark):
        for r in self.all_res[mark:]:
            if r.sem is not None:
                (self.sem_pool_sw if r.sw else self.sem_pool).append((r.sem, r.semcnt))
        del self.all_res[mark:]

    def res(self, name):
        r = Res(name)
        self.all_res.append(r)
        return r

    def engines(self):
        return [self.pe, self.act, self.dve, self.pool, self.sp]

    def barrier(self):
        toks = [(e.sem, e.cnt, "x") for e in self.engines() if e.cnt > 0]
        for r in self.all_res:
            if r.sem is not None and r.semcnt > 0:
                toks.append((r.sem, r.semcnt, "x"))
        for e in self.engines():
            for t in toks:
                if t[0] is e.sem and e.is_pe:
                    continue
                e._wait(t, False)
        for r in self.all_res:
            r.w = None
            r.readers = {}


class Ring:
    def __init__(self, Kx, es, name, shape, dtype, n, psum=False):
        self.tiles = []
        for i in range(n):
            nm = "%s%d" % (name, i)
            t = es.enter_context((Kx.nc.psum_tensor if psum else Kx.nc.sbuf_tensor)(nm, shape, dtype))
            rr = Kx.res(nm)
            rr.psum = psum
            self.tiles.append((t, rr))
        self.i = 0

    def next(self):
        t = self.tiles[self.i % len(self.tiles)]
        self.i += 1
        return t


def one(Kx, es, name, shape, dtype, psum=False):
    t = es.enter_context((Kx.nc.psum_tensor if psum else Kx.nc.sbuf_tensor)(name, shape, dtype))
    rr = Kx.res(name)
    rr.psum = psum
    return t, rr


def build(cfg, debug=False, stop_after=99):
    nc = bass.Bass("TRN2", target_bir_lowering=False)
    D, T, KC, NG, NSC, NH, FC, NT, TB = cfg.D, cfg.T, cfg.KC, cfg.NG, cfg.NSC, cfg.NH, cfg.FC, cfg.NT, cfg.TB
    DSSM, DSC, DXBC, DIN, DFF = cfg.DSSM, cfg.DSC, cfg.DXBC, cfg.DIN, cfg.DFF
    NFM, NXO = cfg.NFM, cfg.NXO
    LT = 2 * T
    TH = T + 128
    H2 = 2 * NH
    NTT = 2 * NT
    TBt = TB // 128
    NXT = DXBC // 128
    SSMT = DSSM // 128
    alpha = float(cfg.alpha)

    def din(name, shape, dt=F32):
        return nc.dram_tensor(name, shape, dt, kind="ExternalInput").ap()

    def dscr(name, shape, dt=F32):
        return nc.dram_tensor(name, shape, dt, kind=("ExternalOutput" if debug else "Internal")).ap()

    x_in = din("x", [LT, D])
    c_fm = din("c_fm", [128, KC])
    w_ada = din("w_ada", [D, 6 * D])
    b_ada_fm = din("b_ada_fm", [128, 6 * KC])
    w_in = din("w_in", [D, DIN])
    w_dt = din("w_dt", [D, H2])
    cw_fm = din("cw_fm", [128, NXT, 5])
    cb_fm = din("cb_fm", [128, NXT])
    dtb_bc = din("dtb_bc", [128, H2])
    alog_bc = din("alog_bc", [128, H2])
    dsk_bc = din("dsk_bc", [128, DSSM])
    nw_bc = din("nw_bc", [128, DSSM])
    scw_fm = din("scw_fm", [128, NSC, 3])
    scn_fm = din("scn_fm", [128, NSC])
    w_out = din("w_out", [D, D])
    ln1g_bc = din("ln1g_bc", [128, D])
    ln1b_bc = din("ln1b_bc", [128, D])
    w_up = din("w_up", [D, DFF])
    w_down = din("w_down", [DFF, D])
    ln2g_bc = din("ln2g_bc", [128, D])
    ln2b_bc = din("ln2b_bc", [128, D])
    consts = din("consts", [128, 6, 128])
    out = nc.dram_tensor("out", [T, D], F32, kind="ExternalOutput").ap()

    PTm = dscr("PTm", [NFM, 128, TH])
    PTo = dscr("PTo", [NXO, 128, T])
    Ztm = dscr("Ztm", [T, DSSM])
    GZ = dscr("GZ", [T, DSSM])
    XBC = dscr("XBC", [NXT, 128, T], BF16)
    XBO = dscr("XBO", [NXO, 128, T], BF16)
    YT = dscr("YT", [KC, 128, T], BF16)
    MIX = dscr("MIX", [T, D])
    X1 = dscr("X1", [T, D])
    H2T = dscr("H2T", [KC, 128, T], BF16)
    UT = dscr("UT", [FC, 128, T], BF16)
    FFs = dscr("FF", [T, D])
    BCS = dscr("BCS", [4, 128, D])
    DBG = dscr("DBG", [128, 8 * NTT * H2]) if debug else None

    with ExitStack() as ges:
        Kx = Kctx(nc, ges)
        Kx.pe = Eng(Kx, nc.tensor, "pe", is_pe=True)
        Kx.act = Eng(Kx, nc.scalar, "act")
        Kx.dve = Eng(Kx, nc.vector, "dve")
        Kx.pool = Eng(Kx, nc.gpsimd, "pool")
        Kx.sp = Eng(Kx, nc.sync, "sp")
        pe, act, dve, pool, sp = Kx.pe, Kx.act, Kx.dve, Kx.pool, Kx.sp

        def stage_end(es, mark):
            Kx.barrier()
            Kx.recycle(mark)
            es.close()

        cst, r_cst = one(Kx, ges, "cst", [128, 6, 128], F32)
        identb, r_identb = one(Kx, ges, "identb", [128, 128], BF16)
        onesdiv, r_onesdiv = one(Kx, ges, "onesdiv", [128, 128], F32)
        modfm, r_mod = one(Kx, ges, "modfm", [128, 6 * KC], F32)
        sc1p, r_sc1p = one(Kx, ges, "sc1p", [128, KC], F32)
        dtraw, r_dtraw = one(Kx, ges, "dtraw", [128, NTT, H2], F32)
        scb, r_scb = one(Kx, ges, "scb", [128, KC], BF16)
        IDENT, ONES, LE, GE, GT, LTm = (cst[:, i, :] for i in range(6))

        sp.dma(cst[:], consts[:, :, :], writes=[r_cst], semres=r_cst)
        dve.op(lambda h: h.tensor_copy(out=identb[:], in_=cst[:, 0, :]), reads=[r_cst], writes=[r_identb])
        dve.op(lambda h: h.tensor_scalar(out=onesdiv[:], in0=cst[:, 1, :], scalar1=1.0 / 128.0, scalar2=None,
                                         op0=ALU.mult), reads=[r_cst], writes=[r_onesdiv])

        def evac_copy(i, out_ap, in_ap, reads, writes):
            if i % 2 == 0:
                act.op(lambda h: h.activation(out=out_ap, in_=in_ap, func=AF.Copy), reads=reads, writes=writes)
            else:
                dve.op(lambda h: h.tensor_copy(out=out_ap, in_=in_ap), reads=reads, writes=writes)

        def load_w_slab(Wring, wsrc, col0, width):
            Wt, rW = Wring.next()
            pool.dma(Wt[:, :, 0:width], wsrc.rearrange("(kc p) n -> p kc n", p=128)[:, :, col0:col0 + width],
                     writes=[rW], semres=rW)
            return Wt, rW

        with ExitStack() as es:
            mark = len(Kx.all_res)
            Wring = Ring(Kx, es, "W0_", [128, KC, 512], BF16, 2)
            cf, r_cf = one(Kx, es, "cf", [128, KC], F32)
            bada, r_bada = one(Kx, es, "bada", [128, 6 * KC], F32)
            psm, r_psm = one(Kx, es, "psm", [128, 512], F32, psum=True)
            sp.dma(cf[:], c_fm[:, :], writes=[r_cf], semres=r_cf)
            sp.dma(bada[:], b_ada_fm[:, :], writes=[r_bada], semres=r_bada)
            act.op(lambda h: h.activation(out=scb[:], in_=cf[:], func=AF.Silu), reads=[r_cf], writes=[r_scb])
            nsl = 2 * D // 512
            nxt = load_w_slab(Wring, w_ada, 0, 512)
            for s in range(nsl):
                Wt, rW = nxt
                if s + 1 < nsl:
                    nxt = load_w_slab(Wring, w_ada, (s + 1) * 512, 512)
                for ct in range(4):
                    j = 4 * s + ct
                    for kc in range(KC):
                        pe.op(lambda h, Wt=Wt, ct=ct, kc=kc, j=j: h.matmul(
                            psm[:, j:j + 1], lhsT=Wt[:, kc, ct * 128:(ct + 1) * 128], rhs=scb[:, kc:kc + 1],
                            start=(kc == 0), stop=(kc == KC - 1)),
                            reads=[rW, r_scb], writes=[r_psm], inc=(kc == KC - 1))
            dve.op(lambda h: h.tensor_tensor(out=modfm[:, 0:2 * KC], in0=psm[:, 0:2 * KC], in1=bada[:, 0:2 * KC], op=ALU.add),
                   reads=[r_psm, r_bada], writes=[r_mod])
            dve.op(lambda h: h.tensor_scalar(out=sc1p[:], in0=modfm[:, KC:2 * KC], scalar1=1.0, scalar2=None,
                                             op0=ALU.add), reads=[r_mod], writes=[r_sc1p])
            stage_end(es, mark)

        if stop_after >= 2:
          with ExitStack() as es:
            mark = len(Kx.all_res)
            Wring = Ring(Kx, es, "W2_", [128, KC, 512], BF16, 2)
            hT, r_hT = one(Kx, es, "hT", [128, KC, TB + 128], BF16)
            xring = Ring(Kx, es, "xt", [128, D], F32, 2)
            wdt, r_wdt = one(Kx, es, "wdt", [128, KC, H2], BF16)
            stg = Ring(Kx, es, "stg", [128, TB + 128], F32, 2)
            stz = Ring(Kx, es, "stz", [128, 512], F32, 3)
            pstr = Ring(Kx, es, "pstr", [128, 512], F32, 2, psum=True)
            psacc = Ring(Kx, es, "psacc", [128, 512], F32, 4, psum=True)
            psdt = Ring(Kx, es, "psdt", [128, 512], F32, 2, psum=True)
            pool.dma(wdt[:], w_dt.rearrange("(kc p) n -> p kc n", p=128), writes=[r_wdt], semres=r_wdt)

            def slab_list(c0, c1):
                r = []
                c = c0
                while c < c1:
                    w = min(512, c1 - c)
                    r.append((c, w))
                    c += w
                return r

            sc0 = DSSM + DXBC + H2
            blocks = []
            for b in range(T // TB):
                tiles = list(range(b * TBt, (b + 1) * TBt))
                last = (b == T // TB - 1)
                if last:
                    tiles.append(NT)
                blocks.append(("main", tiles, last))
            for b in range(T // TB):
                blocks.append(("other", [NT + i for i in range(b * TBt, (b + 1) * TBt)], False))

            ev = 0
            for kind, tiles, last in blocks:
                ntok = len(tiles) * 128
                for ti, tile in enumerate(tiles):
                    xt, rx = xring.next()
                    sp.dma(xt[:], x_in[tile * 128:(tile + 1) * 128, :], writes=[rx], semres=rx)
                    for q4 in range(KC // 4):
                        pt, rpt = pstr.next()
                        for q in range(4):
                            kc = q4 * 4 + q
                            pe.op(lambda h, pt=pt, xt=xt, q=q, kc=kc: h.transpose(
                                out=pt[:, q * 128:(q + 1) * 128], in_=xt[:, kc * 128:(kc + 1) * 128],
                                identity=cst[:, 0, :]), reads=[rx, r_cst], writes=[rpt], inc=(q == 3))
                        for q in range(4):
                            kc = q4 * 4 + q
                            dst = hT[:, kc, ti * 128:(ti + 1) * 128]
                            if (ev % 2) == 0:
                                act.op(lambda h, dst=dst, pt=pt, q=q, kc=kc: h.activation(
                                    out=dst, in_=pt[:, q * 128:(q + 1) * 128], func=AF.Identity,
                                    bias=modfm[:, kc:kc + 1], scale=sc1p[:, kc:kc + 1]),
                                    reads=[rpt, r_mod, r_sc1p], writes=[r_hT])
                            else:
                                dve.op(lambda h, dst=dst, pt=pt, q=q, kc=kc: h.tensor_scalar(
                                    out=dst, in0=pt[:, q * 128:(q + 1) * 128], scalar1=sc1p[:, kc:kc + 1],
                                    scalar2=modfm[:, kc:kc + 1], op0=ALU.mult, op1=ALU.add),
                                    reads=[rpt, r_mod, r_sc1p], writes=[r_hT])
                            ev += 1
                for ti, tile in enumerate(tiles):
                    if tile >= NTT or (kind == "main" and tile == NT):
                        continue
                    pd, rpd = psdt.next()
                    for kc in range(KC):
                        pe.op(lambda h, pd=pd, ti=ti, kc=kc: h.matmul(
                            pd[:, 0:H2], lhsT=hT[:, kc, ti * 128:(ti + 1) * 128], rhs=wdt[:, kc, :],
                            start=(kc == 0), stop=(kc == KC - 1)), reads=[r_hT, r_wdt], writes=[rpd],
                            inc=(kc == KC - 1))
                    evac_copy(ev, dtraw[:, tile, :], pd[:, 0:H2], [rpd], [r_dtraw])
                    ev += 1
                if kind == "main":
                    slabs = [("tm", c, w) for c, w in slab_list(0, DSSM)]
                    slabs += [("fm", c, w) for c, w in slab_list(DSSM, DSSM + DXBC)]
                    slabs += [("fm", c, w) for c, w in slab_list(sc0, DIN)]
                else:
                    slabs = [("fm", c, w) for c, w in slab_list(DSSM, DSSM + DSSM + NG * 128)]
                nxt = load_w_slab(Wring, w_in, slabs[0][1], slabs[0][2])
                for si, (mode, c0, wd) in enumerate(slabs):
                    Wt, rW = nxt
                    if si + 1 < len(slabs):
                        nxt = load_w_slab(Wring, w_in, slabs[si + 1][1], slabs[si + 1][2])
                    if mode == "tm":
                        for ti, tile in enumerate(tiles):
                            if tile == NT:
                                continue
                            pa, rpa = psacc.next()
                            for kc in range(KC):
                                pe.op(lambda h, pa=pa, ti=ti, kc=kc, Wt=Wt, wd=wd: h.matmul(
                                    pa[:, 0:wd], lhsT=hT[:, kc, ti * 128:(ti + 1) * 128], rhs=Wt[:, kc, 0:wd],
                                    start=(kc == 0), stop=(kc == KC - 1)), reads=[r_hT, rW], writes=[rpa],
                                    inc=(kc == KC - 1))
                            sz, rsz = stz.next()
                            evac_copy(ev, sz[:, 0:wd], pa[:, 0:wd], [rpa], [rsz])
                            ev += 1
                            sp.dma(Ztm[tile * 128:(tile + 1) * 128, c0:c0 + wd], sz[:, 0:wd], reads=[rsz], semres=rsz)
                    else:
                        for ct in range(wd // 128):
                            col = c0 + ct * 128
                            fmt = (col - DSSM) // 128 if col < sc0 else NXT + (col - sc0) // 128
                            sg, rsg = stg.next()
                            n0 = 0
                            while n0 < ntok:
                                nn = min(512, ntok - n0)
                                pa, rpa = psacc.next()
                                for kc in range(KC):
                                    pe.op(lambda h, pa=pa, kc=kc, Wt=Wt, ct=ct, n0=n0, nn=nn: h.matmul(
                                        pa[:, 0:nn], lhsT=Wt[:, kc, ct * 128:(ct + 1) * 128], rhs=hT[:, kc, n0:n0 + nn],
                                        start=(kc == 0), stop=(kc == KC - 1)), reads=[r_hT, rW], writes=[rpa],
                                        inc=(kc == KC - 1))
                                evac_copy(ev, sg[:, n0:n0 + nn], pa[:, 0:nn], [rpa], [rsg])
                                ev += 1
                                n0 += nn
                            t0 = tiles[0] * 128
                            if kind == "main":
                                sp.dma(PTm[fmt, :, t0:t0 + ntok], sg[:, 0:ntok], reads=[rsg], semres=rsg)
                            else:
                                sp.dma(PTo[fmt, :, t0 - T:t0 - T + ntok], sg[:, 0:ntok], reads=[rsg], semres=rsg)
            stage_end(es, mark)

        if stop_after >= 3:
          with ExitStack() as es:
            mark = len(Kx.all_res)
            WringB = Ring(Kx, es, "W0b_", [128, KC, 512], BF16, 2)
            badaB, r_badaB = one(Kx, es, "badaB", [128, 6 * KC], F32)
            psmB, r_psmB = one(Kx, es, "psmB", [128, 512], F32, psum=True)
            sp.dma(badaB[:], b_ada_fm[:, :], writes=[r_badaB], semres=r_badaB)
            s_lo, s_hi = 2 * D // 512, 6 * D // 512
            nxt = load_w_slab(WringB, w_ada, s_lo * 512, 512)
            for s in range(s_lo, s_hi):
                Wt, rW = nxt
                if s + 1 < s_hi:
                    nxt = load_w_slab(WringB, w_ada, (s + 1) * 512, 512)
                for ct in range(4):
                    j = 4 * s + ct
                    for kc in range(KC):
                        pe.op(lambda h, Wt=Wt, ct=ct, kc=kc, j=j: h.matmul(
                            psmB[:, j:j + 1], lhsT=Wt[:, kc, ct * 128:(ct + 1) * 128], rhs=scb[:, kc:kc + 1],
                            start=(kc == 0), stop=(kc == KC - 1)),
                            reads=[rW, r_scb], writes=[r_psmB], inc=(kc == KC - 1))
            cw, r_cw = one(Kx, es, "cw", [128, NXT, 5], F32)
            cb, r_cb = one(Kx, es, "cb", [128, NXT], F32)
            sp.dma(cw[:], cw_fm[:, :, :], writes=[r_cw], semres=r_cw)
            sp.dma(cb[:], cb_fm[:, :], writes=[r_cb], semres=r_cb)
            Um = Ring(Kx, es, "Um", [128, T + 4], F32, 3)
            Uo = Ring(Kx, es, "Uo", [128, T + 4], F32, 3)
            accr = Ring(Kx, es, "cacc", [128, T], F32, 2)
            obr = Ring(Kx, es, "cob", [128, T], BF16, 3)
            zr = Ring(Kx, es, "zr", [128, DSSM], F32, 3)
            zo = Ring(Kx, es, "zo", [128, DSSM], F32, 2)
            for (u, ru) in Um.tiles:
                dve.op(lambda h, u=u: h.memset(u[:, 0:2], 0.0), writes=[ru])
            for (u, ru) in Uo.tiles:
                dve.op(lambda h, u=u: h.memset(u[:, T + 2:T + 4], 0.0), writes=[ru])

            def conv_tile(U, rU, i, dst):
                acc, racc = accr.next()
                dve.op(lambda h: h.tensor_scalar(out=acc[:], in0=U[:, 0:T], scalar1=cw[:, i, 0:1],
                                                 scalar2=cb[:, i:i + 1], op0=ALU.mult, op1=ALU.add),
                       reads=[rU, r_cw, r_cb], writes=[racc])
                for k in range(1, 5):
                    dve.op(lambda h, k=k: h.scalar_tensor_tensor(
                        out=acc[:], in0=U[:, k:k + T], scalar=cw[:, i, k:k + 1], in1=acc[:],
                        op0=ALU.mult, op1=ALU.add), reads=[rU, r_cw, racc], writes=[racc])
                ob, rob = obr.next()
                act.op(lambda h: h.activation(out=ob[:], in_=acc[:], func=AF.Silu), reads=[racc], writes=[rob])
                sp.dma(dst, ob[:], reads=[rob], semres=rob)

            def pipelined(n, load_fn, compute_fn, depth=2):
                q = [load_fn(i) for i in range(min(depth, n))]
                for i in range(n):
                    if i + depth < n:
                        q.append(load_fn(i + depth))
                    compute_fn(i, q[i])

            def ld_m(i):
                U, rU = Um.next()
                sp.dma(U[:, 2:T + 4], PTm[i, :, 0:T + 2], writes=[rU], semres=rU)
                return U, rU

            def ld_o(i):
                U, rU = Uo.next()
                sp.dma_multi([(U[:, 2:T + 2], PTo[i, :, 0:T]), (U[:, 0:2], PTm[i, :, T - 2:T])], writes=[rU], semres=rU)
                return U, rU

            def ld_z(tt):
                z, rz = zr.next()
                sp.dma(z[:], Ztm[tt * 128:(tt + 1) * 128, :], writes=[rz], semres=rz)
                return z, rz

            def do_z(tt, zz):
                z, rz = zz
                g, rg = zo.next()
                act.op(lambda h: h.activation(out=g[:], in_=z[:], func=AF.Silu), reads=[rz], writes=[rg])
                sp.dma(GZ[tt * 128:(tt + 1) * 128, :], g[:], reads=[rg], semres=rg)

            pipelined(NXT, ld_m, lambda i, u: conv_tile(u[0], u[1], i, XBC[i]))
            pipelined(NXO, ld_o, lambda i, u: conv_tile(u[0], u[1], i, XBO[i]))
            pipelined(NT, ld_z, do_z)
            dve.op(lambda h: h.tensor_tensor(out=modfm[:, 2 * KC:6 * KC], in0=psmB[:, 2 * KC:6 * KC], in1=badaB[:, 2 * KC:6 * KC], op=ALU.add),
                   reads=[r_psmB, r_badaB], writes=[r_mod])
            stage_end(es, mark)

          with ExitStack() as es:
            mark = len(Kx.all_res)
            p1, r_p1 = one(Kx, es, "p1", [128, 3, KC], F32)
            for q, src in enumerate((2, 4, 5)):
                dve.op(lambda h, q=q, src=src: h.tensor_scalar(
                    out=p1[:, q, :], in0=modfm[:, src * KC:(src + 1) * KC], scalar1=1.0, scalar2=None, op0=ALU.add),
                    reads=[r_mod], writes=[r_p1])
            dg, r_dg = one(Kx, es, "dg", [128, KC, 128], F32)
            bct, r_bct = one(Kx, es, "bct", [128, D], F32)
            lg, r_lg = one(Kx, es, "lg", [128, D], F32)
            lb, r_lb = one(Kx, es, "lb", [128, D], F32)
            sp.dma(lg[:], ln1g_bc[:, :], writes=[r_lg], semres=r_lg)
            sp.dma(lb[:], ln1b_bc[:, :], writes=[r_lb], semres=r_lb)
            psb = Ring(Kx, es, "psb", [128, 512], F32, 2, psum=True)

            def bcast_row(vec_ap):
                dve.op(lambda h: h.tensor_tensor(
                    out=dg[:], in0=cst[:, 0, :].unsqueeze(1).broadcast_to([128, KC, 128]),
                    in1=vec_ap.unsqueeze(2).broadcast_to([128, KC, 128]), op=ALU.mult),
                    reads=[r_cst, r_p1, r_mod], writes=[r_dg])
                for q in range(D // 512):
                    pb, rpb = psb.next()
                    pe.op(lambda h, pb=pb, q=q: h.matmul(pb[:], lhsT=cst[:, 1, :], rhs=dg[:, 4 * q:4 * q + 4, :],
                                                         start=True, stop=True), reads=[r_dg, r_cst], writes=[rpb])
                    evac_copy(q, bct[:, q * 512:(q + 1) * 512], pb[:], [rpb], [r_bct])

            bcast_row(p1[:, 0, :])
            sp.dma(BCS[0], bct[:], reads=[r_bct], semres=r_bct)
            bcast_row(p1[:, 1, :])
            dve.op(lambda h: h.tensor_tensor(out=lg[:], in0=lg[:], in1=bct[:], op=ALU.mult),
                   reads=[r_lg, r_bct], writes=[r_lg])
            dve.op(lambda h: h.tensor_tensor(out=lb[:], in0=lb[:], in1=bct[:], op=ALU.mult),
                   reads=[r_lb, r_bct], writes=[r_lb])
            sp.dma(BCS[1], lg[:], reads=[r_lg], semres=r_lg)
            bcast_row(modfm[:, 3 * KC:4 * KC])
            dve.op(lambda h: h.tensor_tensor(out=lb[:], in0=lb[:], in1=bct[:], op=ALU.add),
                   reads=[r_lb, r_bct], writes=[r_lb])
            sp.dma(BCS[2], lb[:], reads=[r_lb], semres=r_lb)
            bcast_row(p1[:, 2, :])
            sp.dma(BCS[3], bct[:], reads=[r_bct], semres=r_bct)
            stage_end(es, mark)

        if stop_after >= 4:
          with ExitStack() as es:
            mark = len(Kx.all_res)
            scw, r_scw = one(Kx, es, "scw", [128, NSC, 3], F32)
            scn, r_scn = one(Kx, es, "scn", [128, NSC], F32)
            sp.dma(scw[:], scw_fm[:, :, :], writes=[r_scw], semres=r_scw)
            sp.dma(scn[:], scn_fm[:, :], writes=[r_scn], semres=r_scn)
            Hr = Ring(Kx, es, "Hr", [128, T + 1], F32, 2)
            Cr = Ring(Kx, es, "Cr", [128, T + 1], F32, 2)
            Br = Ring(Kx, es, "Br", [128, T], F32, 2)
            Mr = Ring(Kx, es, "Mr", [128, T + 2], F32, 2)
            Ar = Ring(Kx, es, "Ar", [128, T], F32, 2)
            Sr = Ring(Kx, es, "Sr", [128, T], F32, 2)
            Rr = Ring(Kx, es, "Rr", [128, T], F32, 2)
            Or = Ring(Kx, es, "Or", [128, T], BF16, 2)
            pss = Ring(Kx, es, "pss", [128, 512], F32, 4, psum=True)
            for (m, rm) in Mr.tiles:
                dve.op(lambda h, m=m: h.memset(m[:, 0:1], 0.0), writes=[rm])
            def ld_sc(j):
                Hb, rH = Hr.next()
                Cb, rC = Cr.next()
                Bb, rB = Br.next()
                sp.dma(Hb[:], PTm[NXT + j, :, 0:T + 1], writes=[rH], semres=rH)
                sp.dma(Cb[:], PTm[NXT + 2 * NSC + j, :, 0:T + 1], writes=[rC], semres=rC)
                sp.dma(Bb[:], PTm[NXT + NSC + j, :, 0:T], writes=[rB], semres=rB)
                return (Hb, rH, Cb, rC, Bb, rB)

            scq = [ld_sc(0)]
            for j in range(NSC):
                if j + 1 < NSC:
                    scq.append(ld_sc(j + 1))
                Hb, rH, Cb, rC, Bb, rB = scq[j]
                M, rM = Mr.next()
                dve.op(lambda h, M=M, Cb=Cb, Hb=Hb: h.tensor_tensor(out=M[:, 1:T + 2], in0=Cb[:], in1=Hb[:], op=ALU.mult),
                       reads=[rC, rH], writes=[rM])
                A, rA = Ar.next()
                dve.op(lambda h, A=A, M=M, j=j: h.tensor_scalar(out=A[:], in0=M[:, 0:T], scalar1=scw[:, j, 0:1],
                                                                scalar2=None, op0=ALU.mult),
                       reads=[rM, r_scw], writes=[rA])
                for k in (1, 2):
                    dve.op(lambda h, A=A, M=M, j=j, k=k: h.scalar_tensor_tensor(
                        out=A[:], in0=M[:, k:k + T], scalar=scw[:, j, k:k + 1], in1=A[:], op0=ALU.mult, op1=ALU.add),
                        reads=[rM, r_scw, rA], writes=[rA])
                dve.op(lambda h, A=A, Bb=Bb: h.tensor_tensor(out=A[:], in0=A[:], in1=Bb[:], op=ALU.mult),
                       reads=[rA, rB], writes=[rA])
                S, rS = Sr.next()
                act.op(lambda h, S=S, A=A: h.activation(out=S[:], in_=A[:], func=AF.Square), reads=[rA], writes=[rS])
                R, rR = Rr.next()
                for q in range(T // 512):
                    ps, rps = pss.next()
                    pe.op(lambda h, ps=ps, S=S, q=q: h.matmul(ps[:], lhsT=onesdiv[:], rhs=S[:, q * 512:(q + 1) * 512],
                                                              start=True, stop=True), reads=[rS, r_onesdiv], writes=[rps])
                    act.op(lambda h, ps=ps, R=R, q=q: h.activation(out=R[:, q * 512:(q + 1) * 512], in_=ps[:], func=AF.Ln,
                                                                   bias=RMS_EPS), reads=[rps], writes=[rR])
                act.op(lambda h, R=R: h.activation(out=R[:], in_=R[:], func=AF.Exp, scale=-0.5), reads=[rR], writes=[rR])
                O, rO = Or.next()
                dve.op(lambda h, O=O, A=A, R=R, j=j: h.scalar_tensor_tensor(
                    out=O[:], in0=A[:], scalar=scn[:, j:j + 1], in1=R[:], op0=ALU.mult, op1=ALU.mult),
                    reads=[rA, rR, r_scn], writes=[rO])
                sp.dma(YT[SSMT + j], O[:], reads=[rO], semres=rO)
            stage_end(es, mark)

        if stop_after >= 5:
          with ExitStack() as es:
            mark = len(Kx.all_res)
            def sb(name, shape, dt=F32):
                return one(Kx, es, name, shape, dt)
            dtb, r_dtb = sb("dtb", [128, H2])
            alg, r_alg = sb("alg", [128, H2])
            dsk, r_dsk = sb("dsk", [128, DSSM])
            nwb, r_nwb = sb("nwb", [128, DSSM])
            for t_, r_, s_ in ((dtb, r_dtb, dtb_bc), (alg, r_alg, alog_bc), (dsk, r_dsk, dsk_bc), (nwb, r_nwb, nw_bc)):
                sp.dma(t_[:], s_[:, :], writes=[r_], semres=r_)
            dtv, r_dtv = sb("dtv", [128, NTT, H2])
            adt, r_adt = sb("adt", [128, NTT, H2])
            wv, r_wv = sb("wv", [128, NTT, H2])
            cdv, r_cd = sb("cdv", [128, NTT, H2])
            eQ, r_eQ = sb("eQ", [128, NTT, H2])
            psq = Ring(Kx, es, "psq", [128, 512], F32, 2, psum=True)
            pstA, r_pstA = one(Kx, es, "pstA", [128, 1024], BF16, psum=True)
            psseg = Ring(Kx, es, "psseg", [128, 512], F32, 2, psum=True)
            psYr = Ring(Kx, es, "psY", [128, 512], F32, 2, psum=True)
            psO, r_psO = one(Kx, es, "psO", [128, 512], F32, psum=True)

            tes = ExitStack()
            tmark = len(Kx.all_res)
            xb, r_xb = one(Kx, tes, "xb", [128, NTT, H2], F32)
            tA, r_tA = one(Kx, tes, "tA", [128, NTT, H2], F32)
            tB, r_tB = one(Kx, tes, "tB", [128, NTT, H2], F32)
            Qv, r_Q = one(Kx, tes, "Qv", [128, NTT, H2], F32)
            Qt, r_Qt = one(Kx, tes, "Qt", [128, NTT, H2], F32)
            bc3 = lambda ap: ap.unsqueeze(1).broadcast_to([128, NTT, H2])
            dve.op(lambda h: h.tensor_tensor(out=xb[:], in0=dtraw[:], in1=bc3(dtb[:]), op=ALU.add),
                   reads=[r_dtraw, r_dtb], writes=[r_xb])
            dve.op(lambda h: h.scalar_tensor_tensor(out=tA[:], in0=xb[:], scalar=-1.0, in1=xb[:], op0=ALU.mult, op1=ALU.min),
                   reads=[r_xb], writes=[r_tA])
            act.op(lambda h: h.activation(out=tA[:], in_=tA[:], func=AF.Exp), reads=[r_tA], writes=[r_tA])
            act.op(lambda h: h.activation(out=tA[:], in_=tA[:], func=AF.Ln, bias=1.0), reads=[r_tA], writes=[r_tA])
            dve.op(lambda h: h.scalar_tensor_tensor(out=dtv[:], in0=xb[:], scalar=0.0, in1=tA[:], op0=ALU.max, op1=ALU.add),
                   reads=[r_xb, r_tA], writes=[r_dtv])
            act.op(lambda h: h.activation(out=alg[:], in_=alg[:], func=AF.Exp), reads=[r_alg], writes=[r_alg])
            dve.op(lambda h: h.scalar_tensor_tensor(out=adt[:], in0=dtv[:], scalar=-1.0, in1=bc3(alg[:]),
                                                    op0=ALU.mult, op1=ALU.mult), reads=[r_dtv, r_alg], writes=[r_adt])
            for c in range(NTT):
                pq, rpq = psq.next()
                pe.op(lambda h, pq=pq, c=c: h.matmul(pq[:, 0:NH], lhsT=cst[:, 2, :], rhs=adt[:, c, 0:NH], start=True, stop=True),
                      reads=[r_adt, r_cst], writes=[rpq])
                pe.op(lambda h, pq=pq, c=c: h.matmul(pq[:, NH:H2], lhsT=cst[:, 3, :], rhs=adt[:, c, NH:H2], start=True, stop=True),
                      reads=[r_adt, r_cst], writes=[rpq])
                pe.op(lambda h, pq=pq, c=c: h.matmul(pq[:, H2:2 * H2], lhsT=cst[:, 1, :], rhs=adt[:, c, :], start=True, stop=True),
                      reads=[r_adt, r_cst], writes=[rpq])
                dve.op(lambda h, pq=pq, c=c: h.tensor_copy(out=Qv[:, c, :], in_=pq[:, 0:H2]), reads=[rpq], writes=[r_Q])
                act.op(lambda h, pq=pq, c=c: h.activation(out=Qt[:, c, :], in_=pq[:, H2:2 * H2], func=AF.Copy),
                       reads=[rpq], writes=[r_Qt])
            dve.op(lambda h: h.tensor_tensor(out=tB[:], in0=Qt[:], in1=Qv[:], op=ALU.subtract), reads=[r_Qt, r_Q], writes=[r_tB])
            act.op(lambda h: h.activation(out=tB[:], in_=tB[:], func=AF.Exp), reads=[r_tB], writes=[r_tB])
            dve.op(lambda h: h.tensor_tensor(out=wv[:], in0=dtv[:], in1=tB[:], op=ALU.mult), reads=[r_dtv, r_tB], writes=[r_wv])
            act.op(lambda h: h.activation(out=cdv[:], in_=Qt[:], func=AF.Exp), reads=[r_Qt], writes=[r_cd])
            act.op(lambda h: h.activation(out=eQ[:], in_=Qv[:], func=AF.Exp), reads=[r_Q], writes=[r_eQ])
            if debug:
                for qi, (t_, r_) in enumerate(((dtv, r_dtv), (adt, r_adt), (Qv, r_Q), (Qt, r_Qt), (wv, r_wv), (cdv, r_cd), (eQ, r_eQ))):
                    sp.dma(DBG[:, qi * NTT * H2:(qi + 1) * NTT * H2], t_[:].rearrange("p a b -> p (a b)"), reads=[r_], semres=r_)

            Kx.barrier()
            Kx.recycle(tmark)
            tes.close()
            Xf = Ring(Kx, es, "Xf", [128, 2, T], BF16, 2)
            Xo = Ring(Kx, es, "Xo", [128, 2, T], BF16, 1)
            BTr = Ring(Kx, es, "BTr", [128, T], BF16, 2)
            CTr = Ring(Kx, es, "CTr", [128, T], BF16, 2)
            BOr = Ring(Kx, es, "BOr", [128, T], BF16, 1)
            GZr = Ring(Kx, es, "GZr", [128, NT, 256], F32, 1)
            xtm, r_xtm = sb("xtm", [128, NTT, 256], BF16)
            btm, r_btm = sb("btm", [128, NTT, 128], BF16)
            prevb, r_prevb = sb("prevb", [128, NT, 256], BF16)
            carry, r_carry = sb("carry", [128, 256])
            ctmp, r_ctmp = sb("ctmp", [128, 256])
            prevf, r_prevf = sb("prevf", [128, 256], BF16)
            xwr = Ring(Kx, es, "xw", [128, 256], BF16, 3)
            xdr = Ring(Kx, es, "xd", [128, 256], BF16, 4)
            t3r = Ring(Kx, es, "t3", [128, 256], F32, 2)
            ls4r = Ring(Kx, es, "ls4", [128, 4, 128], F32, 2)
            dc4r = Ring(Kx, es, "dc4", [128, 4, 128], F32, 4)
            lt4r = Ring(Kx, es, "lt4", [128, 4, 128], BF16, 4)
            Sfr = Ring(Kx, es, "Sf", [128, 128], F32, 3)
            Sbr = Ring(Kx, es, "Sb", [128, 128], F32, 3)
            t1r = Ring(Kx, es, "t1", [128, 256], F32, 3)
            t2r = Ring(Kx, es, "t2", [128, 256], F32, 2)
            sqr = Ring(Kx, es, "sq", [128, 256], F32, 2)
            ssr = Ring(Kx, es, "ss", [128, 2], F32, 3)
            ynr = Ring(Kx, es, "yn", [128, 256], BF16, 2)
            yTs = Ring(Kx, es, "yTs", [128, 2, T], BF16, 1)

            def hb(ap4):
                return ap4.unsqueeze(2).broadcast_to([128, 4, 64])

            def v4(ap):
                return ap.rearrange("p (h q) -> p h q", h=4)

            def load_main(g):
                X, rX = Xf.next()
                BT, rBT = BTr.next()
                CT, rCT = CTr.next()
                sp.dma_multi([(X[:, half, :], XBC[2 * g + half]) for half in range(2)], writes=[rX], semres=rX)
                sp.dma(BT[:], XBC[SSMT + g], writes=[rBT], semres=rBT)
                sp.dma(CT[:], XBC[SSMT + NG + g], writes=[rCT], semres=rCT)
                return (X, rX, BT, rBT, CT, rCT)

            def load_other(g):
                XO, rXO = Xo.next()
                BO, rBO = BOr.next()
                sp.dma_multi([(XO[:, half, :], XBO[2 * g + half]) for half in range(2)], writes=[rXO], semres=rXO)
                sp.dma(BO[:], XBO[SSMT + g], writes=[rBO], semres=rBO)
                return (XO, rXO, BO, rBO)

            def load_gz(g):
                Gz, rGz = GZr.next()
                sp.dma(Gz[:], GZ.rearrange("(c p) n -> p c n", p=128)[:, :, g * 256:(g + 1) * 256], writes=[rGz], semres=rGz)
                return (Gz, rGz)

            def emit_xw(c, colbase, g):
                xw, rxw = xwr.next()
                dve.op(lambda h: h.tensor_tensor(out=v4(xw[:]), in0=v4(xtm[:, c, :]),
                                                 in1=hb(wv[:, c, colbase + 4 * g:colbase + 4 * g + 4]), op=ALU.mult),
                       reads=[r_xtm, r_wv], writes=[rxw])
                return xw, rxw

            def emit_state(c, colbase, g, xw, rxw):
                ps, rps = psq.next()
                pe.op(lambda h: h.matmul(ps[:, 0:256], lhsT=btm[:, c, :], rhs=xw[:], start=True, stop=True),
                      reads=[r_btm, rxw], writes=[rps])
                dve.op(lambda h: h.tensor_tensor(out=v4(ctmp[:]), in0=v4(carry[:]),
                                                 in1=hb(cdv[:, c, colbase + 4 * g:colbase + 4 * g + 4]), op=ALU.mult),
                       reads=[r_carry, r_cd], writes=[r_ctmp])
                dve.op(lambda h: h.tensor_tensor(out=carry[:], in0=ctmp[:], in1=ps[:, 0:256], op=ALU.add),
                       reads=[r_ctmp, rps], writes=[r_carry])

            nxt = load_main(0)
            nxo = load_other(0)
            for g in range(NG):
                X, rX, BT, rBT, CT, rCT = nxt
                XO, rXO, BO, rBO = nxo
                if g + 1 < NG:
                    nxt = load_main(g + 1)

                def p1_front(c):
                    if c >= NT:
                        xs_, rxs_, bs_, rbs_, cc = XO, rXO, BO, rBO, c - NT
                    else:
                        xs_, rxs_, bs_, rbs_, cc = X, rX, BT, rBT, c
                    for half in range(2):
                        pe.op(lambda h, half=half: h.transpose(
                            out=pstA[:, half * 128:(half + 1) * 128], in_=xs_[:, half, cc * 128:(cc + 1) * 128],
                            identity=identb[:]), reads=[rxs_, r_identb], writes=[r_pstA], inc=False)
                    pe.op(lambda h: h.transpose(
                        out=pstA[:, 256:384], in_=bs_[:, cc * 128:(cc + 1) * 128], identity=identb[:]),
                        reads=[rbs_, r_identb], writes=[r_pstA])
                    act.op(lambda h: h.activation(out=xtm[:, c, :], in_=pstA[:, 0:256], func=AF.Copy),
                           reads=[r_pstA], writes=[r_xtm])
                    dve.op(lambda h: h.tensor_copy(out=btm[:, c, :], in_=pstA[:, 256:384]),
                           reads=[r_pstA], writes=[r_btm])
                    return emit_xw(c, NH, g)

                dve.op(lambda h: h.memset(carry[:], 0.0), writes=[r_carry])
                pend = p1_front(NTT - 1)
                for c in range(NTT - 1, -1, -1):
                    cur = pend
                    if c - 1 >= 0:
                        pend = p1_front(c - 1)
                    if c < NT:
                        act.op(lambda h, c=c: h.activation(out=prevb[:, c, :], in_=carry[:], func=AF.Copy),
                               reads=[r_carry], writes=[r_prevb])
                    emit_state(c, NH, g, *cur)
                if g + 1 < NG:
                    nxo = load_other(g + 1)
                Gz, rGz = load_gz(g)

                def p2_A(c):
                    csl = slice(c * 128, (c + 1) * 128)
                    pssc, r_pssc = psq.next()
                    pe.op(lambda h: h.matmul(pssc[:, 0:128], lhsT=BT[:, csl], rhs=CT[:, csl], start=True, stop=True),
                          reads=[rBT, rCT], writes=[r_pssc])
                    st = {"c": c, "csl": csl}
                    decs = []
                    for d in range(2):
                        c0 = d * NH + 4 * g
                        ls, rls = ls4r.next()
                        dve.op(lambda h, ls=ls, d=d, c0=c0: h.tensor_tensor(
                            out=ls[:], in0=cst[:, 4 + d, :].unsqueeze(1).broadcast_to([128, 4, 128]),
                            in1=adt[:, c, c0:c0 + 4].unsqueeze(2).broadcast_to([128, 4, 128]), op=ALU.mult),
                            reads=[r_cst, r_adt], writes=[rls])
                        pg, rpg = psseg.next()
                        for hh in range(4):
                            pe.op(lambda h, pg=pg, ls=ls, d=d, hh=hh: h.matmul(
                                pg[:, hh * 128:(hh + 1) * 128], lhsT=ls[:, hh, :], rhs=cst[:, 2 + d, :], start=True, stop=True),
                                reads=[rls, r_cst], writes=[rpg], inc=(hh == 3))
                        dc, rdc = dc4r.next()
                        act.op(lambda h, dc=dc, pg=pg: h.activation(out=dc[:].rearrange("p a b -> p (a b)"), in_=pg[:], func=AF.Exp),
                               reads=[rpg], writes=[rdc])
                        decs.append((dc, rdc))
                    st["decs"] = decs
                    Sf, rSf = Sfr.next()
                    Sb, rSb = Sbr.next()
                    dve.op(lambda h: h.tensor_tensor(out=Sf[:], in0=pssc[:, 0:128], in1=cst[:, 2, :], op=ALU.mult),
                           reads=[r_pssc, r_cst], writes=[rSf])
                    dve.op(lambda h: h.tensor_tensor(out=Sb[:], in0=pssc[:, 0:128], in1=cst[:, 3, :], op=ALU.mult),
                           reads=[r_pssc, r_cst], writes=[rSb])
                    st["S"] = [(Sf, rSf), (Sb, rSb)]
                    xds = []
                    for d in range(2):
                        c0 = d * NH + 4 * g
                        xd, rxd = xdr.next()
                        dve.op(lambda h, xd=xd, c0=c0: h.tensor_tensor(out=v4(xd[:]), in0=v4(xtm[:, c, :]),
                                                                       in1=hb(dtv[:, c, c0:c0 + 4]), op=ALU.mult),
                               reads=[r_xtm, r_dtv], writes=[rxd])
                        xds.append((xd, rxd))
                    st["xd"] = xds
                    st["xw"] = emit_xw(c, 0, g)
                    t3, rt3 = t3r.next()
                    dve.op(lambda h: h.tensor_tensor(out=t3[:], in0=xtm[:, c, :], in1=dsk[:, g * 256:(g + 1) * 256], op=ALU.mult),
                           reads=[r_xtm, r_dsk], writes=[rt3])
                    st["t3"] = (t3, rt3)
                    return st

                def p2_B(st):
                    c, csl = st["c"], st["csl"]
                    lts = []
                    for d in range(2):
                        dc, rdc = st["decs"][d]
                        Sd, rSd = st["S"][d]
                        Lt, rLt = lt4r.next()
                        dve.op(lambda h, Lt=Lt, dc=dc, Sd=Sd: h.tensor_tensor(
                            out=Lt[:], in0=dc[:], in1=Sd[:].unsqueeze(1).broadcast_to([128, 4, 128]), op=ALU.mult),
                            reads=[rdc, rSd], writes=[rLt])
                        lts.append((Lt, rLt))
                    pY, rpY = psYr.next()
                    for hh in range(4):
                        for d in range(2):
                            Lt, rLt = lts[d]
                            xd, rxd = st["xd"][d]
                            pe.op(lambda h, Lt=Lt, xd=xd, hh=hh, d=d: h.matmul(
                                pY[:, hh * 64:(hh + 1) * 64], lhsT=Lt[:, hh, :], rhs=xd[:, hh * 64:(hh + 1) * 64],
                                start=(d == 0), stop=(d == 1)), reads=[rLt, rxd], writes=[rpY], inc=(d == 1))
                    st["pY"] = (pY, rpY)
                    act.op(lambda h: h.activation(out=prevf[:], in_=carry[:], func=AF.Copy), reads=[r_carry], writes=[r_prevf])
                    pe.op(lambda h: h.matmul(psO[:, 0:256], lhsT=CT[:, csl], rhs=prevf[:], start=True, stop=True),
                          reads=[rCT, r_prevf], writes=[r_psO], inc=False)
                    pe.op(lambda h: h.matmul(psO[:, 256:512], lhsT=CT[:, csl], rhs=prevb[:, c, :], start=True, stop=True),
                          reads=[rCT, r_prevb], writes=[r_psO])
                    emit_state(c, 0, g, *st["xw"])

                def p2_C(st, yT, ryT):
                    c, csl = st["c"], st["csl"]
                    pY, rpY = st["pY"]
                    t3, rt3 = st["t3"]
                    t1, rt1 = t1r.next()
                    t2, rt2 = t2r.next()
                    dve.op(lambda h: h.tensor_tensor(out=v4(t1[:]), in0=v4(psO[:, 0:256]),
                                                     in1=hb(eQ[:, c, 4 * g:4 * g + 4]), op=ALU.mult),
                           reads=[r_psO, r_eQ], writes=[rt1])
                    dve.op(lambda h: h.tensor_tensor(out=v4(t2[:]), in0=v4(psO[:, 256:512]),
                                                     in1=hb(eQ[:, c, NH + 4 * g:NH + 4 * g + 4]), op=ALU.mult),
                           reads=[r_psO, r_eQ], writes=[rt2])
                    dve.op(lambda h: h.tensor_tensor(out=t2[:], in0=t2[:], in1=t3[:], op=ALU.add),
                           reads=[rt2, rt3], writes=[rt2])
                    dve.op(lambda h: h.tensor_tensor(out=t1[:], in0=t1[:], in1=pY[:, 0:256], op=ALU.add),
                           reads=[rt1, rpY], writes=[rt1])
                    dve.op(lambda h: h.tensor_tensor(out=t1[:], in0=t1[:], in1=t2[:], op=ALU.add),
                           reads=[rt1, rt2], writes=[rt1])
                    dve.op(lambda h: h.tensor_tensor(out=t1[:], in0=t1[:], in1=Gz[:, c, :], op=ALU.mult),
                           reads=[rt1, rGz], writes=[rt1])
                    sq, rsq = sqr.next()
                    ss, rss = ssr.next()
                    act.op(lambda h: h.activation(out=sq[:], in_=t1[:], func=AF.Square, accum_out=ss[:, 0:1]),
                           reads=[rt1], writes=[rsq, rss])
                    act.op(lambda h: h.activation(out=ss[:, 1:2], in_=ss[:, 0:1], func=AF.Ln, scale=1.0 / 256.0, bias=RMS_EPS),
                           reads=[rss], writes=[rss])
                    act.op(lambda h: h.activation(out=ss[:, 1:2], in_=ss[:, 1:2], func=AF.Exp, scale=-0.5),
                           reads=[rss], writes=[rss])
                    st["c2"] = (t1, rt1, ss, rss)

                def p2_D(st, yT, ryT):
                    c, csl = st["c"], st["csl"]
                    t1, rt1, ss, rss = st["c2"]
                    yn, ryn = ynr.next()
                    dve.op(lambda h: h.scalar_tensor_tensor(
                        out=yn[:], in0=t1[:], scalar=ss[:, 1:2], in1=nwb[:, g * 256:(g + 1) * 256], op0=ALU.mult, op1=ALU.mult),
                        reads=[rt1, rss, r_nwb], writes=[ryn])
                    for half in range(2):
                        pe.op(lambda h, half=half: h.transpose(
                            out=pstA[:, 512 + half * 128:512 + (half + 1) * 128], in_=yn[:, half * 128:(half + 1) * 128], identity=identb[:]),
                            reads=[ryn, r_identb], writes=[r_pstA], inc=(half == 1))
                    act.op(lambda h: h.activation(
                        out=yT[:, :, csl], in_=pstA[:, 512:768].rearrange("p (a b) -> p a b", a=2), func=AF.Copy),
                        reads=[r_pstA], writes=[ryT])

                dve.op(lambda h: h.memset(carry[:], 0.0), writes=[r_carry])
                yT, ryT = yTs.next()
                stn = p2_A(0)
                stp = None
                for c in range(NT):
                    stc = stn
                    if c + 1 < NT:
                        stn = p2_A(c + 1)
                    p2_B(stc)
                    p2_C(stc, yT, ryT)
                    if stp is not None:
                        p2_D(stp, yT, ryT)
                    stp = stc
                p2_D(stp, yT, ryT)
                sp.dma_multi([(YT[2 * g + half], yT[:, half, :]) for half in range(2)], reads=[ryT], semres=ryT)
            stage_end(es, mark)

        def gemm_tm(es, src_T, wsrc, ncols, dst, tag):
            Wring = Ring(Kx, es, "W%s_" % tag, [128, KC, 512], BF16, 2)
            aT, r_aT = one(Kx, es, "aT" + tag, [128, KC, TB], BF16)
            r_aTg = [Kx.res("aTg%s_%d" % (tag, i)) for i in range((KC + 7) // 8)]
            stz = Ring(Kx, es, "st" + tag, [128, 512], F32, 4)
            psacc = Ring(Kx, es, "ps" + tag, [128, 512], F32, 6, psum=True)
            ev = 0
            for b in range(T // TB):
                sv = src_T.rearrange("k p t -> p k t")
                for gi, k0 in enumerate(range(0, KC, 8)):
                    sp.dma(aT[:, k0:k0 + 8, :], sv[:, k0:k0 + 8, b * TB:(b + 1) * TB], writes=[r_aTg[gi]], semres=r_aTg[gi])
                nsl = ncols // 512
                nxt = load_w_slab(Wring, wsrc, 0, 512)
                for s in range(nsl):
                    Wt, rW = nxt
                    if s + 1 < nsl:
                        nxt = load_w_slab(Wring, wsrc, (s + 1) * 512, 512)
                    for tt in range(TBt):
                        pa, rpa = psacc.next()
                        for kc in range(KC):
                            pe.op(lambda h, pa=pa, tt=tt, kc=kc, Wt=Wt: h.matmul(
                                pa[:], lhsT=aT[:, kc, tt * 128:(tt + 1) * 128], rhs=Wt[:, kc, :],
                                start=(kc == 0), stop=(kc == KC - 1)), reads=[r_aTg[kc // 8], rW], writes=[rpa], inc=(kc == KC - 1))
                        sz, rsz = stz.next()
                        evac_copy(ev, sz[:], pa[:], [rpa], [rsz])
                        ev += 1
                        r0 = b * TB + tt * 128
                        sp.dma(dst[r0:r0 + 128, s * 512:(s + 1) * 512], sz[:], reads=[rsz], semres=rsz)

        if stop_after >= 6:
          with ExitStack() as es:
            mark = len(Kx.all_res)
            gemm_tm(es, YT, w_out, D, MIX, "4")
            stage_end(es, mark)

        def ln_stage(es, branch, resid, gate_idx, g_src, b_src, dst, with_h2, tag):
            gbc, r_gbc = one(Kx, es, "gbc" + tag, [128, D], F32)
            lg, r_lg = one(Kx, es, "lg" + tag, [128, D], F32)
            lb, r_lb = one(Kx, es, "lb" + tag, [128, D], F32)
            sp.dma(gbc[:], BCS[gate_idx], writes=[r_gbc], semres=r_gbc)
            sp.dma(lg[:], g_src[:, :], writes=[r_lg], semres=r_lg)
            sp.dma(lb[:], b_src[:, :], writes=[r_lb], semres=r_lb)
            if with_h2:
                G2, r_G2 = one(Kx, es, "G2" + tag, [128, D], F32)
                B2, r_B2 = one(Kx, es, "B2" + tag, [128, D], F32)
                sp.dma(G2[:], BCS[1], writes=[r_G2], semres=r_G2)
                sp.dma(B2[:], BCS[2], writes=[r_B2], semres=r_B2)
                h2r = Ring(Kx, es, "h2" + tag, [128, D], BF16, 1)
                h2s = Ring(Kx, es, "h2s" + tag, [128, KC, 256], BF16, 1)
                pst = Ring(Kx, es, "pst" + tag, [128, 1024], BF16, 4, psum=True)
            nring = 2 if with_h2 else 3
            mr = Ring(Kx, es, "mr" + tag, [128, D], F32, nring)
            xr = Ring(Kx, es, "xr" + tag, [128, D], F32, nring)
            str_ = Ring(Kx, es, "bs" + tag, [128, D // 512, 6], F32, 3)
            mvr = Ring(Kx, es, "mv" + tag, [128, 4], F32, 3)
            ev = [0]
            hsb = [None]

            def ln_load(tt):
                rows = slice(tt * 128, (tt + 1) * 128)
                m, rm = mr.next()
                xx, rxx = xr.next()
                sp.dma(m[:], branch[rows, :], writes=[rm], semres=rm)
                sp.dma(xx[:], resid[rows, :], writes=[rxx], semres=rxx)
                return (m, rm, xx, rxx)

            def ln_A(tt, ld):
                m, rm, xx, rxx = ld
                dve.op(lambda h: h.tensor_tensor(out=m[:], in0=m[:], in1=gbc[:], op=ALU.mult), reads=[rm, r_gbc], writes=[rm])
                dve.op(lambda h: h.scalar_tensor_tensor(out=xx[:], in0=xx[:], scalar=alpha, in1=m[:], op0=ALU.mult, op1=ALU.add),
                       reads=[rxx, rm], writes=[rxx])
                st, rst = str_.next()
                for q in range(D // 512):
                    dve.op(lambda h, q=q: h.bn_stats(out=st[:, q, :], in_=xx[:, q * 512:(q + 1) * 512]),
                           reads=[rxx], writes=[rst])
                mv, rmv = mvr.next()
                dve.op(lambda h: h.bn_aggr(out=mv[:, 0:2], in_=st[:].rearrange("p a b -> p (a b)")),
                       reads=[rst], writes=[rmv])
                act.op(lambda h: h.activation(out=mv[:, 2:3], in_=mv[:, 1:2], func=AF.Ln, bias=LN_EPS), reads=[rmv], writes=[rmv])
                act.op(lambda h: h.activation(out=mv[:, 2:3], in_=mv[:, 2:3], func=AF.Exp, scale=-0.5), reads=[rmv], writes=[rmv])
                return (mv, rmv)

            def ln_B(tt, ld, mvv):
                rows = slice(tt * 128, (tt + 1) * 128)
                m, rm, xx, rxx = ld
                mv, rmv = mvv
                dve.op(lambda h: h.scalar_tensor_tensor(out=mv[:, 3:4], in0=mv[:, 0:1], scalar=-1.0, in1=mv[:, 2:3],
                                                        op0=ALU.mult, op1=ALU.mult), reads=[rmv], writes=[rmv])
                act.op(lambda h: h.activation(out=xx[:], in_=xx[:], func=AF.Identity, bias=mv[:, 3:4], scale=mv[:, 2:3]),
                       reads=[rxx, rmv], writes=[rxx])
                dve.op(lambda h: h.tensor_tensor(out=m[:], in0=xx[:], in1=lg[:], op=ALU.mult), reads=[rxx, r_lg], writes=[rm])
                dve.op(lambda h: h.tensor_tensor(out=m[:], in0=m[:], in1=lb[:], op=ALU.add), reads=[rm, r_lb], writes=[rm])
                sp.dma(dst[rows, :], m[:], reads=[rm], semres=rm)
                if with_h2:
                    h2, rh2 = h2r.next()
                    dve.op(lambda h: h.tensor_tensor(out=xx[:], in0=xx[:], in1=G2[:], op=ALU.mult), reads=[rxx, r_G2], writes=[rxx])
                    dve.op(lambda h: h.tensor_tensor(out=h2[:], in0=xx[:], in1=B2[:], op=ALU.add), reads=[rxx, r_B2], writes=[rh2])
                    if tt % 2 == 0:
                        hsb[0] = h2s.next()
                    hs, rhs = hsb[0]
                    for q8 in range(KC // 8):
                        pt, rpt = pst.next()
                        for q in range(8):
                            kc = q8 * 8 + q
                            pe.op(lambda h, q=q, kc=kc: h.transpose(
                                out=pt[:, q * 128:(q + 1) * 128], in_=h2[:, kc * 128:(kc + 1) * 128], identity=identb[:]),
                                reads=[rh2, r_identb], writes=[rpt], inc=(q == 7))
                        o_ap = hs[:, q8 * 8:(q8 + 1) * 8, (tt % 2) * 128:(tt % 2 + 1) * 128]
                        i_ap = pt[:].rearrange("p (a b) -> p a b", a=8)
                        evac_copy(ev[0], o_ap, i_ap, [rpt], [rhs])
                        ev[0] += 1
                    if tt % 2 == 1:
                        t0 = (tt - 1) * 128
                        hv = H2T.rearrange("k p t -> p k t")
                        sp.dma_multi([(hv[:, k0:k0 + 8, t0:t0 + 256], hs[:, k0:k0 + 8, :]) for k0 in range(0, KC, 8)],
                                     reads=[rhs], semres=rhs)

            lds = [ln_load(0)]
            pend = None
            for tt in range(NT):
                if nring >= 3 and tt + 1 < NT:
                    lds.append(ln_load(tt + 1))
                mvv = ln_A(tt, lds[tt])
                if pend is not None:
                    ln_B(*pend)
                if nring < 3 and tt + 1 < NT:
                    lds.append(ln_load(tt + 1))
                pend = (tt, lds[tt], mvv)
            ln_B(*pend)

        if stop_after >= 7:
          with ExitStack() as es:
            mark = len(Kx.all_res)
            ln_stage(es, MIX, x_in, 0, ln1g_bc, ln1b_bc, X1, True, "5")
            stage_end(es, mark)

        if stop_after >= 8:
          with ExitStack() as es:
            mark = len(Kx.all_res)
            Wring = Ring(Kx, es, "W6_", [128, KC, 512], BF16, 2)
            aT, r_aT = one(Kx, es, "aT6", [128, KC, TB], BF16)
            r_aTg = [Kx.res("aTg6_%d" % i) for i in range((KC + 7) // 8)]
            rr = Ring(Kx, es, "rl6", [128, 512], F32, 3)
            us = Ring(Kx, es, "us6", [128, TB], BF16, 3)
            psacc = Ring(Kx, es, "ps6", [128, 512], F32, 6, psum=True)
            for b in range(T // TB):
                sv = H2T.rearrange("k p t -> p k t")
                for gi, k0 in enumerate(range(0, KC, 8)):
                    sp.dma(aT[:, k0:k0 + 8, :], sv[:, k0:k0 + 8, b * TB:(b + 1) * TB], writes=[r_aTg[gi]], semres=r_aTg[gi])
                nsl = DFF // 512
                nxt = load_w_slab(Wring, w_up, 0, 512)
                for s in range(nsl):
                    Wt, rW = nxt
                    if s + 1 < nsl:
                        nxt = load_w_slab(Wring, w_up, (s + 1) * 512, 512)
                    for ct in range(4):
                        u, ru = us.next()
                        for sub in range(TB // 512):
                            pa, rpa = psacc.next()
                            for kc in range(KC):
                                pe.op(lambda h, pa=pa, kc=kc, Wt=Wt, ct=ct, sub=sub: h.matmul(
                                    pa[:], lhsT=Wt[:, kc, ct * 128:(ct + 1) * 128], rhs=aT[:, kc, sub * 512:(sub + 1) * 512],
                                    start=(kc == 0), stop=(kc == KC - 1)), reads=[r_aTg[kc // 8], rW], writes=[rpa], inc=(kc == KC - 1))
                            r_, rr_ = rr.next()
                            act.op(lambda h, r_=r_, pa=pa: h.activation(out=r_[:], in_=pa[:], func=AF.Relu), reads=[rpa], writes=[rr_])
                            dve.op(lambda h, r_=r_, u=u, sub=sub: h.tensor_tensor(out=u[:, sub * 512:(sub + 1) * 512], in0=r_[:], in1=r_[:], op=ALU.mult),
                                   reads=[rr_], writes=[ru])
                        sp.dma(UT[4 * s + ct, :, b * TB:(b + 1) * TB], u[:], reads=[ru], semres=ru)
            stage_end(es, mark)

        if stop_after >= 9:
          with ExitStack() as es:
            mark = len(Kx.all_res)
            FCG = 8
            uT, r_uT = one(Kx, es, "uT7", [128, FC, 512], BF16)
            r_uTg = [Kx.res("uTg7_%d" % i) for i in range((FC + 7) // 8)]
            Wd = Ring(Kx, es, "Wd7", [128, 2, FCG, 512], BF16, 2)
            stz = Ring(Kx, es, "st7", [128, 512], F32, 4)
            ps8 = [one(Kx, es, "ps7_%d" % i, [128, 512], F32, psum=True) for i in range(8)]
            wdv = w_down.rearrange("(fc p) n -> p fc n", p=128)

            def load_wd(sp_i, fcg):
                W_, rW_ = Wd.next()
                pool.dma_multi([(W_[:, s, :, :], wdv[:, fcg * FCG:(fcg + 1) * FCG, (2 * sp_i + s) * 512:(2 * sp_i + s + 1) * 512])
                                for s in range(2)], writes=[rW_], semres=rW_)
                return W_, rW_

            ev = 0
            for b in range(T // 512):
                sv = UT.rearrange("f p t -> p f t")
                for gi, k0 in enumerate(range(0, FC, 8)):
                    sp.dma(uT[:, k0:k0 + 8, :], sv[:, k0:k0 + 8, b * 512:(b + 1) * 512], writes=[r_uTg[gi]], semres=r_uTg[gi])
                seq = [(spi, fcg) for spi in range(D // 1024) for fcg in range(FC // FCG)]
                nxt = load_wd(*seq[0])
                for qi, (spi, fcg) in enumerate(seq):
                    W_, rW_ = nxt
                    if qi + 1 < len(seq):
                        nxt = load_wd(*seq[qi + 1])
                    for fcl in range(FCG):
                        fc = fcg * FCG + fcl
                        for tt in range(4):
                            for s in range(2):
                                pa, rpa = ps8[tt * 2 + s]
                                pe.op(lambda h, pa=pa, fc=fc, tt=tt, s=s, fcl=fcl, W_=W_: h.matmul(
                                    pa[:], lhsT=uT[:, fc, tt * 128:(tt + 1) * 128], rhs=W_[:, s, fcl, :],
                                    start=(fc == 0), stop=(fc == FC - 1)), reads=[r_uTg[fc // 8], rW_], writes=[rpa],
                                    inc=(fc == FC - 1) or (fcl == FCG - 1 and tt == 3 and s == 1))
                    if fcg == FC // FCG - 1:
                        for tt in range(4):
                            for s in range(2):
                                pa, rpa = ps8[tt * 2 + s]
                                sz, rsz = stz.next()
                                evac_copy(ev, sz[:], pa[:], [rpa], [rsz])
                                ev += 1
                                r0 = b * 512 + tt * 128
                                c0 = (2 * spi + s) * 512
                                sp.dma(FFs[r0:r0 + 128, c0:c0 + 512], sz[:], reads=[rsz], semres=rsz)
            stage_end(es, mark)

        if stop_after >= 10:
          with ExitStack() as es:
            mark = len(Kx.all_res)
            ln_stage(es, FFs, X1, 3, ln2g_bc, ln2b_bc, out, False, "8")
            stage_end(es, mark)
        Kx.barrier()
    return nc


def make_consts():
    r = np.arange(128)[:, None]
    c = np.arange(128)[None, :]
    m = np.stack([(r == c), np.ones((128, 128), bool), (r <= c), (r >= c), (r > c), (r < c)], axis=1)
    return np.ascontiguousarray(m.astype(np.float32))


def fm(v, nchunk):
    return np.ascontiguousarray(np.asarray(v).reshape(nchunk, 128).T)


def bc(v):
    v = np.asarray(v, dtype=np.float32).reshape(1, -1)
    return np.ascontiguousarray(np.broadcast_to(v, (128, v.shape[1])))


def prep_inputs(cfg, inp, n_batch):
    D, T, KC, NG, NSC, NH = cfg.D, cfg.T, cfg.KC, cfg.NG, cfg.NSC, cfg.NH
    DSSM, DXBC = cfg.DSSM, cfg.DXBC
    f32 = lambda a: np.ascontiguousarray(np.asarray(a, dtype=np.float32))
    x = np.asarray(inp["x"])
    w_in = f32(inp["w_in"][0])
    dt0 = DSSM + DXBC
    wdt_e = f32(w_in[:, dt0:dt0 + 2 * NH])
    wdt_o = f32(np.concatenate([w_in[:, dt0 + NH:dt0 + 2 * NH], w_in[:, dt0:dt0 + NH]], axis=1))
    shared = {
        "w_ada": f32(inp["w_ada"][0]), "b_ada_fm": fm(inp["b_ada"][0], 6 * KC), "w_in": w_in,
        "cb_fm": fm(inp["ssm_conv_b"][0], DXBC // 128),
        "dsk_bc": bc(np.repeat(np.asarray(inp["ssm_d"][0]), 64)), "nw_bc": bc(inp["ssm_norm_w"][0]),
        "scn_fm": fm(inp["sc_norm_w"][0], NSC), "w_out": f32(inp["w_out"][0]),
        "ln1g_bc": bc(inp["ln1_g"][0]), "ln1b_bc": bc(inp["ln1_b"][0]),
        "w_up": f32(inp["w_up"][0]), "w_down": f32(inp["w_down"][0]),
        "ln2g_bc": bc(inp["ln2_g"][0]), "ln2b_bc": bc(inp["ln2_b"][0]), "consts": make_consts(),
    }
    cwv = np.asarray(inp["ssm_conv_w"][0])
    scwv = np.asarray(inp["sc_conv_w"][0])
    par = []
    for odd in (0, 1):
        cw_ = cwv[::-1] if odd else cwv
        sc_ = scwv[::-1] if odd else scwv
        f, b_ = ("b", "f") if odd else ("f", "b")
        par.append({
            "w_dt": wdt_o if odd else wdt_e,
            "cw_fm": np.ascontiguousarray(cw_.T.reshape(DXBC // 128, 128, 5).transpose(1, 0, 2).astype(np.float32)),
            "scw_fm": np.ascontiguousarray(sc_.T.reshape(NSC, 128, 3).transpose(1, 0, 2).astype(np.float32)),
            "dtb_bc": bc(np.concatenate([inp["ssm_dt_bias_" + f][0], inp["ssm_dt_bias_" + b_][0]])),
            "alog_bc": bc(np.concatenate([inp["ssm_a_log_" + f][0], inp["ssm_a_log_" + b_][0]])),
        })
    maps = []
    for core in range(2 * n_batch):
        b, odd = core // 2, core % 2
        xl = x[b, ::-1] if odd else x[b]
        m = dict(shared)
        m.update(par[odd])
        m["x"] = f32(xl)
        m["c_fm"] = fm(inp["c"][b], KC)
        maps.append(m)
    return maps


def assemble(cfg, results, n_batch):
    T, D = cfg.T, cfg.D
    o = np.empty((n_batch, 2 * T, D), np.float32)
    for core in range(2 * n_batch):
        b, odd = core // 2, core % 2
        r = np.asarray(results[core]["out"])
        if odd:
            o[b, T:] = r[::-1]
        else:
            o[b, :T] = r
    return o


_NC_CACHE = {}


def kernel(**inputs):
    cfg = FULL
    if "nc" not in _NC_CACHE:
        _NC_CACHE["nc"] = build(cfg)
    nc = _NC_CACHE["nc"]
    maps = prep_inputs(cfg, inputs, 4)
    res = run_bass_kernel_spmd(nc, maps, core_ids=list(range(8)))
    return assemble(cfg, res.results, 4)
```

```python
import numpy as np
from contextlib import ExitStack
import concourse.bass as bass
import concourse.mybir as mybir
from concourse.bass_utils import run_bass_kernel_spmd

F32 = mybir.dt.float32
BF16 = mybir.dt.bfloat16
AF = mybir.ActivationFunctionType
ALU = mybir.AluOpType

LN_EPS = 1e-5
RMS_EPS = 1e-5


class Cfg:
    def __init__(s, D, T, NG, NSC, DFF, alpha):
        s.D, s.T, s.NG, s.NSC, s.DFF, s.alpha = D, T, NG, NSC, DFF, alpha
        s.KC = D // 128
        s.DSSM = NG * 256
        s.NH = NG * 4
        s.DSC = NSC * 128
        s.DXBC = s.DSSM + 2 * NG * 128
        s.DIN = s.DSSM + s.DXBC + 2 * s.NH + 3 * s.DSC
        s.FC = DFF // 128
        s.NT = T // 128
        s.TB = min(1024, T)
        s.NFM = (s.DXBC + 3 * s.DSC) // 128
        s.NXO = (s.DSSM + NG * 128) // 128
        assert s.DSSM + s.DSC == D


FULL = Cfg(4096, 2048, 8, 16, 16384, 2.0 ** 0.25)


class Res:
    __slots__ = ("name", "w", "readers", "sem", "semcnt", "sw", "psum")

    def __init__(self, name):
        self.name = name
        self.w = None
        self.readers = {}
        self.sem = None
        self.semcnt = 0
        self.sw = False
        self.psum = False


class Eng:
    def __init__(self, Kx, h, name, is_pe=False):
        self.K, self.h, self.name, self.is_pe = Kx, h, name, is_pe
        self.sem = Kx.new_sem("e_" + name)
        self.cnt = 0
        self.seen = {}

    def _wait(self, tok, war):
        if tok is None:
            return
        sem, val, en = tok
        if en == self.name and (self.is_pe or war):
            return
        key = id(sem)
        if self.seen.get(key, 0) >= val:
            return
        self.h.wait_ge(sem, val)
        self.seen[key] = val

    def deps(self, reads, writes):
        for r in reads:
            self._wait(r.w, False)
            if r.psum:
                for t in r.readers.values():
                    self._wait(t, True)
        for w in writes:
            self._wait(w.w, False)
            for t in w.readers.values():
                self._wait(t, True)

    def op(self, fn, reads=(), writes=(), inc=True):
        self.deps(reads, writes)
        ins = fn(self.h)
        if inc:
            ins.then_inc(self.sem, 1)
            self.cnt += 1
            tok = (self.sem, self.cnt, self.name)
        else:
            tok = (self.sem, self.cnt + 1, self.name)
        for r in reads:
            r.readers[id(tok[0])] = tok
        for w in writes:
            w.w = tok
            w.readers = {}
        return ins

    def dma(self, out, in_, reads=(), writes=(), semres=None):
        self.dma_multi([(out, in_)], reads, writes, semres)

    def dma_multi(self, pairs, reads=(), writes=(), semres=None):
        self.deps(reads, writes)
        if semres.sem is None:
            semres.sw = (self.name == "pool")
            semres.sem, semres.semcnt = self.K.take_dma_sem("d_" + semres.name, semres.sw)
        assert semres.sw == (self.name == "pool"), semres.name
        for out, in_ in pairs:
            self.h.dma_start(out=out, in_=in_).then_inc(semres.sem, 16)
            semres.semcnt += 16
        tok = (semres.sem, semres.semcnt, "dma")
        for r in reads:
            r.readers[id(tok[0])] = tok
        for w in writes:
            w.w = tok
            w.readers = {}


class Kctx:
    def __init__(self, nc, es):
        self.nc, self.es = nc, es
        self.nsem = 0
        self.all_res = []
        self.sem_pool = []
        self.sem_pool_sw = []

    def new_sem(self, name):
        self.nsem += 1
        return self.es.enter_context(self.nc.semaphore("%s_%d" % (name[:20], self.nsem)))

    def take_dma_sem(self, name, sw):
        pool = self.sem_pool_sw if sw else self.sem_pool
        if pool:
            return pool.pop()
        return self.new_sem(name), 0

    def recycle(self, mark):
        for r in self.all_res[mark:]:
            if r.sem is not None:
                (self.sem_pool_sw if r.sw else self.sem_pool).append((r.sem, r.semcnt))
        del self.all_res[mark:]

    def res(self, name):
        r = Res(name)
        self.all_res.append(r)
        return r

    def engines(self):
        return [self.pe, self.act, self.dve, self.pool, self.sp]

    def barrier(self):
        toks = [(e.sem, e.cnt, "x") for e in self.engines() if e.cnt > 0]
        for r in self.all_res:
            if r.sem is not None and r.semcnt > 0:
                toks.append((r.sem, r.semcnt, "x"))
        for e in self.engines():
            for t in toks:
                if t[0] is e.sem and e.is_pe:
                    continue
                e._wait(t, False)
        for r in self.all_res:
            r.w = None
            r.readers = {}


class Ring:
    def __init__(self, Kx, es, name, shape, dtype, n, psum=False):
        self.tiles = []
        for i in range(n):
            nm = "%s%d" % (name, i)
            t = es.enter_context((Kx.nc.psum_tensor if psum else Kx.nc.sbuf_tensor)(nm, shape, dtype))
            rr = Kx.res(nm)
            rr.psum = psum
            self.tiles.append((t, rr))
        self.i = 0

    def next(self):
        t = self.tiles[self.i % len(self.tiles)]
        self.i += 1
        return t


def one(Kx, es, name, shape, dtype, psum=False):
    t = es.enter_context((Kx.nc.psum_tensor if psum else Kx.nc.sbuf_tensor)(name, shape, dtype))
    rr = Kx.res(name)
    rr.psum = psum
    return t, rr


def build(cfg, debug=False, stop_after=99):
    nc = bass.Bass("TRN2", target_bir_lowering=False)
    D, T, KC, NG, NSC, NH, FC, NT, TB = cfg.D, cfg.T, cfg.KC, cfg.NG, cfg.NSC, cfg.NH, cfg.FC, cfg.NT, cfg.TB
    DSSM, DSC, DXBC, DIN, DFF = cfg.DSSM, cfg.DSC, cfg.DXBC, cfg.DIN, cfg.DFF
    NFM, NXO = cfg.NFM, cfg.NXO
    LT = 2 * T
    TH = T + 128
    H2 = 2 * NH
    NTT = 2 * NT
    TBt = TB // 128
    NXT = DXBC // 128
    SSMT = DSSM // 128
    alpha = float(cfg.alpha)

    def din(name, shape, dt=F32):
        return nc.dram_tensor(name, shape, dt, kind="ExternalInput").ap()

    def dscr(name, shape, dt=F32):
        return nc.dram_tensor(name, shape, dt, kind=("ExternalOutput" if debug else "Internal")).ap()

    x_in = din("x", [LT, D])
    c_fm = din("c_fm", [128, KC])
    w_ada = din("w_ada", [D, 6 * D])
    b_ada_fm = din("b_ada_fm", [128, 6 * KC])
    w_in = din("w_in", [D, DIN])
    w_dt = din("w_dt", [D, H2])
    cw_fm = din("cw_fm", [128, NXT, 5])
    cb_fm = din("cb_fm", [128, NXT])
    dtb_bc = din("dtb_bc", [128, H2])
    alog_bc = din("alog_bc", [128, H2])
    dsk_bc = din("dsk_bc", [128, DSSM])
    nw_bc = din("nw_bc", [128, DSSM])
    scw_fm = din("scw_fm", [128, NSC, 3])
    scn_fm = din("scn_fm", [128, NSC])
    w_out = din("w_out", [D, D])
    ln1g_bc = din("ln1g_bc", [128, D])
    ln1b_bc = din("ln1b_bc", [128, D])
    w_up = din("w_up", [D, DFF])
    w_down = din("w_down", [DFF, D])
    ln2g_bc = din("ln2g_bc", [128, D])
    ln2b_bc = din("ln2b_bc", [128, D])
    consts = din("consts", [128, 6, 128])
    out = nc.dram_tensor("out", [T, D], F32, kind="ExternalOutput").ap()

    PTm = dscr("PTm", [NFM, 128, TH])
    PTo = dscr("PTo", [NXO, 128, T])
    Ztm = dscr("Ztm", [T, DSSM])
    GZ = dscr("GZ", [T, DSSM])
    XBC = dscr("XBC", [NXT, 128, T], BF16)
    XBO = dscr("XBO", [NXO, 128, T], BF16)
    YT = dscr("YT", [KC, 128, T], BF16)
    MIX = dscr("MIX", [T, D])
    X1 = dscr("X1", [T, D])
    H2T = dscr("H2T", [KC, 128, T], BF16)
    UT = dscr("UT", [FC, 128, T], BF16)
    FFs = dscr("FF", [T, D])
    BCS = dscr("BCS", [4, 128, D])
    DBG = dscr("DBG", [128, 8 * NTT * H2]) if debug else None

    with ExitStack() as ges:
        Kx = Kctx(nc, ges)
        Kx.pe = Eng(Kx, nc.tensor, "pe", is_pe=True)
        Kx.act = Eng(Kx, nc.scalar, "act")
        Kx.dve = Eng(Kx, nc.vector, "dve")
        Kx.pool = Eng(Kx, nc.gpsimd, "pool")
        Kx.sp = Eng(Kx, nc.sync, "sp")
        pe, act, dve, pool, sp = Kx.pe, Kx.act, Kx.dve, Kx.pool, Kx.sp

        def stage_end(es, mark):
            Kx.barrier()
            Kx.recycle(mark)
            es.close()

        cst, r_cst = one(Kx, ges, "cst", [128, 6, 128], F32)
        identb, r_identb = one(Kx, ges, "identb", [128, 128], BF16)
        onesdiv, r_onesdiv = one(Kx, ges, "onesdiv", [128, 128], F32)
        modfm, r_mod = one(Kx, ges, "modfm", [128, 6 * KC], F32)
        sc1p, r_sc1p = one(Kx, ges, "sc1p", [128, KC], F32)
        dtraw, r_dtraw = one(Kx, ges, "dtraw", [128, NTT, H2], F32)
        scb, r_scb = one(Kx, ges, "scb", [128, KC], BF16)
        IDENT, ONES, LE, GE, GT, LTm = (cst[:, i, :] for i in range(6))

        sp.dma(cst[:], consts[:, :, :], writes=[r_cst], semres=r_cst)
        dve.op(lambda h: h.tensor_copy(out=identb[:], in_=cst[:, 0, :]), reads=[r_cst], writes=[r_identb])
        dve.op(lambda h: h.tensor_scalar(out=onesdiv[:], in0=cst[:, 1, :], scalar1=1.0 / 128.0, scalar2=None,
                                         op0=ALU.mult), reads=[r_cst], writes=[r_onesdiv])

        def evac_copy(i, out_ap, in_ap, reads, writes):
            if i % 2 == 0:
                act.op(lambda h: h.activation(out=out_ap, in_=in_ap, func=AF.Copy), reads=reads, writes=writes)
            else:
                dve.op(lambda h: h.tensor_copy(out=out_ap, in_=in_ap), reads=reads, writes=writes)

        def load_w_slab(Wring, wsrc, col0, width):
            Wt, rW = Wring.next()
            pool.dma(Wt[:, :, 0:width], wsrc.rearrange("(kc p) n -> p kc n", p=128)[:, :, col0:col0 + width],
                     writes=[rW], semres=rW)
            return Wt, rW

        with ExitStack() as es:
            mark = len(Kx.all_res)
            Wring = Ring(Kx, es, "W0_", [128, KC, 512], BF16, 2)
            cf, r_cf = one(Kx, es, "cf", [128, KC], F32)
            bada, r_bada = one(Kx, es, "bada", [128, 6 * KC], F32)
            psm, r_psm = one(Kx, es, "psm", [128, 512], F32, psum=True)
            sp.dma(cf[:], c_fm[:, :], writes=[r_cf], semres=r_cf)
            sp.dma(bada[:], b_ada_fm[:, :], writes=[r_bada], semres=r_bada)
            act.op(lambda h: h.activation(out=scb[:], in_=cf[:], func=AF.Silu), reads=[r_cf], writes=[r_scb])
            nsl = 2 * D // 512
            nxt = load_w_slab(Wring, w_ada, 0, 512)
            for s in range(nsl):
                Wt, rW = nxt
                if s + 1 < nsl:
                    nxt = load_w_slab(Wring, w_ada, (s + 1) * 512, 512)
                for ct in range(4):
                    j = 4 * s + ct
                    for kc in range(KC):
                        pe.op(lambda h, Wt=Wt, ct=ct, kc=kc, j=j: h.matmul(
                            psm[:, j:j + 1], lhsT=Wt[:, kc, ct * 128:(ct + 1) * 128], rhs=scb[:, kc:kc + 1],
                            start=(kc == 0), stop=(kc == KC - 1)),
                            reads=[rW, r_scb], writes=[r_psm], inc=(kc == KC - 1))
            dve.op(lambda h: h.tensor_tensor(out=modfm[:, 0:2 * KC], in0=psm[:, 0:2 * KC], in1=bada[:, 0:2 * KC], op=ALU.add),
                   reads=[r_psm, r_bada], writes=[r_mod])
            dve.op(lambda h: h.tensor_scalar(out=sc1p[:], in0=modfm[:, KC:2 * KC], scalar1=1.0, scalar2=None,
                                             op0=ALU.add), reads=[r_mod], writes=[r_sc1p])
            stage_end(es, mark)

        if stop_after >= 2:
          with ExitStack() as es:
            mark = len(Kx.all_res)
            Wring = Ring(Kx, es, "W2_", [128, KC, 512], BF16, 2)
            hT, r_hT = one(Kx, es, "hT", [128, KC, TB + 128], BF16)
            xring = Ring(Kx, es, "xt", [128, D], F32, 2)
            wdt, r_wdt = one(Kx, es, "wdt", [128, KC, H2], BF16)
            stg = Ring(Kx, es, "stg", [128, TB + 128], F32, 2)
            stz = Ring(Kx, es, "stz", [128, 512], F32, 3)
            pstr = Ring(Kx, es, "pstr", [128, 512], F32, 2, psum=True)
            psacc = Ring(Kx, es, "psacc", [128, 512], F32, 4, psum=True)
            psdt = Ring(Kx, es, "psdt", [128, 512], F32, 2, psum=True)
            pool.dma(wdt[:], w_dt.rearrange("(kc p) n -> p kc n", p=128), writes=[r_wdt], semres=r_wdt)

            def slab_list(c0, c1):
                r = []
                c = c0
                while c < c1:
                    w = min(512, c1 - c)
                    r.append((c, w))
                    c += w
                return r

            sc0 = DSSM + DXBC + H2
            blocks = []
            for b in range(T // TB):
                tiles = list(range(b * TBt, (b + 1) * TBt))
                last = (b == T // TB - 1)
                if last:
                    tiles.append(NT)
                blocks.append(("main", tiles, last))
            for b in range(T // TB):
                blocks.append(("other", [NT + i for i in range(b * TBt, (b + 1) * TBt)], False))

            ev = 0
            for kind, tiles, last in blocks:
                ntok = len(tiles) * 128
                for ti, tile in enumerate(tiles):
                    xt, rx = xring.next()
                    sp.dma(xt[:], x_in[tile * 128:(tile + 1) * 128, :], writes=[rx], semres=rx)
                    for q4 in range(KC // 4):
                        pt, rpt = pstr.next()
                        for q in range(4):
                            kc = q4 * 4 + q
                            pe.op(lambda h, pt=pt, xt=xt, q=q, kc=kc: h.transpose(
                                out=pt[:, q * 128:(q + 1) * 128], in_=xt[:, kc * 128:(kc + 1) * 128],
                                identity=cst[:, 0, :]), reads=[rx, r_cst], writes=[rpt], inc=(q == 3))
                        for q in range(4):
                            kc = q4 * 4 + q
                            dst = hT[:, kc, ti * 128:(ti + 1) * 128]
                            if (ev % 2) == 0:
                                act.op(lambda h, dst=dst, pt=pt, q=q, kc=kc: h.activation(
                                    out=dst, in_=pt[:, q * 128:(q + 1) * 128], func=AF.Identity,
                                    bias=modfm[:, kc:kc + 1], scale=sc1p[:, kc:kc + 1]),
                                    reads=[rpt, r_mod, r_sc1p], writes=[r_hT])
                            else:
                                dve.op(lambda h, dst=dst, pt=pt, q=q, kc=kc: h.tensor_scalar(
                                    out=dst, in0=pt[:, q * 128:(q + 1) * 128], scalar1=sc1p[:, kc:kc + 1],
                                    scalar2=modfm[:, kc:kc + 1], op0=ALU.mult, op1=ALU.add),
                                    reads=[rpt, r_mod, r_sc1p], writes=[r_hT])
                            ev += 1
                for ti, tile in enumerate(tiles):
                    if tile >= NTT or (kind == "main" and tile == NT):
                        continue
                    pd, rpd = psdt.next()
                    for kc in range(KC):
                        pe.op(lambda h, pd=pd, ti=ti, kc=kc: h.matmul(
                            pd[:, 0:H2], lhsT=hT[:, kc, ti * 128:(ti + 1) * 128], rhs=wdt[:, kc, :],
                            start=(kc == 0), stop=(kc == KC - 1)), reads=[r_hT, r_wdt], writes=[rpd],
                            inc=(kc == KC - 1))
                    evac_copy(ev, dtraw[:, tile, :], pd[:, 0:H2], [rpd], [r_dtraw])
                    ev += 1
                if kind == "main":
                    slabs = [("tm", c, w) for c, w in slab_list(0, DSSM)]
                    slabs += [("fm", c, w) for c, w in slab_list(DSSM, DSSM + DXBC)]
                    slabs += [("fm", c, w) for c, w in slab_list(sc0, DIN)]
                else:
                    slabs = [("fm", c, w) for c, w in slab_list(DSSM, DSSM + DSSM + NG * 128)]
                nxt = load_w_slab(Wring, w_in, slabs[0][1], slabs[0][2])
                for si, (mode, c0, wd) in enumerate(slabs):
                    Wt, rW = nxt
                    if si + 1 < len(slabs):
                        nxt = load_w_slab(Wring, w_in, slabs[si + 1][1], slabs[si + 1][2])
                    if mode == "tm":
                        for ti, tile in enumerate(tiles):
                            if tile == NT:
                                continue
                            pa, rpa = psacc.next()
                            for kc in range(KC):
                                pe.op(lambda h, pa=pa, ti=ti, kc=kc, Wt=Wt, wd=wd: h.matmul(
                                    pa[:, 0:wd], lhsT=hT[:, kc, ti * 128:(ti + 1) * 128], rhs=Wt[:, kc, 0:wd],
                                    start=(kc == 0), stop=(kc == KC - 1)), reads=[r_hT, rW], writes=[rpa],
                                    inc=(kc == KC - 1))
                            sz, rsz = stz.next()
                            evac_copy(ev, sz[:, 0:wd], pa[:, 0:wd], [rpa], [rsz])
                            ev += 1
                            sp.dma(Ztm[tile * 128:(tile + 1) * 128, c0:c0 + wd], sz[:, 0:wd], reads=[rsz], semres=rsz)
                    else:
                        for ct in range(wd // 128):
                            col = c0 + ct * 128
                            fmt = (col - DSSM) // 128 if col < sc0 else NXT + (col - sc0) // 128
                            sg, rsg = stg.next()
                            n0 = 0
                            while n0 < ntok:
                                nn = min(512, ntok - n0)
                                pa, rpa = psacc.next()
                                for kc in range(KC):
                                    pe.op(lambda h, pa=pa, kc=kc, Wt=Wt, ct=ct, n0=n0, nn=nn: h.matmul(
                                        pa[:, 0:nn], lhsT=Wt[:, kc, ct * 128:(ct + 1) * 128], rhs=hT[:, kc, n0:n0 + nn],
                                        start=(kc == 0), stop=(kc == KC - 1)), reads=[r_hT, rW], writes=[rpa],
                                        inc=(kc == KC - 1))
                                evac_copy(ev, sg[:, n0:n0 + nn], pa[:, 0:nn], [rpa], [rsg])
                                ev += 1
                                n0 += nn
                            t0 = tiles[0] * 128
                            if kind == "main":
                                sp.dma(PTm[fmt, :, t0:t0 + ntok], sg[:, 0:ntok], reads=[rsg], semres=rsg)
                            else:
                                sp.dma(PTo[fmt, :, t0 - T:t0 - T + ntok], sg[:, 0:ntok], reads=[rsg], semres=rsg)
            stage_end(es, mark)

        if stop_after >= 3:
          with ExitStack() as es:
            mark = len(Kx.all_res)
            WringB = Ring(Kx, es, "W0b_", [128, KC, 512], BF16, 2)
            badaB, r_badaB = one(Kx, es, "badaB", [128, 6 * KC], F32)
            psmB, r_psmB = one(Kx, es, "psmB", [128, 512], F32, psum=True)
            sp.dma(badaB[:], b_ada_fm[:, :], writes=[r_badaB], semres=r_badaB)
            s_lo, s_hi = 2 * D // 512, 6 * D // 512
            nxt = load_w_slab(WringB, w_ada, s_lo * 512, 512)
            for s in range(s_lo, s_hi):
                Wt, rW = nxt
                if s + 1 < s_hi:
                    nxt = load_w_slab(WringB, w_ada, (s + 1) * 512, 512)
                for ct in range(4):
                    j = 4 * s + ct
                    for kc in range(KC):
                        pe.op(lambda h, Wt=Wt, ct=ct, kc=kc, j=j: h.matmul(
                            psmB[:, j:j + 1], lhsT=Wt[:, kc, ct * 128:(ct + 1) * 128], rhs=scb[:, kc:kc + 1],
                            start=(kc == 0), stop=(kc == KC - 1)),
                            reads=[rW, r_scb], writes=[r_psmB], inc=(kc == KC - 1))
            cw, r_cw = one(Kx, es, "cw", [128, NXT, 5], F32)
            cb, r_cb = one(Kx, es, "cb", [128, NXT], F32)
            sp.dma(cw[:], cw_fm[:, :, :], writes=[r_cw], semres=r_cw)
            sp.dma(cb[:], cb_fm[:, :], writes=[r_cb], semres=r_cb)
            Um = Ring(Kx, es, "Um", [128, T + 4], F32, 3)
            Uo = Ring(Kx, es, "Uo", [128, T + 4], F32, 3)
            accr = Ring(Kx, es, "cacc", [128, T], F32, 2)
            obr = Ring(Kx, es, "cob", [128, T], BF16, 3)
            zr = Ring(Kx, es, "zr", [128, DSSM], F32, 3)
            zo = Ring(Kx, es, "zo", [128, DSSM], F32, 2)
            for (u, ru) in Um.tiles:
                dve.op(lambda h, u=u: h.memset(u[:, 0:2], 0.0), writes=[ru])
            for (u, ru) in Uo.tiles:
                dve.op(lambda h, u=u: h.memset(u[:, T + 2:T + 4], 0.0), writes=[ru])

            def conv_tile(U, rU, i, dst):
                acc, racc = accr.next()
                dve.op(lambda h: h.tensor_scalar(out=acc[:], in0=U[:, 0:T], scalar1=cw[:, i, 0:1],
                                                 scalar2=cb[:, i:i + 1], op0=ALU.mult, op1=ALU.add),
                       reads=[rU, r_cw, r_cb], writes=[racc])
                for k in range(1, 5):
                    dve.op(lambda h, k=k: h.scalar_tensor_tensor(
                        out=acc[:], in0=U[:, k:k + T], scalar=cw[:, i, k:k + 1], in1=acc[:],
                        op0=ALU.mult, op1=ALU.add), reads=[rU, r_cw, racc], writes=[racc])
                ob, rob = obr.next()
                act.op(lambda h: h.activation(out=ob[:], in_=acc[:], func=AF.Silu), reads=[racc], writes=[rob])
                sp.dma(dst, ob[:], reads=[rob], semres=rob)

            def pipelined(n, load_fn, compute_fn, depth=2):
                q = [load_fn(i) for i in range(min(depth, n))]
                for i in range(n):
                    if i + depth < n:
                        q.append(load_fn(i + depth))
                    compute_fn(i, q[i])

            def ld_m(i):
                U, rU = Um.next()
                sp.dma(U[:, 2:T + 4], PTm[i, :, 0:T + 2], writes=[rU], semres=rU)
                return U, rU

            def ld_o(i):
                U, rU = Uo.next()
                sp.dma_multi([(U[:, 2:T + 2], PTo[i, :, 0:T]), (U[:, 0:2], PTm[i, :, T - 2:T])], writes=[rU], semres=rU)
                return U, rU

            def ld_z(tt):
                z, rz = zr.next()
                sp.dma(z[:], Ztm[tt * 128:(tt + 1) * 128, :], writes=[rz], semres=rz)
                return z, rz

            def do_z(tt, zz):
                z, rz = zz
                g, rg = zo.next()
                act.op(lambda h: h.activation(out=g[:], in_=z[:], func=AF.Silu), reads=[rz], writes=[rg])
                sp.dma(GZ[tt * 128:(tt + 1) * 128, :], g[:], reads=[rg], semres=rg)

            pipelined(NXT, ld_m, lambda i, u: conv_tile(u[0], u[1], i, XBC[i]))
            pipelined(NXO, ld_o, lambda i, u: conv_tile(u[0], u[1], i, XBO[i]))
            pipelined(NT, ld_z, do_z)
            dve.op(lambda h: h.tensor_tensor(out=modfm[:, 2 * KC:6 * KC], in0=psmB[:, 2 * KC:6 * KC], in1=badaB[:, 2 * KC:6 * KC], op=ALU.add),
                   reads=[r_psmB, r_badaB], writes=[r_mod])
            stage_end(es, mark)

          with ExitStack() as es:
            mark = len(Kx.all_res)
            p1, r_p1 = one(Kx, es, "p1", [128, 3, KC], F32)
            for q, src in enumerate((2, 4, 5)):
                dve.op(lambda h, q=q, src=src: h.tensor_scalar(
                    out=p1[:, q, :], in0=modfm[:, src * KC:(src + 1) * KC], scalar1=1.0, scalar2=None, op0=ALU.add),
                    reads=[r_mod], writes=[r_p1])
            dg, r_dg = one(Kx, es, "dg", [128, KC, 128], F32)
            bct, r_bct = one(Kx, es, "bct", [128, D], F32)
            lg, r_lg = one(Kx, es, "lg", [128, D], F32)
            lb, r_lb = one(Kx, es, "lb", [128, D], F32)
            sp.dma(lg[:], ln1g_bc[:, :], writes=[r_lg], semres=r_lg)
            sp.dma(lb[:], ln1b_bc[:, :], writes=[r_lb], semres=r_lb)
            psb = Ring(Kx, es, "psb", [128, 512], F32, 2, psum=True)

            def bcast_row(vec_ap):
                dve.op(lambda h: h.tensor_tensor(
                    out=dg[:], in0=cst[:, 0, :].unsqueeze(1).broadcast_to([128, KC, 128]),
                    in1=vec_ap.unsqueeze(2).broadcast_to([128, KC, 128]), op=ALU.mult),
                    reads=[r_cst, r_p1, r_mod], writes=[r_dg])
                for q in range(D // 512):
                    pb, rpb = psb.next()
                    pe.op(lambda h, pb=pb, q=q: h.matmul(pb[:], lhsT=cst[:, 1, :], rhs=dg[:, 4 * q:4 * q + 4, :],
                                                         start=True, stop=True), reads=[r_dg, r_cst], writes=[rpb])
                    evac_copy(q, bct[:, q * 512:(q + 1) * 512], pb[:], [rpb], [r_bct])

            bcast_row(p1[:, 0, :])
            sp.dma(BCS[0], bct[:], reads=[r_bct], semres=r_bct)
            bcast_row(p1[:, 1, :])
            dve.op(lambda h: h.tensor_tensor(out=lg[:], in0=lg[:], in1=bct[:], op=ALU.mult),
                   reads=[r_lg, r_bct], writes=[r_lg])
            dve.op(lambda h: h.tensor_tensor(out=lb[:], in0=lb[:], in1=bct[:], op=ALU.mult),
                   reads=[r_lb, r_bct], writes=[r_lb])
            sp.dma(BCS[1], lg[:], reads=[r_lg], semres=r_lg)
            bcast_row(modfm[:, 3 * KC:4 * KC])
            dve.op(lambda h: h.tensor_tensor(out=lb[:], in0=lb[:], in1=bct[:], op=ALU.add),
                   reads=[r_lb, r_bct], writes=[r_lb])
            sp.dma(BCS[2], lb[:], reads=[r_lb], semres=r_lb)
            bcast_row(p1[:, 2, :])
            sp.dma(BCS[3], bct[:], reads=[r_bct], semres=r_bct)
            stage_end(es, mark)

        if stop_after >= 4:
          with ExitStack() as es:
            mark = len(Kx.all_res)
            scw, r_scw = one(Kx, es, "scw", [128, NSC, 3], F32)
            scn, r_scn = one(Kx, es, "scn", [128, NSC], F32)
            sp.dma(scw[:], scw_fm[:, :, :], writes=[r_scw], semres=r_scw)
            sp.dma(scn[:], scn_fm[:, :], writes=[r_scn], semres=r_scn)
            Hr = Ring(Kx, es, "Hr", [128, T + 1], F32, 2)
            Cr = Ring(Kx, es, "Cr", [128, T + 1], F32, 2)
            Br = Ring(Kx, es, "Br", [128, T], F32, 2)
            Mr = Ring(Kx, es, "Mr", [128, T + 2], F32, 2)
            Ar = Ring(Kx, es, "Ar", [128, T], F32, 2)
            Sr = Ring(Kx, es, "Sr", [128, T], F32, 2)
            Rr = Ring(Kx, es, "Rr", [128, T], F32, 2)
            Or = Ring(Kx, es, "Or", [128, T], BF16, 2)
            pss = Ring(Kx, es, "pss", [128, 512], F32, 4, psum=True)
            for (m, rm) in Mr.tiles:
                dve.op(lambda h, m=m: h.memset(m[:, 0:1], 0.0), writes=[rm])
            def ld_sc(j):
                Hb, rH = Hr.next()
                Cb, rC = Cr.next()
                Bb, rB = Br.next()
                sp.dma(Hb[:], PTm[NXT + j, :, 0:T + 1], writes=[rH], semres=rH)
                sp.dma(Cb[:], PTm[NXT + 2 * NSC + j, :, 0:T + 1], writes=[rC], semres=rC)
                sp.dma(Bb[:], PTm[NXT + NSC + j, :, 0:T], writes=[rB], semres=rB)
                return (Hb, rH, Cb, rC, Bb, rB)

            scq = [ld_sc(0)]
            for j in range(NSC):
                if j + 1 < NSC:
                    scq.append(ld_sc(j + 1))
                Hb, rH, Cb, rC, Bb, rB = scq[j]
                M, rM = Mr.next()
                dve.op(lambda h, M=M, Cb=Cb, Hb=Hb: h.tensor_tensor(out=M[:, 1:T + 2], in0=Cb[:], in1=Hb[:], op=ALU.mult),
                       reads=[rC, rH], writes=[rM])
                A, rA = Ar.next()
                dve.op(lambda h, A=A, M=M, j=j: h.tensor_scalar(out=A[:], in0=M[:, 0:T], scalar1=scw[:, j, 0:1],
                                                                scalar2=None, op0=ALU.mult),
                       reads=[rM, r_scw], writes=[rA])
                for k in (1, 2):
                    dve.op(lambda h, A=A, M=M, j=j, k=k: h.scalar_tensor_tensor(
                        out=A[:], in0=M[:, k:k + T], scalar=scw[:, j, k:k + 1], in1=A[:], op0=ALU.mult, op1=ALU.add),
                        reads=[rM, r_scw, rA], writes=[rA])
                dve.op(lambda h, A=A, Bb=Bb: h.tensor_tensor(out=A[:], in0=A[:], in1=Bb[:], op=ALU.mult),
                       reads=[rA, rB], writes=[rA])
                S, rS = Sr.next()
                act.op(lambda h, S=S, A=A: h.activation(out=S[:], in_=A[:], func=AF.Square), reads=[rA], writes=[rS])
                R, rR = Rr.next()
                for q in range(T // 512):
                    ps, rps = pss.next()
                    pe.op(lambda h, ps=ps, S=S, q=q: h.matmul(ps[:], lhsT=onesdiv[:], rhs=S[:, q * 512:(q + 1) * 512],
                                                              start=True, stop=True), reads=[rS, r_onesdiv], writes=[rps])
                    act.op(lambda h, ps=ps, R=R, q=q: h.activation(out=R[:, q * 512:(q + 1) * 512], in_=ps[:], func=AF.Ln,
                                                                   bias=RMS_EPS), reads=[rps], writes=[rR])
                act.op(lambda h, R=R: h.activation(out=R[:], in_=R[:], func=AF.Exp, scale=-0.5), reads=[rR], writes=[rR])
                O, rO = Or.next()
                dve.op(lambda h, O=O, A=A, R=R, j=j: h.scalar_tensor_tensor(
                    out=O[:], in0=A[:], scalar=scn[:, j:j + 1], in1=R[:], op0=ALU.mult, op1=ALU.mult),
                    reads=[rA, rR, r_scn], writes=[rO])
                sp.dma(YT[SSMT + j], O[:], reads=[rO], semres=rO)
            stage_end(es, mark)

        if stop_after >= 5:
          with ExitStack() as es:
            mark = len(Kx.all_res)
            def sb(name, shape, dt=F32):
                return one(Kx, es, name, shape, dt)
            dtb, r_dtb = sb("dtb", [128, H2])
            alg, r_alg = sb("alg", [128, H2])
            dsk, r_dsk = sb("dsk", [128, DSSM])
            nwb, r_nwb = sb("nwb", [128, DSSM])
            for t_, r_, s_ in ((dtb, r_dtb, dtb_bc), (alg, r_alg, alog_bc), (dsk, r_dsk, dsk_bc), (nwb, r_nwb, nw_bc)):
                sp.dma(t_[:], s_[:, :], writes=[r_], semres=r_)
            dtv, r_dtv = sb("dtv", [128, NTT, H2])
            adt, r_adt = sb("adt", [128, NTT, H2])
            wv, r_wv = sb("wv", [128, NTT, H2])
            cdv, r_cd = sb("cdv", [128, NTT, H2])
            eQ, r_eQ = sb("eQ", [128, NTT, H2])
            psq = Ring(Kx, es, "psq", [128, 512], F32, 2, psum=True)
            pstA, r_pstA = one(Kx, es, "pstA", [128, 1024], BF16, psum=True)
            psseg = Ring(Kx, es, "psseg", [128, 512], F32, 2, psum=True)
            psYr = Ring(Kx, es, "psY", [128, 512], F32, 2, psum=True)
            psO, r_psO = one(Kx, es, "psO", [128, 512], F32, psum=True)

            tes = ExitStack()
            tmark = len(Kx.all_res)
            xb, r_xb = one(Kx, tes, "xb", [128, NTT, H2], F32)
            tA, r_tA = one(Kx, tes, "tA", [128, NTT, H2], F32)
            tB, r_tB = one(Kx, tes, "tB", [128, NTT, H2], F32)
            Qv, r_Q = one(Kx, tes, "Qv", [128, NTT, H2], F32)
            Qt, r_Qt = one(Kx, tes, "Qt", [128, NTT, H2], F32)
            bc3 = lambda ap: ap.unsqueeze(1).broadcast_to([128, NTT, H2])
            dve.op(lambda h: h.tensor_tensor(out=xb[:], in0=dtraw[:], in1=bc3(dtb[:]), op=ALU.add),
                   reads=[r_dtraw, r_dtb], writes=[r_xb])
            dve.op(lambda h: h.scalar_tensor_tensor(out=tA[:], in0=xb[:], scalar=-1.0, in1=xb[:], op0=ALU.mult, op1=ALU.min),
                   reads=[r_xb], writes=[r_tA])
            act.op(lambda h: h.activation(out=tA[:], in_=tA[:], func=AF.Exp), reads=[r_tA], writes=[r_tA])
            act.op(lambda h: h.activation(out=tA[:], in_=tA[:], func=AF.Ln, bias=1.0), reads=[r_tA], writes=[r_tA])
            dve.op(lambda h: h.scalar_tensor_tensor(out=dtv[:], in0=xb[:], scalar=0.0, in1=tA[:], op0=ALU.max, op1=ALU.add),
                   reads=[r_xb, r_tA], writes=[r_dtv])
            act.op(lambda h: h.activation(out=alg[:], in_=alg[:], func=AF.Exp), reads=[r_alg], writes=[r_alg])
            dve.op(lambda h: h.scalar_tensor_tensor(out=adt[:], in0=dtv[:], scalar=-1.0, in1=bc3(alg[:]),
                                                    op0=ALU.mult, op1=ALU.mult), reads=[r_dtv, r_alg], writes=[r_adt])
            for c in range(NTT):
                pq, rpq = psq.next()
                pe.op(lambda h, pq=pq, c=c: h.matmul(pq[:, 0:NH], lhsT=cst[:, 2, :], rhs=adt[:, c, 0:NH], start=True, stop=True),
                      reads=[r_adt, r_cst], writes=[rpq])
                pe.op(lambda h, pq=pq, c=c: h.matmul(pq[:, NH:H2], lhsT=cst[:, 3, :], rhs=adt[:, c, NH:H2], start=True, stop=True),
                      reads=[r_adt, r_cst], writes=[rpq])
                pe.op(lambda h, pq=pq, c=c: h.matmul(pq[:, H2:2 * H2], lhsT=cst[:, 1, :], rhs=adt[:, c, :], start=True, stop=True),
                      reads=[r_adt, r_cst], writes=[rpq])
                dve.op(lambda h, pq=pq, c=c: h.tensor_copy(out=Qv[:, c, :], in_=pq[:, 0:H2]), reads=[rpq], writes=[r_Q])
                act.op(lambda h, pq=pq, c=c: h.activation(out=Qt[:, c, :], in_=pq[:, H2:2 * H2], func=AF.Copy),
                       reads=[rpq], writes=[r_Qt])
            dve.op(lambda h: h.tensor_tensor(out=tB[:], in0=Qt[:], in1=Qv[:], op=ALU.subtract), reads=[r_Qt, r_Q], writes=[r_tB])
            act.op(lambda h: h.activation(out=tB[:], in_=tB[:], func=AF.Exp), reads=[r_tB], writes=[r_tB])
            dve.op(lambda h: h.tensor_tensor(out=wv[:], in0=dtv[:], in1=tB[:], op=ALU.mult), reads=[r_dtv, r_tB], writes=[r_wv])
            act.op(lambda h: h.activation(out=cdv[:], in_=Qt[:], func=AF.Exp), reads=[r_Qt], writes=[r_cd])
            act.op(lambda h: h.activation(out=eQ[:], in_=Qv[:], func=AF.Exp), reads=[r_Q], writes=[r_eQ])
            if debug:
                for qi, (t_, r_) in enumerate(((dtv, r_dtv), (adt, r_adt), (Qv, r_Q), (Qt, r_Qt), (wv, r_wv), (cdv, r_cd), (eQ, r_eQ))):
                    sp.dma(DBG[:, qi * NTT * H2:(qi + 1) * NTT * H2], t_[:].rearrange("p a b -> p (a b)"), reads=[r_], semres=r_)

            Kx.barrier()
            Kx.recycle(tmark)
            tes.close()
            Xf = Ring(Kx, es, "Xf", [128, 2, T], BF16, 2)
            Xo = Ring(Kx, es, "Xo", [128, 2, T], BF16, 1)
            BTr = Ring(Kx, es, "BTr", [128, T], BF16, 2)
            CTr = Ring(Kx, es, "CTr", [128, T], BF16, 2)
            BOr = Ring(Kx, es, "BOr", [128, T], BF16, 1)
            GZr = Ring(Kx, es, "GZr", [128, NT, 256], F32, 1)
            xtm, r_xtm = sb("xtm", [128, NTT, 256], BF16)
            btm, r_btm = sb("btm", [128, NTT, 128], BF16)
            prevb, r_prevb = sb("prevb", [128, NT, 256], BF16)
            carry, r_carry = sb("carry", [128, 256])
            ctmp, r_ctmp = sb("ctmp", [128, 256])
            prevf, r_prevf = sb("prevf", [128, 256], BF16)
            xwr = Ring(Kx, es, "xw", [128, 256], BF16, 3)
            xdr = Ring(Kx, es, "xd", [128, 256], BF16, 4)
            t3r = Ring(Kx, es, "t3", [128, 256], F32, 2)
            ls4r = Ring(Kx, es, "ls4", [128, 4, 128], F32, 2)
            dc4r = Ring(Kx, es, "dc4", [128, 4, 128], F32, 4)
            lt4r = Ring(Kx, es, "lt4", [128, 4, 128], BF16, 4)
            Sfr = Ring(Kx, es, "Sf", [128, 128], F32, 3)
            Sbr = Ring(Kx, es, "Sb", [128, 128], F32, 3)
            t1r = Ring(Kx, es, "t1", [128, 256], F32, 3)
            t2r = Ring(Kx, es, "t2", [128, 256], F32, 2)
            sqr = Ring(Kx, es, "sq", [128, 256], F32, 2)
            ssr = Ring(Kx, es, "ss", [128, 2], F32, 3)
            ynr = Ring(Kx, es, "yn", [128, 256], BF16, 2)
            yTs = Ring(Kx, es, "yTs", [128, 2, T], BF16, 1)

            def hb(ap4):
                return ap4.unsqueeze(2).broadcast_to([128, 4, 64])

            def v4(ap):
                return ap.rearrange("p (h q) -> p h q", h=4)

            def load_main(g):
                X, rX = Xf.next()
                BT, rBT = BTr.next()
                CT, rCT = CTr.next()
                sp.dma_multi([(X[:, half, :], XBC[2 * g + half]) for half in range(2)], writes=[rX], semres=rX)
                sp.dma(BT[:], XBC[SSMT + g], writes=[rBT], semres=rBT)
                sp.dma(CT[:], XBC[SSMT + NG + g], writes=[rCT], semres=rCT)
                return (X, rX, BT, rBT, CT, rCT)

            def load_other(g):
                XO, rXO = Xo.next()
                BO, rBO = BOr.next()
                sp.dma_multi([(XO[:, half, :], XBO[2 * g + half]) for half in range(2)], writes=[rXO], semres=rXO)
                sp.dma(BO[:], XBO[SSMT + g], writes=[rBO], semres=rBO)
                return (XO, rXO, BO, rBO)

            def load_gz(g):
                Gz, rGz = GZr.next()
                sp.dma(Gz[:], GZ.rearrange("(c p) n -> p c n", p=128)[:, :, g * 256:(g + 1) * 256], writes=[rGz], semres=rGz)
                return (Gz, rGz)

            def emit_xw(c, colbase, g):
                xw, rxw = xwr.next()
                dve.op(lambda h: h.tensor_tensor(out=v4(xw[:]), in0=v4(xtm[:, c, :]),
                                                 in1=hb(wv[:, c, colbase + 4 * g:colbase + 4 * g + 4]), op=ALU.mult),
                       reads=[r_xtm, r_wv], writes=[rxw])
                return xw, rxw

            def emit_state(c, colbase, g, xw, rxw):
                ps, rps = psq.next()
                pe.op(lambda h: h.matmul(ps[:, 0:256], lhsT=btm[:, c, :], rhs=xw[:], start=True, stop=True),
                      reads=[r_btm, rxw], writes=[rps])
                dve.op(lambda h: h.tensor_tensor(out=v4(ctmp[:]), in0=v4(carry[:]),
                                                 in1=hb(cdv[:, c, colbase + 4 * g:colbase + 4 * g + 4]), op=ALU.mult),
                       reads=[r_carry, r_cd], writes=[r_ctmp])
                dve.op(lambda h: h.tensor_tensor(out=carry[:], in0=ctmp[:], in1=ps[:, 0:256], op=ALU.add),
                       reads=[r_ctmp, rps], writes=[r_carry])

            nxt = load_main(0)
            nxo = load_other(0)
            for g in range(NG):
                X, rX, BT, rBT, CT, rCT = nxt
                XO, rXO, BO, rBO = nxo
                if g + 1 < NG:
                    nxt = load_main(g + 1)

                def p1_front(c):
                    if c >= NT:
                        xs_, rxs_, bs_, rbs_, cc = XO, rXO, BO, rBO, c - NT
                    else:
                        xs_, rxs_, bs_, rbs_, cc = X, rX, BT, rBT, c
                    for half in range(2):
                        pe.op(lambda h, half=half: h.transpose(
                            out=pstA[:, half * 128:(half + 1) * 128], in_=xs_[:, half, cc * 128:(cc + 1) * 128],
                            identity=identb[:]), reads=[rxs_, r_identb], writes=[r_pstA], inc=False)
                    pe.op(lambda h: h.transpose(
                        out=pstA[:, 256:384], in_=bs_[:, cc * 128:(cc + 1) * 128], identity=identb[:]),
                        reads=[rbs_, r_identb], writes=[r_pstA])
                    act.op(lambda h: h.activation(out=xtm[:, c, :], in_=pstA[:, 0:256], func=AF.Copy),
                           reads=[r_pstA], writes=[r_xtm])
                    dve.op(lambda h: h.tensor_copy(out=btm[:, c, :], in_=pstA[:, 256:384]),
                           reads=[r_pstA], writes=[r_btm])
                    return emit_xw(c, NH, g)

                dve.op(lambda h: h.memset(carry[:], 0.0), writes=[r_carry])
                pend = p1_front(NTT - 1)
                for c in range(NTT - 1, -1, -1):
                    cur = pend
                    if c - 1 >= 0:
                        pend = p1_front(c - 1)
                    if c < NT:
                        act.op(lambda h, c=c: h.activation(out=prevb[:, c, :], in_=carry[:], func=AF.Copy),
                               reads=[r_carry], writes=[r_prevb])
                    emit_state(c, NH, g, *cur)
                if g + 1 < NG:
                    nxo = load_other(g + 1)
                Gz, rGz = load_gz(g)

                def p2_A(c):
                    csl = slice(c * 128, (c + 1) * 128)
                    pssc, r_pssc = psq.next()
                    pe.op(lambda h: h.matmul(pssc[:, 0:128], lhsT=BT[:, csl], rhs=CT[:, csl], start=True, stop=True),
                          reads=[rBT, rCT], writes=[r_pssc])
                    st = {"c": c, "csl": csl}
                    decs = []
                    for d in range(2):
                        c0 = d * NH + 4 * g
                        ls, rls = ls4r.next()
                        dve.op(lambda h, ls=ls, d=d, c0=c0: h.tensor_tensor(
                            out=ls[:], in0=cst[:, 4 + d, :].unsqueeze(1).broadcast_to([128, 4, 128]),
                            in1=adt[:, c, c0:c0 + 4].unsqueeze(2).broadcast_to([128, 4, 128]), op=ALU.mult),
                            reads=[r_cst, r_adt], writes=[rls])
                        pg, rpg = psseg.next()
                        for hh in range(4):
                            pe.op(lambda h, pg=pg, ls=ls, d=d, hh=hh: h.matmul(
                                pg[:, hh * 128:(hh + 1) * 128], lhsT=ls[:, hh, :], rhs=cst[:, 2 + d, :], start=True, stop=True),
                                reads=[rls, r_cst], writes=[rpg], inc=(hh == 3))
                        dc, rdc = dc4r.next()
                        act.op(lambda h, dc=dc, pg=pg: h.activation(out=dc[:].rearrange("p a b -> p (a b)"), in_=pg[:], func=AF.Exp),
                               reads=[rpg], writes=[rdc])
                        decs.append((dc, rdc))
                    st["decs"] = decs
                    Sf, rSf = Sfr.next()
                    Sb, rSb = Sbr.next()
                    dve.op(lambda h: h.tensor_tensor(out=Sf[:], in0=pssc[:, 0:128], in1=cst[:, 2, :], op=ALU.mult),
                           reads=[r_pssc, r_cst], writes=[rSf])
                    dve.op(lambda h: h.tensor_tensor(out=Sb[:], in0=pssc[:, 0:128], in1=cst[:, 3, :], op=ALU.mult),
                           reads=[r_pssc, r_cst], writes=[rSb])
                    st["S"] = [(Sf, rSf), (Sb, rSb)]
                    xds = []
                    for d in range(2):
                        c0 = d * NH + 4 * g
                        xd, rxd = xdr.next()
                        dve.op(lambda h, xd=xd, c0=c0: h.tensor_tensor(out=v4(xd[:]), in0=v4(xtm[:, c, :]),
                                                                       in1=hb(dtv[:, c, c0:c0 + 4]), op=ALU.mult),
                               reads=[r_xtm, r_dtv], writes=[rxd])
                        xds.append((xd, rxd))
                    st["xd"] = xds
                    st["xw"] = emit_xw(c, 0, g)
                    t3, rt3 = t3r.next()
                    dve.op(lambda h: h.tensor_tensor(out=t3[:], in0=xtm[:, c, :], in1=dsk[:, g * 256:(g + 1) * 256], op=ALU.mult),
                           reads=[r_xtm, r_dsk], writes=[rt3])
                    st["t3"] = (t3, rt3)
                    return st

                def p2_B(st):
                    c, csl = st["c"], st["csl"]
                    lts = []
                    for d in range(2):
                        dc, rdc = st["decs"][d]
                        Sd, rSd = st["S"][d]
                        Lt, rLt = lt4r.next()
                        dve.op(lambda h, Lt=Lt, dc=dc, Sd=Sd: h.tensor_tensor(
                            out=Lt[:], in0=dc[:], in1=Sd[:].unsqueeze(1).broadcast_to([128, 4, 128]), op=ALU.mult),
                            reads=[rdc, rSd], writes=[rLt])
                        lts.append((Lt, rLt))
                    pY, rpY = psYr.next()
                    for hh in range(4):
                        for d in range(2):
                            Lt, rLt = lts[d]
                            xd, rxd = st["xd"][d]
                            pe.op(lambda h, Lt=Lt, xd=xd, hh=hh, d=d: h.matmul(
                                pY[:, hh * 64:(hh + 1) * 64], lhsT=Lt[:, hh, :], rhs=xd[:, hh * 64:(hh + 1) * 64],
                                start=(d == 0), stop=(d == 1)), reads=[rLt, rxd], writes=[rpY], inc=(d == 1))
                    st["pY"] = (pY, rpY)
                    act.op(lambda h: h.activation(out=prevf[:], in_=carry[:], func=AF.Copy), reads=[r_carry], writes=[r_prevf])
                    pe.op(lambda h: h.matmul(psO[:, 0:256], lhsT=CT[:, csl], rhs=prevf[:], start=True, stop=True),
                          reads=[rCT, r_prevf], writes=[r_psO], inc=False)
                    pe.op(lambda h: h.matmul(psO[:, 256:512], lhsT=CT[:, csl], rhs=prevb[:, c, :], start=True, stop=True),
                          reads=[rCT, r_prevb], writes=[r_psO])
                    emit_state(c, 0, g, *st["xw"])

                def p2_C(st, yT, ryT):
                    c, csl = st["c"], st["csl"]
                    pY, rpY = st["pY"]
                    t3, rt3 = st["t3"]
                    t1, rt1 = t1r.next()
                    t2, rt2 = t2r.next()
                    dve.op(lambda h: h.tensor_tensor(out=v4(t1[:]), in0=v4(psO[:, 0:256]),
                                                     in1=hb(eQ[:, c, 4 * g:4 * g + 4]), op=ALU.mult),
                           reads=[r_psO, r_eQ], writes=[rt1])
                    dve.op(lambda h: h.tensor_tensor(out=v4(t2[:]), in0=v4(psO[:, 256:512]),
                                                     in1=hb(eQ[:, c, NH + 4 * g:NH + 4 * g + 4]), op=ALU.mult),
                           reads=[r_psO, r_eQ], writes=[rt2])
                    dve.op(lambda h: h.tensor_tensor(out=t2[:], in0=t2[:], in1=t3[:], op=ALU.add),
                           reads=[rt2, rt3], writes=[rt2])
                    dve.op(lambda h: h.tensor_tensor(out=t1[:], in0=t1[:], in1=pY[:, 0:256], op=ALU.add),
                           reads=[rt1, rpY], writes=[rt1])
                    dve.op(lambda h: h.tensor_tensor(out=t1[:], in0=t1[:], in1=t2[:], op=ALU.add),
                           reads=[rt1, rt2], writes=[rt1])
                    dve.op(lambda h: h.tensor_tensor(out=t1[:], in0=t1[:], in1=Gz[:, c, :], op=ALU.mult),
                           reads=[rt1, rGz], writes=[rt1])
                    sq, rsq = sqr.next()
                    ss, rss = ssr.next()
                    act.op(lambda h: h.activation(out=sq[:], in_=t1[:], func=AF.Square, accum_out=ss[:, 0:1]),
                           reads=[rt1], writes=[rsq, rss])
                    act.op(lambda h: h.activation(out=ss[:, 1:2], in_=ss[:, 0:1], func=AF.Ln, scale=1.0 / 256.0, bias=RMS_EPS),
                           reads=[rss], writes=[rss])
                    act.op(lambda h: h.activation(out=ss[:, 1:2], in_=ss[:, 1:2], func=AF.Exp, scale=-0.5),
                           reads=[rss], writes=[rss])
                    st["c2"] = (t1, rt1, ss, rss)

                def p2_D(st, yT, ryT):
                    c, csl = st["c"], st["csl"]
                    t1, rt1, ss, rss = st["c2"]
                    yn, ryn = ynr.next()
                    dve.op(lambda h: h.scalar_tensor_tensor(
                        out=yn[:], in0=t1[:], scalar=ss[:, 1:2], in1=nwb[:, g * 256:(g + 1) * 256], op0=ALU.mult, op1=ALU.mult),
                        reads=[rt1, rss, r_nwb], writes=[ryn])
                    for half in range(2):
                        pe.op(lambda h, half=half: h.transpose(
                            out=pstA[:, 512 + half * 128:512 + (half + 1) * 128], in_=yn[:, half * 128:(half + 1) * 128], identity=identb[:]),
                            reads=[ryn, r_identb], writes=[r_pstA], inc=(half == 1))
                    act.op(lambda h: h.activation(
                        out=yT[:, :, csl], in_=pstA[:, 512:768].rearrange("p (a b) -> p a b", a=2), func=AF.Copy),
                        reads=[r_pstA], writes=[ryT])

                dve.op(lambda h: h.memset(carry[:], 0.0), writes=[r_carry])
                yT, ryT = yTs.next()
                stn = p2_A(0)
                stp = None
                for c in range(NT):
                    stc = stn
                    if c + 1 < NT:
                        stn = p2_A(c + 1)
                    p2_B(stc)
                    p2_C(stc, yT, ryT)
                    if stp is not None:
                        p2_D(stp, yT, ryT)
                    stp = stc
                p2_D(stp, yT, ryT)
                sp.dma_multi([(YT[2 * g + half], yT[:, half, :]) for half in range(2)], reads=[ryT], semres=ryT)
            stage_end(es, mark)

        def gemm_tm(es, src_T, wsrc, ncols, dst, tag):
            Wring = Ring(Kx, es, "W%s_" % tag, [128, KC, 512], BF16, 2)
            aT, r_aT = one(Kx, es, "aT" + tag, [128, KC, TB], BF16)
            r_aTg = [Kx.res("aTg%s_%d" % (tag, i)) for i in range((KC + 7) // 8)]
            stz = Ring(Kx, es, "st" + tag, [128, 512], F32, 4)
            psacc = Ring(Kx, es, "ps" + tag, [128, 512], F32, 6, psum=True)
            ev = 0
            for b in range(T // TB):
                sv = src_T.rearrange("k p t -> p k t")
                for gi, k0 in enumerate(range(0, KC, 8)):
                    sp.dma(aT[:, k0:k0 + 8, :], sv[:, k0:k0 + 8, b * TB:(b + 1) * TB], writes=[r_aTg[gi]], semres=r_aTg[gi])
                nsl = ncols // 512
                nxt = load_w_slab(Wring, wsrc, 0, 512)
                for s in range(nsl):
                    Wt, rW = nxt
                    if s + 1 < nsl:
                        nxt = load_w_slab(Wring, wsrc, (s + 1) * 512, 512)
                    for tt in range(TBt):
                        pa, rpa = psacc.next()
                        for kc in range(KC):
                            pe.op(lambda h, pa=pa, tt=tt, kc=kc, Wt=Wt: h.matmul(
                                pa[:], lhsT=aT[:, kc, tt * 128:(tt + 1) * 128], rhs=Wt[:, kc, :],
                                start=(kc == 0), stop=(kc == KC - 1)), reads=[r_aTg[kc // 8], rW], writes=[rpa], inc=(kc == KC - 1))
                        sz, rsz = stz.next()
                        evac_copy(ev, sz[:], pa[:], [rpa], [rsz])
                        ev += 1
                        r0 = b * TB + tt * 128
                        sp.dma(dst[r0:r0 + 128, s * 512:(s + 1) * 512], sz[:], reads=[rsz], semres=rsz)

        if stop_after >= 6:
          with ExitStack() as es:
            mark = len(Kx.all_res)
            gemm_tm(es, YT, w_out, D, MIX, "4")
            stage_end(es, mark)

        def ln_stage(es, branch, resid, gate_idx, g_src, b_src, dst, with_h2, tag):
            gbc, r_gbc = one(Kx, es, "gbc" + tag, [128, D], F32)
            lg, r_lg = one(Kx, es, "lg" + tag, [128, D], F32)
            lb, r_lb = one(Kx, es, "lb" + tag, [128, D], F32)
            sp.dma(gbc[:], BCS[gate_idx], writes=[r_gbc], semres=r_gbc)
            sp.dma(lg[:], g_src[:, :], writes=[r_lg], semres=r_lg)
            sp.dma(lb[:], b_src[:, :], writes=[r_lb], semres=r_lb)
            if with_h2:
                G2, r_G2 = one(Kx, es, "G2" + tag, [128, D], F32)
                B2, r_B2 = one(Kx, es, "B2" + tag, [128, D], F32)
                sp.dma(G2[:], BCS[1], writes=[r_G2], semres=r_G2)
                sp.dma(B2[:], BCS[2], writes=[r_B2], semres=r_B2)
                h2r = Ring(Kx, es, "h2" + tag, [128, D], BF16, 1)
                h2s = Ring(Kx, es, "h2s" + tag, [128, KC, 256], BF16, 1)
                pst = Ring(Kx, es, "pst" + tag, [128, 1024], BF16, 4, psum=True)
            nring = 2 if with_h2 else 3
            mr = Ring(Kx, es, "mr" + tag, [128, D], F32, nring)
            xr = Ring(Kx, es, "xr" + tag, [128, D], F32, nring)
            str_ = Ring(Kx, es, "bs" + tag, [128, D // 512, 6], F32, 3)
            mvr = Ring(Kx, es, "mv" + tag, [128, 8], F32, 3)
            junk, r_junk = one(Kx, es, "junk" + tag, [128, D], F32)
            ev = [0]
            hsb = [None]

            def ln_load(tt):
                rows = slice(tt * 128, (tt + 1) * 128)
                m, rm = mr.next()
                xx, rxx = xr.next()
                sp.dma(m[:], branch[rows, :], writes=[rm], semres=rm)
                sp.dma(xx[:], resid[rows, :], writes=[rxx], semres=rxx)
                return (m, rm, xx, rxx)

            def ln_A(tt, ld):
                m, rm, xx, rxx = ld
                dve.op(lambda h: h.tensor_tensor(out=m[:], in0=m[:], in1=gbc[:], op=ALU.mult), reads=[rm, r_gbc], writes=[rm])
                dve.op(lambda h: h.scalar_tensor_tensor(out=xx[:], in0=xx[:], scalar=alpha, in1=m[:], op0=ALU.mult, op1=ALU.add),
                       reads=[rxx, rm], writes=[rxx])
                mv, rmv = mvr.next()
                act.op(lambda h: h.activation(out=junk[:], in_=xx[:], func=AF.Identity, accum_out=mv[:, 4:5]),
                       reads=[rxx], writes=[r_junk, rmv])
                act.op(lambda h: h.activation(out=junk[:], in_=xx[:], func=AF.Square, accum_out=mv[:, 5:6]),
                       reads=[rxx, rmv], writes=[r_junk, rmv])
                return (mv, rmv)

            def ln_B(tt, ld, mvv):
                rows = slice(tt * 128, (tt + 1) * 128)
                m, rm, xx, rxx = ld
                mv, rmv = mvv
                dve.op(lambda h: h.tensor_scalar(out=mv[:, 0:1], in0=mv[:, 4:5], scalar1=1.0 / D, scalar2=None, op0=ALU.mult),
                       reads=[rmv], writes=[rmv])
                dve.op(lambda h: h.tensor_tensor(out=mv[:, 6:7], in0=mv[:, 0:1], in1=mv[:, 0:1], op=ALU.mult),
                       reads=[rmv], writes=[rmv])
                dve.op(lambda h: h.scalar_tensor_tensor(out=mv[:, 1:2], in0=mv[:, 5:6], scalar=1.0 / D, in1=mv[:, 6:7],
                                                        op0=ALU.mult, op1=ALU.subtract), reads=[rmv], writes=[rmv])
                act.op(lambda h: h.activation(out=mv[:, 2:3], in_=mv[:, 1:2], func=AF.Ln, bias=LN_EPS), reads=[rmv], writes=[rmv])
                act.op(lambda h: h.activation(out=mv[:, 2:3], in_=mv[:, 2:3], func=AF.Exp, scale=-0.5), reads=[rmv], writes=[rmv])
                dve.op(lambda h: h.scalar_tensor_tensor(out=mv[:, 3:4], in0=mv[:, 0:1], scalar=-1.0, in1=mv[:, 2:3],
                                                        op0=ALU.mult, op1=ALU.mult), reads=[rmv], writes=[rmv])
                act.op(lambda h: h.activation(out=xx[:], in_=xx[:], func=AF.Identity, bias=mv[:, 3:4], scale=mv[:, 2:3]),
                       reads=[rxx, rmv], writes=[rxx])
                dve.op(lambda h: h.tensor_tensor(out=m[:], in0=xx[:], in1=lg[:], op=ALU.mult), reads=[rxx, r_lg], writes=[rm])
                dve.op(lambda h: h.tensor_tensor(out=m[:], in0=m[:], in1=lb[:], op=ALU.add), reads=[rm, r_lb], writes=[rm])
                sp.dma(dst[rows, :], m[:], reads=[rm], semres=rm)
                if with_h2:
                    h2, rh2 = h2r.next()
                    dve.op(lambda h: h.tensor_tensor(out=xx[:], in0=xx[:], in1=G2[:], op=ALU.mult), reads=[rxx, r_G2], writes=[rxx])
                    dve.op(lambda h: h.tensor_tensor(out=h2[:], in0=xx[:], in1=B2[:], op=ALU.add), reads=[rxx, r_B2], writes=[rh2])
                    if tt % 2 == 0:
                        hsb[0] = h2s.next()
                    hs, rhs = hsb[0]
                    for q8 in range(KC // 8):
                        pt, rpt = pst.next()
                        for q in range(8):
                            kc = q8 * 8 + q
                            pe.op(lambda h, q=q, kc=kc: h.transpose(
                                out=pt[:, q * 128:(q + 1) * 128], in_=h2[:, kc * 128:(kc + 1) * 128], identity=identb[:]),
                                reads=[rh2, r_identb], writes=[rpt], inc=(q == 7))
                        o_ap = hs[:, q8 * 8:(q8 + 1) * 8, (tt % 2) * 128:(tt % 2 + 1) * 128]
                        i_ap = pt[:].rearrange("p (a b) -> p a b", a=8)
                        evac_copy(ev[0], o_ap, i_ap, [rpt], [rhs])
                        ev[0] += 1
                    if tt % 2 == 1:
                        t0 = (tt - 1) * 128
                        hv = H2T.rearrange("k p t -> p k t")
                        sp.dma_multi([(hv[:, k0:k0 + 8, t0:t0 + 256], hs[:, k0:k0 + 8, :]) for k0 in range(0, KC, 8)],
                                     reads=[rhs], semres=rhs)

            lds = [ln_load(0)]
            pend = None
            for tt in range(NT):
                if nring >= 3 and tt + 1 < NT:
                    lds.append(ln_load(tt + 1))
                mvv = ln_A(tt, lds[tt])
                if pend is not None:
                    ln_B(*pend)
                if nring < 3 and tt + 1 < NT:
                    lds.append(ln_load(tt + 1))
                pend = (tt, lds[tt], mvv)
            ln_B(*pend)

        if stop_after >= 7:
          with ExitStack() as es:
            mark = len(Kx.all_res)
            ln_stage(es, MIX, x_in, 0, ln1g_bc, ln1b_bc, X1, True, "5")
            stage_end(es, mark)

        if stop_after >= 8:
          with ExitStack() as es:
            mark = len(Kx.all_res)
            Wring = Ring(Kx, es, "W6_", [128, KC, 512], BF16, 2)
            aT, r_aT = one(Kx, es, "aT6", [128, KC, TB], BF16)
            r_aTg = [Kx.res("aTg6_%d" % i) for i in range((KC + 7) // 8)]
            rr = Ring(Kx, es, "rl6", [128, 512], F32, 3)
            us = Ring(Kx, es, "us6", [128, TB], BF16, 3)
            psacc = Ring(Kx, es, "ps6", [128, 512], F32, 6, psum=True)
            for b in range(T // TB):
                sv = H2T.rearrange("k p t -> p k t")
                for gi, k0 in enumerate(range(0, KC, 8)):
                    sp.dma(aT[:, k0:k0 + 8, :], sv[:, k0:k0 + 8, b * TB:(b + 1) * TB], writes=[r_aTg[gi]], semres=r_aTg[gi])
                nsl = DFF // 512
                nxt = load_w_slab(Wring, w_up, 0, 512)
                for s in range(nsl):
                    Wt, rW = nxt
                    if s + 1 < nsl:
                        nxt = load_w_slab(Wring, w_up, (s + 1) * 512, 512)
                    for ct in range(4):
                        u, ru = us.next()
                        for sub in range(TB // 512):
                            pa, rpa = psacc.next()
                            for kc in range(KC):
                                pe.op(lambda h, pa=pa, kc=kc, Wt=Wt, ct=ct, sub=sub: h.matmul(
                                    pa[:], lhsT=Wt[:, kc, ct * 128:(ct + 1) * 128], rhs=aT[:, kc, sub * 512:(sub + 1) * 512],
                                    start=(kc == 0), stop=(kc == KC - 1)), reads=[r_aTg[kc // 8], rW], writes=[rpa], inc=(kc == KC - 1))
                            r_, rr_ = rr.next()
                            act.op(lambda h, r_=r_, pa=pa: h.activation(out=r_[:], in_=pa[:], func=AF.Relu), reads=[rpa], writes=[rr_])
                            dve.op(lambda h, r_=r_, u=u, sub=sub: h.tensor_tensor(out=u[:, sub * 512:(sub + 1) * 512], in0=r_[:], in1=r_[:], op=ALU.mult),
                                   reads=[rr_], writes=[ru])
                        sp.dma(UT[4 * s + ct, :, b * TB:(b + 1) * TB], u[:], reads=[ru], semres=ru)
            stage_end(es, mark)

        if stop_after >= 9:
          with ExitStack() as es:
            mark = len(Kx.all_res)
            FCG = 8
            uT, r_uT = one(Kx, es, "uT7", [128, FC, 512], BF16)
            r_uTg = [Kx.res("uTg7_%d" % i) for i in range((FC + 7) // 8)]
            Wd = Ring(Kx, es, "Wd7", [128, 2, FCG, 512], BF16, 2)
            stz = Ring(Kx, es, "st7", [128, 512], F32, 4)
            ps8 = [one(Kx, es, "ps7_%d" % i, [128, 512], F32, psum=True) for i in range(8)]
            wdv = w_down.rearrange("(fc p) n -> p fc n", p=128)

            def load_wd(sp_i, fcg):
                W_, rW_ = Wd.next()
                pool.dma_multi([(W_[:, s, :, :], wdv[:, fcg * FCG:(fcg + 1) * FCG, (2 * sp_i + s) * 512:(2 * sp_i + s + 1) * 512])
                                for s in range(2)], writes=[rW_], semres=rW_)
                return W_, rW_

            ev = 0
            for b in range(T // 512):
                sv = UT.rearrange("f p t -> p f t")
                for gi, k0 in enumerate(range(0, FC, 8)):
                    sp.dma(uT[:, k0:k0 + 8, :], sv[:, k0:k0 + 8, b * 512:(b + 1) * 512], writes=[r_uTg[gi]], semres=r_uTg[gi])
                seq = [(spi, fcg) for spi in range(D // 1024) for fcg in range(FC // FCG)]
                nxt = load_wd(*seq[0])
                for qi, (spi, fcg) in enumerate(seq):
                    W_, rW_ = nxt
                    if qi + 1 < len(seq):
                        nxt = load_wd(*seq[qi + 1])
                    for fcl in range(FCG):
                        fc = fcg * FCG + fcl
                        for tt in range(4):
                            for s in range(2):
                                pa, rpa = ps8[tt * 2 + s]
                                pe.op(lambda h, pa=pa, fc=fc, tt=tt, s=s, fcl=fcl, W_=W_: h.matmul(
                                    pa[:], lhsT=uT[:, fc, tt * 128:(tt + 1) * 128], rhs=W_[:, s, fcl, :],
                                    start=(fc == 0), stop=(fc == FC - 1)), reads=[r_uTg[fc // 8], rW_], writes=[rpa],
                                    inc=(fc == FC - 1) or (fcl == FCG - 1 and tt == 3 and s == 1))
                    if fcg == FC // FCG - 1:
                        for tt in range(4):
                            for s in range(2):
                                pa, rpa = ps8[tt * 2 + s]
                                sz, rsz = stz.next()
                                evac_copy(ev, sz[:], pa[:], [rpa], [rsz])
                                ev += 1
                                r0 = b * 512 + tt * 128
                                c0 = (2 * spi + s) * 512
                                sp.dma(FFs[r0:r0 + 128, c0:c0 + 512], sz[:], reads=[rsz], semres=rsz)
            stage_end(es, mark)

        if stop_after >= 10:
          with ExitStack() as es:
            mark = len(Kx.all_res)
            ln_stage(es, FFs, X1, 3, ln2g_bc, ln2b_bc, out, False, "8")
            stage_end(es, mark)
        Kx.barrier()
    return nc


def make_consts():
    r = np.arange(128)[:, None]
    c = np.arange(128)[None, :]
    m = np.stack([(r == c), np.ones((128, 128), bool), (r <= c), (r >= c), (r > c), (r < c)], axis=1)
    return np.ascontiguousarray(m.astype(np.float32))


def fm(v, nchunk):
    return np.ascontiguousarray(np.asarray(v).reshape(nchunk, 128).T)


def bc(v):
    v = np.asarray(v, dtype=np.float32).reshape(1, -1)
    return np.ascontiguousarray(np.broadcast_to(v, (128, v.shape[1])))


def prep_inputs(cfg, inp, n_batch):
    D, T, KC, NG, NSC, NH = cfg.D, cfg.T, cfg.KC, cfg.NG, cfg.NSC, cfg.NH
    DSSM, DXBC = cfg.DSSM, cfg.DXBC
    f32 = lambda a: np.ascontiguousarray(np.asarray(a, dtype=np.float32))
    x = np.asarray(inp["x"])
    w_in = f32(inp["w_in"][0])
    dt0 = DSSM + DXBC
    wdt_e = f32(w_in[:, dt0:dt0 + 2 * NH])
    wdt_o = f32(np.concatenate([w_in[:, dt0 + NH:dt0 + 2 * NH], w_in[:, dt0:dt0 + NH]], axis=1))
    shared = {
        "w_ada": f32(inp["w_ada"][0]), "b_ada_fm": fm(inp["b_ada"][0], 6 * KC), "w_in": w_in,
        "cb_fm": fm(inp["ssm_conv_b"][0], DXBC // 128),
        "dsk_bc": bc(np.repeat(np.asarray(inp["ssm_d"][0]), 64)), "nw_bc": bc(inp["ssm_norm_w"][0]),
        "scn_fm": fm(inp["sc_norm_w"][0], NSC), "w_out": f32(inp["w_out"][0]),
        "ln1g_bc": bc(inp["ln1_g"][0]), "ln1b_bc": bc(inp["ln1_b"][0]),
        "w_up": f32(inp["w_up"][0]), "w_down": f32(inp["w_down"][0]),
        "ln2g_bc": bc(inp["ln2_g"][0]), "ln2b_bc": bc(inp["ln2_b"][0]), "consts": make_consts(),
    }
    cwv = np.asarray(inp["ssm_conv_w"][0])
    scwv = np.asarray(inp["sc_conv_w"][0])
    par = []
    for odd in (0, 1):
        cw_ = cwv[::-1] if odd else cwv
        sc_ = scwv[::-1] if odd else scwv
        f, b_ = ("b", "f") if odd else ("f", "b")
        par.append({
            "w_dt": wdt_o if odd else wdt_e,
            "cw_fm": np.ascontiguousarray(cw_.T.reshape(DXBC // 128, 128, 5).transpose(1, 0, 2).astype(np.float32)),
            "scw_fm": np.ascontiguousarray(sc_.T.reshape(NSC, 128, 3).transpose(1, 0, 2).astype(np.float32)),
            "dtb_bc": bc(np.concatenate([inp["ssm_dt_bias_" + f][0], inp["ssm_dt_bias_" + b_][0]])),
            "alog_bc": bc(np.concatenate([inp["ssm_a_log_" + f][0], inp["ssm_a_log_" + b_][0]])),
        })
    maps = []
    for core in range(2 * n_batch):
        b, odd = core // 2, core % 2
        xl = x[b, ::-1] if odd else x[b]
        m = dict(shared)
        m.update(par[odd])
        m["x"] = f32(xl)
        m["c_fm"] = fm(inp["c"][b], KC)
        maps.append(m)
    return maps


def assemble(cfg, results, n_batch):
    T, D = cfg.T, cfg.D
    o = np.empty((n_batch, 2 * T, D), np.float32)
    for core in range(2 * n_batch):
        b, odd = core // 2, core % 2
        r = np.asarray(results[core]["out"])
        if odd:
            o[b, T:] = r[::-1]
        else:
            o[b, :T] = r
    return o


_NC_CACHE = {}


def kernel(**inputs):
    cfg = FULL
    if "nc" not in _NC_CACHE:
        _NC_CACHE["nc"] = build(cfg)
    nc = _NC_CACHE["nc"]
    maps = prep_inputs(cfg, inputs, 4)
    res = run_bass_kernel_spmd(nc, maps, core_ids=list(range(8)))
    return assemble(cfg, res.results, 4)
```
